# Optimizing a Trainium2 kernel written in Bass

```python
import jax, jax.numpy as jnp
from jax import lax
import numpy as np

D_MODEL = 1024
BATCH = 8
SEQ = 2048
DEPTH = 1
DEC_BATCH = 128
DEC_SEQ = 8
PAST_LEN = 16384
PAGE_SIZE = 128

D_CONV = D_MODEL
CONV_WIDTH = 3
POOL_WINDOWS = (2, 4, 8, 16)
N_POOL_GROUPS = len(POOL_WINDOWS)
D_POOL = D_MODEL
POOL_GROUP = D_POOL // N_POOL_GROUPS
POOL_BUF = max(POOL_WINDOWS) - 1
D_FF = 4 * D_MODEL
D_IN_ALL = 3 * D_CONV + D_POOL + 2 * D_MODEL
EPS = 1e-6

kernel_name = "gated_conv_pool_hybrid_step"


def rmsnorm(x, g):
    xf = x.astype(jnp.float32)
    r = lax.rsqrt(jnp.mean(xf * xf, axis=-1, keepdims=True) + EPS)
    return (xf * r * g.astype(jnp.float32)).astype(x.dtype)


def short_gated_conv(xv, b, c, buf, w_conv):
    T = xv.shape[1]
    z = c * xv
    zc = jnp.concatenate([buf.astype(z.dtype), z], axis=1)
    y = zc[:, 0:T] * w_conv[0]
    for k in range(1, CONV_WIDTH):
        y = y + zc[:, k:k + T] * w_conv[k]
    return b * y, zc[:, -(CONV_WIDTH - 1):]


def multiscale_pool(u, buf, start_pos, w_pool_group, pool_scale):
    Bsz, T, _ = u.shape
    uc = jnp.concatenate([buf.astype(u.dtype), u], axis=1)
    cs = jnp.cumsum(uc.astype(jnp.float32), axis=1)
    cs = jnp.concatenate([jnp.zeros((Bsz, 1, D_POOL), jnp.float32), cs], axis=1)
    pos = start_pos + jnp.arange(T, dtype=jnp.int32)
    uf = u.astype(jnp.float32)
    off = POOL_BUF + 1
    groups = []
    for gi, w in enumerate(POOL_WINDOWS):
        sl = slice(gi * POOL_GROUP, (gi + 1) * POOL_GROUP)
        win_sum = cs[:, off:off + T, sl] - cs[:, off - w:off - w + T, sl]
        count = jnp.minimum(w, pos + 1).astype(jnp.float32)[None, :, None]
        groups.append(win_sum / count - uf[:, :, sl])
    pooled = jnp.stack(groups, axis=2)
    y = jnp.einsum('btgi,gio->btgo', pooled, w_pool_group.astype(jnp.float32))
    y = y.reshape(Bsz, T, D_POOL) * pool_scale.astype(jnp.float32)
    return y.astype(u.dtype), uc[:, -POOL_BUF:]


def layer(x, conv_buf, pool_buf, start_pos, g_pre_mix, w_in, b_gate, w_conv, w_out_conv,
          w_pool_group, pool_scale, w_o, g_post_mix, g_pre_mlp, w_up, w_down, g_post_mlp):
    h = rmsnorm(x, g_pre_mix)
    p = h @ w_in
    o = 0
    xv = p[..., o:o + D_CONV]; o += D_CONV
    bg = p[..., o:o + D_CONV]; o += D_CONV
    cg = p[..., o:o + D_CONV]; o += D_CONV
    u = p[..., o:o + D_POOL]; o += D_POOL
    gate_pre = p[..., o:o + 2 * D_MODEL]

    ya, conv_new = short_gated_conv(xv, bg, cg, conv_buf, w_conv)
    ya = ya @ w_out_conv
    yb, pool_new = multiscale_pool(u, pool_buf, start_pos, w_pool_group, pool_scale)

    gates = jax.nn.sigmoid((gate_pre + b_gate).astype(jnp.float32)).astype(x.dtype)
    merged = gates[..., :D_MODEL] * ya + gates[..., D_MODEL:] * yb
    x = x + rmsnorm(merged @ w_o, g_post_mix)

    h2 = rmsnorm(x, g_pre_mlp)
    f = jnp.square(jax.nn.relu(h2 @ w_up)) @ w_down
    x = x + rmsnorm(f, g_post_mlp)
    return x, conv_new, pool_new


def setup_inputs(seed: int = 0) -> dict:
    key = jax.random.key(seed)
    ks = jax.random.split(key, 16)
    n = jax.random.normal
    f32 = jnp.float32
    return {
        "x_prompt": n(ks[0], (BATCH, SEQ, D_MODEL), f32),
        "x_sample": n(ks[1], (DEC_BATCH, DEC_SEQ, D_MODEL), f32),
        "state_conv": n(ks[2], (DEPTH, DEC_BATCH, CONV_WIDTH - 1, D_CONV), f32),
        "state_pool": n(ks[3], (DEPTH, DEC_BATCH, POOL_BUF, D_POOL), f32),
        "g_pre_mix": 1.0 + 0.05 * n(ks[4], (DEPTH, D_MODEL), f32),
        "w_in": n(ks[5], (DEPTH, D_MODEL, D_IN_ALL), f32) * D_MODEL ** -0.5,
        "b_gate": 0.1 * n(ks[6], (DEPTH, 2 * D_MODEL), f32),
        "w_conv": n(ks[7], (DEPTH, CONV_WIDTH, D_CONV), f32) * CONV_WIDTH ** -0.5,
        "w_out_conv": n(ks[8], (DEPTH, D_CONV, D_MODEL), f32) * D_CONV ** -0.5,
        "w_pool_group": n(ks[9], (DEPTH, N_POOL_GROUPS, POOL_GROUP, POOL_GROUP), f32) * POOL_GROUP ** -0.5,
        "pool_scale": 1.0 + 0.05 * n(ks[10], (DEPTH, D_POOL), f32),
        "w_o": n(ks[11], (DEPTH, D_MODEL, D_MODEL), f32) * D_MODEL ** -0.5,
        "g_post_mix": 1.0 + 0.05 * n(ks[12], (DEPTH, D_MODEL), f32),
        "g_pre_mlp": 1.0 + 0.05 * n(ks[13], (DEPTH, D_MODEL), f32),
        "w_up": n(ks[14], (DEPTH, D_MODEL, D_FF), f32) * D_MODEL ** -0.5,
        "w_down": n(ks[15], (DEPTH, D_FF, D_MODEL), f32) * D_FF ** -0.5,
        "g_post_mlp": 1.0 + 0.05 * n(jax.random.fold_in(key, 99), (DEPTH, D_MODEL), f32),
    }


def reference(x_prompt, x_sample, state_conv, state_pool, g_pre_mix, w_in, b_gate, w_conv,
              w_out_conv, w_pool_group, pool_scale, w_o, g_post_mix, g_pre_mlp, w_up, w_down,
              g_post_mlp):
    yp, ys = x_prompt, x_sample
    nb_p = x_prompt.shape[0]
    conv_p, pool_p, conv_s, pool_s = [], [], [], []
    for l in range(DEPTH):
        params = (g_pre_mix[l], w_in[l], b_gate[l], w_conv[l], w_out_conv[l], w_pool_group[l],
                  pool_scale[l], w_o[l], g_post_mix[l], g_pre_mlp[l], w_up[l], w_down[l],
                  g_post_mlp[l])
        zero_conv = jnp.zeros((nb_p, CONV_WIDTH - 1, D_CONV), x_prompt.dtype)
        zero_pool = jnp.zeros((nb_p, POOL_BUF, D_POOL), x_prompt.dtype)
        yp, cp, pp = layer(yp, zero_conv, zero_pool, 0, *params)
        ys, cs_, ps = layer(ys, state_conv[l], state_pool[l], PAST_LEN, *params)
        conv_p.append(cp); pool_p.append(pp); conv_s.append(cs_); pool_s.append(ps)
    new_conv_prompt = jnp.stack(conv_p, axis=0)
    new_pool_prompt = jnp.stack(pool_p, axis=0)
    new_conv_sample = jnp.stack(conv_s, axis=0)
    new_pool_sample = jnp.stack(pool_s, axis=0)
    return (yp, ys, new_conv_prompt, new_pool_prompt, new_conv_sample, new_pool_sample)
```

```python
import numpy as np
from contextlib import ExitStack
import concourse.bass as bass
import concourse.mybir as mybir
from concourse.bass_utils import run_bass_kernel_spmd

F32 = mybir.dt.float32
BF16 = mybir.dt.bfloat16
ALU = mybir.AluOpType
AF = mybir.ActivationFunctionType
SZ = {F32: 4, BF16: 2}

D = 1024
KC = 8
DFF = 4096
EPS = 1e-6
NCORES = 8
POOL_W = (2, 4, 8, 16)


def _prod(xs):
    r = 1
    for x in xs:
        r *= int(x)
    return r


class _Op:
    __slots__ = ("eng", "fn", "deps", "signal", "pos", "dma", "name", "sig", "wait_val")

    def __init__(self, eng, fn, name):
        self.eng = eng
        self.fn = fn
        self.deps = set()
        self.signal = True
        self.pos = None
        self.dma = None
        self.name = name
        self.sig = None
        self.wait_val = None


class Prog:
    ENG = ("pe", "act", "dve", "pool", "sp")

    def __init__(self, nc):
        self.nc = nc
        self.engs = {"pe": nc.tensor, "act": nc.scalar, "dve": nc.vector,
                     "pool": nc.gpsimd, "sp": nc.sync}
        self.streams = {e: [] for e in self.ENG}
        self.recs = {}
        self.dma_cnt = {}
        self.out_dmas = []

    @staticmethod
    def _iv(ap):
        t = ap.tensor
        if type(t).__name__.startswith("DRam"):
            return None
        esz = SZ[ap.dtype]
        pbytes = _prod(t.shape[1:]) * SZ[t.dtype]
        off = int(ap.offset) * esz
        lo = off % pbytes
        dims = list(ap.ap)
        p0 = off // pbytes
        p1 = p0 + ((int(dims[0][1]) - 1) * abs(int(dims[0][0])) * esz) // pbytes + 1
        fd = sorted([(abs(int(st)) * esz, int(cnt)) for (st, cnt) in dims[1:] if int(cnt) > 1], key=lambda d: -d[0])

        def rec(base, ds):
            if not ds:
                return [(base, base + esz)]
            inner = esz + sum((c - 1) * s_ for s_, c in ds[1:])
            s0, c0 = ds[0]
            if s0 >= 2 * inner and c0 <= 32:
                out = []
                for i in range(c0):
                    out += rec(base + i * s0, ds[1:])
                return out
            return [(base, base + esz + sum((c - 1) * s_ for s_, c in ds))]
        ivs = rec(lo, fd)
        if len(ivs) > 64:
            ivs = [(lo, lo + esz + sum((c - 1) * s_ for s_, c in fd))]
        return t.name, ivs, p0, p1

    def _acc(self, op, ap, is_write):
        iv = self._iv(ap)
        if iv is None:
            return
        key, ivs, p0, p1 = iv
        for (lo, hi) in ivs:
            self._acc1(op, key, lo, hi, p0, p1, is_write)

    def _acc1(self, op, key, lo, hi, p0, p1, is_write):
        recs = self.recs.get(key, [])
        new = []
        for r in recs:
            rlo, rhi, rop, rw, rp0, rp1 = r
            if rhi <= lo or rlo >= hi or rp1 <= p0 or rp0 >= p1:
                new.append(r)
                continue
            if (is_write or rw) and rop is not op:
                op.deps.add(rop)
            cov = rlo >= lo and rhi <= hi and rp0 >= p0 and rp1 <= p1
            if is_write and cov:
                continue
            if (not is_write) and (not rw) and rop.dma is None and op.dma is None \
                    and rop.eng == op.eng and cov:
                continue
            new.append(r)
        new.append((lo, hi, op, is_write, p0, p1))
        self.recs[key] = new

    def add(self, eng, fn, reads=(), writes=(), signal=True, name="", dma=None):
        op = _Op(eng, fn, name)
        op.signal = signal
        op.dma = dma
        for ap in reads:
            if ap is not None and not isinstance(ap, (int, float)):
                self._acc(op, ap, False)
        for ap in writes:
            self._acc(op, ap, True)
        op.pos = len(self.streams[eng])
        self.streams[eng].append(op)
        return op

    def dma(self, q, out, in_, semkey, name="", is_output=False, deps=(), **kw):
        cnt = self.dma_cnt.get(semkey, 0) + 16
        self.dma_cnt[semkey] = cnt
        op = self.add(q, lambda e: e.dma_start(out=out, in_=in_, **kw), reads=[in_], writes=[out],
                      name=name, dma=(semkey, cnt))
        op.deps.update(deps)
        if is_output:
            self.out_dmas.append(op)
        return op

    def mm(self, out, lhsT, rhs, start, stop, signal=None):
        sig = stop if signal is None else signal
        return self.add("pe", lambda e: e.matmul(out, lhsT, rhs, start=start, stop=stop),
                        reads=[lhsT, rhs], writes=[out], signal=sig, name="mm")

    def tr(self, out, in_, ident, signal=True):
        return self.add("pe", lambda e: e.transpose(out, in_, ident), reads=[in_, ident], writes=[out],
                        signal=signal, name="tr")

    def act(self, out, in_, func, bias=None, scale=None, accum_out=None, name="act"):
        kw = {}
        if bias is not None:
            kw["bias"] = bias
        if scale is not None:
            kw["scale"] = scale
        if accum_out is not None:
            kw["accum_out"] = accum_out
        rd = [in_] + [x for x in (bias, scale) if x is not None and not isinstance(x, (int, float))]
        wr = [out] + ([accum_out] if accum_out is not None else [])
        return self.add("act", lambda e: e.activation(out, in_, func, **kw), reads=rd, writes=wr, name=name)

    def tt(self, eng, out, in0, in1, op, name="tt"):
        return self.add(eng, lambda e: e.tensor_tensor(out, in0, in1, op), reads=[in0, in1], writes=[out], name=name)

    def ts(self, eng, out, in0, s1, s2, op0, op1=None, name="ts"):
        rd = [in0] + [x for x in (s1, s2) if x is not None and not isinstance(x, (int, float))]
        if op1 is None:
            return self.add(eng, lambda e: e.tensor_scalar(out, in0, s1, s2, op0), reads=rd, writes=[out], name=name)
        return self.add(eng, lambda e: e.tensor_scalar(out, in0, s1, s2, op0, op1), reads=rd, writes=[out], name=name)

    def stt(self, eng, out, in0, scalar, in1, op0, op1, name="stt"):
        rd = [in0, in1] + ([scalar] if not isinstance(scalar, (int, float)) else [])
        return self.add(eng, lambda e: e.scalar_tensor_tensor(out, in0, scalar, in1, op0, op1),
                        reads=rd, writes=[out], name=name)

    def copy(self, eng, out, in_, name="copy"):
        if eng == "act":
            return self.act(out, in_, AF.Identity, name=name)
        return self.add(eng, lambda e: e.tensor_copy(out, in_), reads=[in_], writes=[out], name=name)

    def memset(self, eng, out, val, name="memset"):
        return self.add(eng, lambda e: e.memset(out, val), writes=[out], name=name)

    def emit(self, stack):
        nc = self.nc
        sems = {}
        for e in ("pe", "act", "dve", "pool"):
            sems[e] = stack.enter_context(nc.semaphore("s_" + e))
        for k in self.dma_cnt:
            sems[("dma", k)] = stack.enter_context(nc.semaphore("d_" + str(k)))
        fin = _Op("sp", None, "final")
        fin.deps = set(self.out_dmas)
        self.streams["sp"].append(fin)
        for e in self.ENG:
            c = 0
            for op in self.streams[e]:
                if op.dma is None and op.fn is not None and op.signal:
                    c += 1
                    op.sig = c
            nxt = None
            for op in reversed(self.streams[e]):
                if op.dma is None and op.fn is not None:
                    if op.signal:
                        nxt = op.sig
                    op.wait_val = nxt
        nwait = 0
        for e in self.ENG:
            eng = self.engs[e]
            waited = {}
            for op in self.streams[e]:
                need = {}
                for d in op.deps:
                    if d.dma is not None:
                        key = ("dma", d.dma[0])
                        val = d.dma[1]
                    else:
                        if d.eng == "pe" and e == "pe":
                            continue
                        key = d.eng
                        val = d.wait_val
                        assert val is not None, (d.name, op.name)
                    if waited.get(key, 0) >= val:
                        continue
                    if need.get(key, 0) < val:
                        need[key] = val
                for key, val in need.items():
                    waited[key] = val
                    eng.wait_ge(sems[key], val)
                    nwait += 1
                if op.fn is not None:
                    ins = op.fn(eng)
                    if op.dma is not None:
                        ins.then_inc(sems[("dma", op.dma[0])], 16)
                    elif op.signal:
                        ins.then_inc(sems[e], 1)
        return nwait


def _view(t, boff, dt, shape):
    n = _prod(shape)
    esz = SZ[dt]
    assert boff % 4 == 0
    e0 = boff // 2
    e1 = e0 + n * esz // 2
    ap = t[:, e0:e1]
    if dt != BF16:
        ap = ap.bitcast(dt)
    if len(shape) == 2:
        ap = ap.rearrange("p (a b) -> p a b", b=shape[1])
    elif len(shape) == 3:
        ap = ap.rearrange("p (a b c) -> p a b c", b=shape[1], c=shape[2])
    return ap


def build_program(ntg=2, with_sample=True, npass=2):
    nc = bass.Bass("TRN2", target_bir_lowering=False)
    P = Prog(nc)
    NPP = 512 * ntg
    NPT = NPP * npass
    NSEQ = 16
    NST = NSEQ * 8
    TMAX = NPP + (NST if with_sample else 0)

    def din(name, shape):
        return nc.dram_tensor(name, list(shape), F32, kind="ExternalInput").ap()

    def dout(name, shape):
        return nc.dram_tensor(name, list(shape), F32, kind="ExternalOutput").ap()

    x_p = din("x_p", [NPT, D])
    x_s = din("x_s", [NST, D])
    st_conv = din("st_conv", [NSEQ * 2, D])
    st_pool = din("st_pool", [NSEQ * 15, D])
    g_pre_mix = din("g_pre_mix", [D])
    w_in = din("w_in", [D, 6 * D])
    b_gate = din("b_gate", [2 * D])
    w_conv = din("w_conv", [3, D])
    w_out_conv = din("w_out_conv", [D, D])
    w_pool_group = din("w_pool_group", [4, 256, 256])
    pool_scale = din("pool_scale", [D])
    w_o = din("w_o", [D, D])
    g_post_mix = din("g_post_mix", [D])
    g_pre_mlp = din("g_pre_mlp", [D])
    w_up = din("w_up", [D, DFF])
    w_down = din("w_down", [DFF, D])
    g_post_mlp = din("g_post_mlp", [D])

    y_p = dout("y_p", [NPT, D])
    y_s = dout("y_s", [NST, D])
    ncp = dout("ncp", [2, D])
    npp = dout("npp", [15, D])
    ncs = dout("ncs", [NSEQ * 2, D])
    nps = dout("nps", [NSEQ * 15, D])

    wd_bf = nc.dram_tensor("wd_bf", [8, 128, KC * 512], BF16, kind="Internal").ap()

    stack = ExitStack()
    with stack:
        NT_MAX = TMAX // 128
        A_ACT = 0
        A_HT = 0
        A_AP = KC * TMAX * 2
        A_PL = 2 * KC * TMAX * 2
        A_MG = 3 * KC * TMAX * 2
        A_X1 = 32 * TMAX * 2
        A_H2 = A_X1 + NT_MAX * 4096
        A_END = A_H2 + KC * TMAX * 2
        AR = stack.enter_context(nc.sbuf_tensor("AR", [128, A_END // 2], BF16))
        NSLOT = 6
        WS = stack.enter_context(nc.sbuf_tensor("WS", [128, NSLOT * 4096], BF16))
        WPG = stack.enter_context(nc.sbuf_tensor("WPG", [128, 4, 2, 256], BF16))
        TM_BYTES = 20096
        TM = stack.enter_context(nc.sbuf_tensor("TM", [128, TM_BYTES // 2], BF16))
        GBC = stack.enter_context(nc.sbuf_tensor("GBC", [128, 2, D], F32))
        IDB = stack.enter_context(nc.sbuf_tensor("IDB", [128, 128], BF16))
        IDF = stack.enter_context(nc.sbuf_tensor("IDF", [128, 128], F32))
        CS = stack.enter_context(nc.sbuf_tensor("CS", [128, 96], F32))
        STT_ = stack.enter_context(nc.sbuf_tensor("STAT", [128, 64], F32))
        ZST = stack.enter_context(nc.sbuf_tensor("ZST", [128, KC, 2], F32))
        UST = stack.enter_context(nc.sbuf_tensor("UST", [128, KC, 15], F32))
        FAC = stack.enter_context(nc.sbuf_tensor("FAC", [128, 5, 16], F32))
        PS = stack.enter_context(nc.psum_tensor("PS", [128, 8, 512], F32))

        hT = _view(AR, A_H2, BF16, [KC, TMAX])
        ApT = _view(AR, A_AP, BF16, [KC, TMAX])
        plT = _view(AR, A_PL, BF16, [KC, TMAX])
        mgT = _view(AR, A_MG, BF16, [KC, TMAX])
        actT = _view(AR, A_ACT, BF16, [32, TMAX])
        x1 = _view(AR, A_X1, F32, [NT_MAX, D])
        h2T = _view(AR, A_H2, BF16, [KC, TMAX])
        US_ = _view(AR, A_X1, F32, [KC, 128])
        ZS_ = _view(AR, A_X1 + KC * 128 * 4, F32, [KC, 32])

        C_GPM = 0
        C_GPL = 8
        C_BG = 16
        C_WC = 32
        C_PS = 56
        C_EPS = 64

        def cs(col, n=1):
            return CS[:, col:col + n]

        stat_ctr = [0]

        def stat():
            i = stat_ctr[0] % 64
            stat_ctr[0] += 1
            return STT_[:, i:i + 1]

        bank_ctr = [0]
        bank_pool = [None]

        def bank():
            if bank_pool[0] is not None:
                pool_ = bank_pool[0]
                b = pool_[bank_ctr[0] % len(pool_)]
                bank_ctr[0] += 1
                return b
            b = bank_ctr[0] % 8
            bank_ctr[0] += 1
            return b

        def bank2():
            if bank_ctr[0] % 2:
                bank_ctr[0] += 1
            b = bank_ctr[0] % 8
            bank_ctr[0] += 2
            return b

        slot_ctr = [0]

        def wslot():
            s = slot_ctr[0] % NSLOT
            slot_ctr[0] += 1
            return s

        def s5_order(gi_):
            od = [(nh, kg) for nh in range(2) for kg in range(4)]
            return od if gi_ % 2 == 0 else od[::-1]

        plan = []
        for ps_i in range(npass):
            nt_ = NPP // 128 + (1 if (with_sample and ps_i == npass - 1) else 0)
            for b in range(2):
                plan.append([(w_in, 0, 0 * D + b * 512), (w_in, 0, 1 * D + b * 512), (w_in, 0, 2 * D + b * 512)])
            for b in range(2):
                plan.append([(w_in, 0, 3 * D + b * 512), (w_in, 0, 4 * D + b * 512), (w_in, 0, 5 * D + b * 512),
                             (w_out_conv, 0, b * 512)])
            plan.append([(w_o, 0, 0), (w_o, 0, 512)])
            for blk in range(DFF // 512):
                plan.append([(w_up, 0, blk * 512)])
            for gi_, a in enumerate(range(0, nt_, 3)):
                for (nh, kg) in s5_order(gi_):
                    plan.append([(w_down, kg * 1024, nh * 512)])
        entries = [e for g in plan for e in g]
        gfirst = []
        gof = []
        acc_ = 0
        for gi_, g in enumerate(plan):
            gfirst.append(acc_)
            acc_ += len(g)
            gof += [gi_] * len(g)
        keyof = [(id(W_), r_, c_) for (W_, r_, c_) in entries]
        slot_of, need_load, free_after = [], [], []
        content = [None] * NSLOT
        last_use = [-1] * NSLOT
        for e, k_ in enumerate(keyof):
            if k_ in content and e >= 3 and k_[0] == id(w_down):
                sl_ = content.index(k_)
                slot_of.append(sl_)
                need_load.append(False)
                free_after.append(-1)
            else:
                same_grp = [slot_of[e2] for e2 in range(gfirst[gof[e]], e)]
                best, best_key = None, None
                for sl_ in range(NSLOT):
                    if sl_ in same_grp:
                        continue
                    if content[sl_] is None or content[sl_][0] != id(w_down):
                        nu = 10 ** 9
                    else:
                        nu = 10 ** 9
                        for e2 in range(e + 1, min(len(keyof), e + 24)):
                            if keyof[e2] == content[sl_]:
                                nu = e2
                                break
                    cand = (nu, -last_use[sl_])
                    if best_key is None or cand > best_key:
                        best, best_key = sl_, cand
                slot_of.append(best)
                need_load.append(True)
                free_after.append(gof[last_use[best]] if last_use[best] >= 0 else -1)
                content[best] = k_
            last_use[slot_of[e]] = e
        wstate = {"g": 0, "issued": 0, "conv": 0}
        conv_ops = []

        class WV:
            def __init__(self, e, split):
                self.split = split
                if split:
                    self.v = _view(WS, slot_of[e] * 8192, BF16, [4, KC, 128])
                else:
                    self.v = _view(WS, slot_of[e] * 8192, BF16, [KC, 512])

            def lhsT(self, k, jj):
                return self.v[:, jj, k, :] if self.split else self.v[:, k, jj * 128:(jj + 1) * 128]

            def full(self, k):
                assert not self.split
                return self.v[:, k, :]

        def wslot_view(e):
            return WV(e, e < 3)

        def wgroup(expect=None, topup=False):
            if topup:
                wstate["g"] -= 1
            g = wstate["g"]
            wstate["g"] += 1
            first, cnt = gfirst[g], len(plan[g])
            if expect is not None:
                assert [(a is b_, r, c) for (a, r, c), (b_, r2, c2) in zip(plan[g], expect)] and \
                    all(a is b_ and r == r2 and c == c2 for (a, r, c), (b_, r2, c2) in zip(plan[g], expect)), g
            if 1 <= g <= 4 and not topup:
                for c in (2 * (g - 1), 2 * (g - 1) + 1):
                    kg_, nh_ = c // 2, c % 2
                    conv_ops.append(P.dma("pool", wd_bf[c, :, :].rearrange("p (k c) -> p k c", c=512),
                                          w_down[kg_ * 1024:(kg_ + 1) * 1024, nh_ * 512:(nh_ + 1) * 512].rearrange(
                                              "(k p) c -> p k c", p=128),
                                          "wdc", name="wdconv"))
                    if c == 7:
                        for op_ in conv_ops:
                            op_.dma = ("wdc", P.dma_cnt["wdc"])
            while wstate["issued"] < len(entries):
                e = wstate["issued"]
                own = e < first + cnt
                ahead_ok = (free_after[e] < g) and not (g == 0 and not topup)
                if not (own or ahead_ok):
                    break
                if own:
                    assert free_after[e] < g
                W, r0, c0 = entries[e]
                if not need_load[e]:
                    pass
                elif e < 3:
                    if e == 0:
                        for jj_ in range(4):
                            for e_ in range(3):
                                W_, r_0, c_0 = entries[e_]
                                src = W_[r_0:r_0 + KC * 128, c_0 + jj_ * 128:c_0 + (jj_ + 1) * 128].rearrange(
                                    "(k p) c -> p k c", p=128)
                                P.dma("pool", wslot_view(e_).v[:, jj_, :, :], src, "wi%d_%d" % (e_, jj_), name="wload0",
                                      deps=([xload0[min(3, len(xload0) - 1)]] if (jj_ > 0 and xload0) else []))
                elif W is w_down:
                    src = wd_bf[(r0 // 1024) * 2 + (c0 // 512), :, :].rearrange("p (k c) -> p k c", c=512)
                    P.dma("pool", wslot_view(e).v, src, "w%d" % slot_of[e], name="wload_bf", deps=conv_ops)
                else:
                    src = W[r0:r0 + KC * 128, c0:c0 + 512].rearrange("(k p) c -> p k c", p=128)
                    P.dma("pool", wslot_view(e).v, src, "w%d" % slot_of[e], name="wload")
                wstate["issued"] += 1
            return [wslot_view(first + i) for i in range(cnt)]

        CST = _view(TM, 8192, F32, [128])
        r_ = lambda v: v.rearrange("(k p) -> k p", p=128)
        P.dma("act", CST[0:8, :], r_(g_pre_mix), "const")
        P.dma("act", CST[8:16, :], r_(g_pre_mlp), "const")
        P.dma("act", CST[16:32, :], r_(b_gate), "const")
        P.dma("act", CST[32:56, :], w_conv.rearrange("w (k p) -> (w k) p", p=128), "const")
        P.dma("act", CST[56:64, :], r_(pool_scale), "const")
        tot = P.dma_cnt["const"]
        for op in P.streams["act"]:
            if op.dma is not None and op.dma[0] == "const":
                op.dma = ("const", tot)

        P.memset("pool", IDF[:, :], 0.0)
        P.add("pool", lambda e: e.affine_select(out=IDF[:, :], in_=IDF[:, :], compare_op=ALU.not_equal,
                                                 fill=1.0, base=0, pattern=[[-1, 128]], channel_multiplier=1),
              reads=[IDF[:, :]], writes=[IDF[:, :]], name="ident")
        P.copy("dve", IDB[:, :], IDF[:, :])
        P.act(STT_[:, 63:64], IDF[:, 0:1], AF.Square, name="act_table_warm")
        bt0 = 0
        P.tr(PS[:, bt0, 0:64], CST[0:64, :], IDF[0:64, 0:64])
        P.copy("dve", CS[:, 0:64], PS[:, bt0, 0:64])
        P.add("pool", lambda e: e.iota(FAC[:, 4, :], pattern=[[1, 16]], base=1, channel_multiplier=0,
                                       allow_small_or_imprecise_dtypes=True),
              writes=[FAC[:, 4, :]], name="iota")
        for g, w in enumerate(POOL_W):
            P.ts("dve", FAC[:, g, :], FAC[:, 4, :], float(w), None, ALU.min)
            P.add("dve", lambda e, g=g: e.reciprocal(FAC[:, g, :], FAC[:, g, :]),
                  reads=[FAC[:, g, :]], writes=[FAC[:, g, :]], name="recip")
            P.ts("dve", FAC[:, g, :], FAC[:, g, :], float(w), None, ALU.mult)
            P.ts("dve", cs(C_PS + 2 * g, 2), cs(C_PS + 2 * g, 2), 1.0 / w, None, ALU.mult)
        P.memset("dve", cs(C_EPS), EPS)
        P.memset("dve", ZST[:, :, :], 0.0)
        P.memset("dve", UST[:, :, :], 0.0)

        def tmv(off, dt, shape):
            return _view(TM, off, dt, shape)

        XT = [tmv(0 + i * 4096, F32, [D]) for i in range(2)]
        XN = [tmv(8192 + i * 2048, BF16, [D]) for i in range(2)]
        JUNK = tmv(12288, BF16, [D])
        XV = [tmv(0 + i * 2048, F32, [512]) for i in range(2)]
        ZC = [tmv(4096 + i * 2064, F32, [514]) for i in range(2)]
        TCV = [tmv(8224 + i * 2048, F32, [512]) for i in range(2)]
        ZSX = [tmv(12320 + i * 640, F32, [16, 10]) for i in range(2)]
        UC = [tmv(i * 2112, F32, [527]) for i in range(3)]
        PB = tmv(6336, F32, [527])
        QB = tmv(8448, F32, [527])
        PBP = tmv(10560, F32, [527])
        QBP = tmv(12672, F32, [527])
        TMPC = tmv(14784, F32, [16])
        USX = [_view(AR, A_X1 + 2 * 4096 + i * 1472, F32, [16, 23]) for i in range(2)]
        PSX = _view(AR, A_X1 + 2 * 4096 + 2944, F32, [16, 23])
        QSX = _view(AR, A_X1 + 2 * 4096 + 4416, F32, [16, 23])
        SPT = [_view(AR, A_X1 + (NT_MAX - 1) * 4096, F32, [KC, 120]), tmv(14848, F32, [KC, 120])]
        SCT = tmv(18688, F32, [KC, 32])
        XNF = tmv(8192, F32, [D])
        SG1 = [tmv(0 + i * 2048, F32, [512]) for i in range(2)]
        SG2 = [tmv(4096 + i * 2048, F32, [512]) for i in range(2)]
        MT1 = [tmv(8192 + i * 2048, F32, [512]) for i in range(2)]
        MT2 = [tmv(12288 + i * 2048, F32, [512]) for i in range(2)]
        RL = [tmv(i * 2048, F32, [512]) for i in range(3)]
        if NT_MAX >= 9:
            STGS = [_view(AR, A_X1 + (4 + i) * 4096, F32, [D]) for i in range(4)]
        else:
            STGS = [tmv(i * 4096, F32, [D]) for i in range(4)]

        ctr = {"xt": 0, "xn": 0, "t4k": 0, "cv": 0, "pl": 0, "sg": 0, "rl": 0, "sx": 0}

        def rot(key, n):
            i = ctr[key] % n
            ctr[key] += 1
            return i

        def rstd_from(ss):
            P.act(ss, ss, AF.Sqrt, bias=cs(C_EPS), scale=1.0 / D, name="sqrt")
            P.add("dve", lambda e: e.reciprocal(ss, ss), reads=[ss], writes=[ss], name="rstd")
            return ss

        def make_pass(ps_i):
            last_pass = ps_i == npass - 1
            has_s = with_sample and last_pass
            tiles = []
            for i in range(NPP // 128):
                r0 = ps_i * NPP + i * 128
                tiles.append((x_p[r0:r0 + 128, :], y_p[r0:r0 + 128, :]))
            if has_s:
                tiles.append((x_s[:, :], y_s[:, :]))
            tgs = [(g * 512, 512, "p") for g in range(ntg)]
            if has_s:
                tgs.append((NPP, 128, "s"))
            return dict(idx=ps_i, last_pass=last_pass, has_s=has_s, tiles=tiles, tgs=tgs, NT=len(tiles))

        passes = [make_pass(i) for i in range(npass)]

        def norm_prep(src_tok, xn_eng="act"):
            ss = stat()
            P.act(JUNK, src_tok, AF.Square, accum_out=ss, name="sq")
            rstd_from(ss)
            xn = XN[rot("xn", 2)]
            if xn_eng == "act":
                P.act(xn, src_tok, AF.Identity, scale=ss, name="xn")
            else:
                P.ts("dve", xn, src_tok, ss, None, ALU.mult, name="xn")
            return xn

        def norm_tr(xn, dstT, col0, gcol):
            b = bank()
            pb = PS[:, b, :].bitcast(BF16)
            for k in range(KC):
                P.tr(pb[:, k * 128:(k + 1) * 128], xn[:, k * 128:(k + 1) * 128], IDB[:, :], signal=(k == KC - 1))
            gb = bass.AP(CS, gcol, [[96, 128], [1, KC], [0, 128]])
            P.tt("dve", dstT[:, :, col0:col0 + 128], pb.rearrange("p (k t) -> p k t", t=128), gb, ALU.mult,
                 name="evacT")

        def s0_steps(pinfo, lo=0, hi=None, with_state=True):
            st = {}
            steps = []
            NT_ = pinfo["NT"]
            hi = NT_ if hi is None else hi

            def mk(u):
                def step():
                    if lo + 2 <= u:
                        norm_tr(st.pop(u - 2), hT, (u - 2) * 128, C_GPM)
                    if u < hi:
                        if pinfo["idx"] == 0:
                            if u == 0:
                                for i_ in range(NT_):
                                    xload0.append(P.dma("sp", x1[:, i_, :], pinfo["tiles"][i_][0], "x0_%d" % i_, name="xload0"))
                            xt = x1[:, u, :]
                        else:
                            xs_ = rot("xt", 2)
                            xt = XT[xs_]
                            P.dma("sp", xt, pinfo["tiles"][u][0], "xt%d" % xs_, name="xload")
                        st[u] = norm_prep(xt, "dve" if pinfo["idx"] == 0 else "act")
                return step
            for u in range(lo, hi + 2):
                steps.append(mk(u))
            if pinfo["has_s"] and with_state:
                def ld():
                    P.dma("sp", XT[0][0:120, :], st_pool[0:120, :], "xt0", name="stload")
                    P.dma("sp", XT[1][0:120, :], st_pool[120:240, :], "xt1", name="stload")
                    P.dma("sp", XNF[0:32, :], st_conv[:, :], "stc", name="stload")

                def trs(which):
                    def f():
                        ci = 0
                        for j in range(KC):
                            for (src, nrow, dst) in which(j):
                                bt = bank()
                                P.tr(PS[:, bt, 0:nrow], src, IDF[0:nrow, 0:nrow])
                                P.copy("dve" if ci % 2 == 0 else "act", dst, PS[:, bt, 0:nrow])
                                ci += 1
                    return f
                steps.append(ld)
                steps.append(lambda: None)
                steps.append(trs(lambda j: [(XNF[0:32, j * 128:(j + 1) * 128], 32, SCT[:, j, :]),
                                            (XT[0][0:120, j * 128:(j + 1) * 128], 120, SPT[0][:, j, :])]))
                steps.append(trs(lambda j: [(XT[1][0:120, j * 128:(j + 1) * 128], 120, SPT[1][:, j, :])]))
            return steps

        xload0 = []
        if passes[0]["NT"] >= 8:
            pending_s0 = s0_steps(passes[0], 0, 4)
            s0_rest_p0 = s0_steps(passes[0], 4, None)
        else:
            pending_s0 = s0_steps(passes[0])
            s0_rest_p0 = []

        for ps_i in range(npass):
            pinfo = passes[ps_i]
            last_pass, has_s, tiles, tgs, NT = pinfo["last_pass"], pinfo["has_s"], pinfo["tiles"], pinfo["tgs"], pinfo["NT"]
            T = NT * 128

            s0_rest = s0_rest_p0 if ps_i == 0 else []
            for step in pending_s0:
                step()
            pending_s0 = []
            if ps_i == 0:
                gd_ = [xload0[-1]] if xload0 else []
                P.dma("sp", GBC[:, 0, :], bass.AP(g_post_mix.tensor, 0, [[0, 128], [1, D]]), "gbc0", deps=gd_)
                P.dma("sp", GBC[:, 1, :], bass.AP(g_post_mlp.tensor, 0, [[0, 128], [1, D]]), "gbc1", deps=gd_)

            for b in range(2):
                wxv, wbg, wcg = wgroup([(w_in, 0, 0 * D + b * 512), (w_in, 0, 1 * D + b * 512), (w_in, 0, 2 * D + b * 512)])
                for jj in range(4):
                    j = b * 4 + jj
                    prev_zc = None
                    for (c0, n, kind) in tgs:
                        bxv, bcg, bbg = bank(), bank(), bank()
                        for (bk, wt) in ((bxv, wxv), (bcg, wcg), (bbg, wbg)):
                            for k in range(KC):
                                P.mm(PS[:, bk, 0:n], wt.lhsT(k, jj), hT[:, k, c0:c0 + n],
                                     start=(k == 0), stop=(k == KC - 1))
                        si = rot("cv", 2)
                        xv = XV[si]
                        tcv = TCV[si]
                        if kind == "p":
                            zc = ZC[si]
                            zsl = lambda a, bb, zc=zc: zc[:, a:bb]
                            pview = lambda ap: ap
                            if prev_zc is None:
                                P.copy("dve", zc[:, 0:2], ZST[:, j, :])
                            else:
                                P.copy("dve", zc[:, 0:2], prev_zc[:, 512:514])
                            nn = n
                            xv_v, tcv_v = xv[:, 0:n], tcv[:, 0:n]
                            pcg, pbg, pxv = PS[:, bcg, 0:n], PS[:, bbg, 0:n], PS[:, bxv, 0:n]
                        else:
                            zc = ZSX[rot("sx", 2)]
                            zsl = lambda a, bb, zc=zc: zc[:, :, a:bb]
                            nn = 8
                            r3 = lambda ap: ap.rearrange("p (s t) -> p s t", t=8)
                            xv_v, tcv_v = r3(xv[:, 0:n]), r3(tcv[:, 0:n])
                            pcg, pbg, pxv = r3(PS[:, bcg, 0:n]), r3(PS[:, bbg, 0:n]), r3(PS[:, bxv, 0:n])
                            P.copy("dve", zc[:, :, 0:2], SCT[:, j, :].rearrange("p (s t) -> p s t", t=2))
                        P.copy("act", xv_v, pxv, name="xv")
                        P.tt("dve", zsl(2, 2 + nn), pcg, xv_v, ALU.mult, name="z")
                        P.act(tcv_v, zsl(0, nn), AF.Identity, scale=cs(C_WC + 0 * 8 + j), name="cv0")
                        P.stt("dve", tcv_v, zsl(1, 1 + nn), cs(C_WC + 1 * 8 + j), tcv_v, ALU.mult, ALU.add, name="cv1")
                        P.stt("dve", tcv_v, zsl(2, 2 + nn), cs(C_WC + 2 * 8 + j), tcv_v, ALU.mult, ALU.add, name="cv2")
                        if kind == "p":
                            P.tt("dve", ApT[:, j, c0:c0 + n], pbg, tcv_v, ALU.mult, name="Ap")
                            prev_zc = zc
                            if c0 + n == NPP:
                                P.copy("dve", ZST[:, j, :], zc[:, 512:514])
                        else:
                            P.tt("dve", ApT[:, j, c0:c0 + n].rearrange("p (s t) -> p s t", t=8), pbg, tcv_v, ALU.mult,
                                 name="Ap")
                            P.copy("dve", ZS_[:, j, :].rearrange("p (s t) -> p s t", t=2), zc[:, :, 8:10])
                        if ps_i == 0 and j == 0 and c0 == 0:
                            for step in s0_rest:
                                step()
                            s0_rest = []
                            wgroup(topup=True)
                            P.dma("pool", WPG[:, :, :, :], w_pool_group.rearrange("g (c p) o -> p g c o", p=128), "wpg")

            deferred = []

            def flush_deferred():
                while deferred:
                    deferred.pop(0)[1]()

            def s1b_chunk(j, jj, wu):
                g = j // 2
                w = POOL_W[g]
                prev_uc = None
                for (c0, n, kind) in tgs:
                    bu = bank()
                    for k in range(KC):
                        P.mm(PS[:, bu, 0:n], wu.lhsT(k, jj), hT[:, k, c0:c0 + n],
                             start=(k == 0), stop=(k == KC - 1))
                    pe_ = "pool" if (w > 2 and ((kind == "p" and (c0 // 512) % 2 == 0) or (kind == "s" and j % 2 == 0))) else "dve"
                    if kind == "p":
                        si = rot("pl", 3)
                        uc, pb_, qb_ = UC[si], PB, QB
                        if pe_ == "pool":
                            pb_, qb_ = PBP, QBP
                        sl = lambda buf, a, bb: buf[:, a:bb]
                        nn = n
                        if prev_uc is None:
                            P.copy("dve", uc[:, 0:15], UST[:, j, :])
                        else:
                            P.copy("dve", uc[:, 0:15], prev_uc[:, 512:527])
                        pu = PS[:, bu, 0:n]
                        outv = plT[:, j, c0:c0 + n]
                    else:
                        uc, pb_, qb_ = USX[j % 2], PSX, QSX
                        sl = lambda buf, a, bb: buf[:, :, a:bb]
                        nn = 8
                        pu = PS[:, bu, 0:n].rearrange("p (s t) -> p s t", t=8)
                        outv = plT[:, j, c0:c0 + n].rearrange("p (s t) -> p s t", t=8)
                        for h in range(2):
                            P.copy("dve" if h == 0 else "act", uc[:, h * 8:(h + 1) * 8, 0:15],
                                   SPT[h][:, j, :].rearrange("p (s t) -> p s t", t=15))
                    E = 15 + nn
                    if pe_ == "pool":
                        for d_ in [d for d in deferred if d[0] == kind]:
                            deferred.remove(d_)
                            d_[1]()
                    P.copy("act", sl(uc, 15, E), pu, name="u")
                    if w == 2:
                        P.tt(pe_, outv, sl(uc, 14, E - 1), sl(uc, 15, E), ALU.subtract, name="pool2")
                        sw = None
                    else:
                        P.tt(pe_, sl(pb_, 1, E), sl(uc, 1, E), sl(uc, 0, E - 1), ALU.add, name="s2")
                        P.tt(pe_, sl(qb_, 3, E), sl(pb_, 3, E), sl(pb_, 1, E - 2), ALU.add, name="s4")
                        sw = qb_
                        if w >= 8:
                            P.tt(pe_, sl(pb_, 7, E), sl(qb_, 7, E), sl(qb_, 3, E - 4), ALU.add, name="s8")
                            sw = pb_
                        if w >= 16:
                            P.tt(pe_, sl(qb_, 15, E), sl(pb_, 15, E), sl(pb_, 7, E - 8), ALU.add, name="s16")
                            sw = qb_
                    def fin(w=w, outv=outv, uc=uc, sw=sw, sl=sl, E=E, kind=kind, c0=c0, j=j, g=g):
                        if w > 2:
                            P.stt("dve", outv, sl(uc, 15, E), -float(w), sl(sw, 15, E), ALU.mult, ALU.add, name="pooled")
                        if kind == "p" and ps_i == 0 and c0 == 0:
                            if w == 2:
                                P.memset("dve", plT[:, j, 0:1], 0.0)
                            else:
                                nfix = w - 1
                                P.tt("dve", TMPC[:, 0:nfix], sw[:, 15:15 + nfix], FAC[:, g, 0:nfix], ALU.mult)
                                P.stt("dve", plT[:, j, 0:nfix], uc[:, 15:15 + nfix], -float(w), TMPC[:, 0:nfix],
                                      ALU.mult, ALU.add)
                    if kind == "p":
                        prev_uc = uc
                        if c0 + n == NPP:
                            P.copy("act", UST[:, j, :], uc[:, 512:527])
                    else:
                        P.copy("act", US_[:, j, :].rearrange("p (s t) -> p s t", t=8), uc[:, :, 15:23])
                    if pe_ == "pool":
                        deferred.append((kind, fin))
                    else:
                        fin()
                    while len(deferred) > (1 if pe_ == "pool" else 0):
                        deferred.pop(0)[1]()

            def emit_state_outputs():
                if last_pass:
                    b2 = bank2()
                    for j in range(KC):
                        P.tr(PS[0:2, b2 + j // 4, (j % 4) * 128:(j % 4 + 1) * 128], ZST[:, j, :], IDF[:, :])
                    pz = bass.AP(PS, b2 * 512, [[4096, 2], [1, D]])
                    P.copy("dve", STGS[0][0:2, :], pz)
                    P.dma("sp", ncp[:, :], STGS[0][0:2, :], "so0", is_output=True)
                    b2 = bank2()
                    for j in range(KC):
                        P.tr(PS[0:15, b2 + j // 4, (j % 4) * 128:(j % 4 + 1) * 128], UST[:, j, :], IDF[:, :])
                    pz = bass.AP(PS, b2 * 512, [[4096, 15], [1, D]])
                    P.copy("act", STGS[1][0:15, :], pz)
                    P.dma("sp", npp[:, :], STGS[1][0:15, :], "so1", is_output=True)
                    if has_s:
                        b2 = bank2()
                        for j in range(KC):
                            P.tr(PS[0:32, b2 + j // 4, (j % 4) * 128:(j % 4 + 1) * 128], ZS_[:, j, :], IDF[:, :])
                        pz = bass.AP(PS, b2 * 512, [[4096, 32], [1, D]])
                        P.copy("dve", STGS[2][0:32, :], pz)
                        P.dma("sp", ncs[:, :], STGS[2][0:32, :], "so2", is_output=True)
                        b2 = bank2()
                        for j in range(KC):
                            P.tr(PS[:, b2 + j // 4, (j % 4) * 128:(j % 4 + 1) * 128], US_[:, j, :], IDF[:, :])
                        pz = bass.AP(PS, b2 * 512, [[4096, 128], [1, D]])
                        P.copy("act", STGS[3][:, :], pz)
                        for s in range(NSEQ):
                            P.dma("sp", nps[s * 15 + 7:s * 15 + 15, :], STGS[3][s * 8:(s + 1) * 8, :], "so3", is_output=True)
                        P.dma("sp", nps.rearrange("(s r) d -> s r d", r=15)[:, 0:7, :],
                              st_pool.rearrange("(s r) d -> s r d", r=15)[:, 8:15, :], "so4", is_output=True)


            if has_s:
                S2T = [_view(AR, A_X1 + 5120, F32, [512]), _view(AR, A_X1 + 14080, F32, [512])]
            else:
                S2T = [tmv(14848, F32, [512]), tmv(16896, F32, [512])]

            def s2_chunk(m, jj, wgc, wgp, woc):
                flush_deferred()
                g = m // 2
                for (c0, n, kind) in tgs:
                    bA, bB, bg1, bg2 = bank(), bank(), bank(), bank()
                    for k in range(KC):
                        P.mm(PS[:, bg1, 0:n], wgc.lhsT(k, jj), hT[:, k, c0:c0 + n],
                             start=(k == 0), stop=(k == KC - 1))
                    for k in range(KC):
                        P.mm(PS[:, bg2, 0:n], wgp.lhsT(k, jj), hT[:, k, c0:c0 + n],
                             start=(k == 0), stop=(k == KC - 1))
                    for k in range(KC):
                        P.mm(PS[:, bA, 0:n], woc.lhsT(k, jj), ApT[:, k, c0:c0 + n],
                             start=(k == 0), stop=(k == KC - 1))
                    for c in range(2):
                        P.mm(PS[:, bB, 0:n], WPG[:, g, c, (m % 2) * 128:(m % 2 + 1) * 128],
                             plT[:, 2 * g + c, c0:c0 + n], start=(c == 0), stop=(c == 1))
                    s1, s2 = S2T[0][:, 0:n], S2T[1][:, 0:n]
                    t1, t2 = s1, s2
                    P.act(s1, PS[:, bg1, 0:n], AF.Sigmoid, bias=cs(C_BG + m), name="sig1")
                    P.act(s2, PS[:, bg2, 0:n], AF.Sigmoid, bias=cs(C_BG + 8 + m), name="sig2")
                    P.tt("dve", t1, PS[:, bA, 0:n], s1, ALU.mult, name="mA")
                    P.stt("dve", t2, PS[:, bB, 0:n], cs(C_PS + m), s2, ALU.mult, ALU.mult, name="mB")
                    P.tt("dve", mgT[:, m, c0:c0 + n], t1, t2, ALU.add, name="merge")

            for b in range(2):
                wu, wgc, wgp, woc = wgroup([(w_in, 0, 3 * D + b * 512), (w_in, 0, 4 * D + b * 512),
                                            (w_in, 0, 5 * D + b * 512), (w_out_conv, 0, b * 512)])
                c_ = b * 4
                s1b_chunk(c_ + 0, 0, wu)
                s1b_chunk(c_ + 1, 1, wu)
                s2_chunk(c_ + 0, 0, wgc, wgp, woc)
                s1b_chunk(c_ + 2, 2, wu)
                s2_chunk(c_ + 1, 1, wgc, wgp, woc)
                s1b_chunk(c_ + 3, 3, wu)
                s2_chunk(c_ + 2, 2, wgc, wgp, woc)
                if b == 1:
                    emit_state_outputs()
                s2_chunk(c_ + 3, 3, wgc, wgp, woc)
            flush_deferred()

            wo = wgroup([(w_o, 0, 0), (w_o, 0, 512)])
            s3b, s3x = {}, {}

            def s3_A(i):
                b2 = bank2()
                for nh in range(2):
                    for k in range(KC):
                        P.mm(PS[:, b2 + nh, :], mgT[:, k, i * 128:(i + 1) * 128], wo[nh].full(k),
                             start=(k == 0), stop=(k == KC - 1))
                s3b[i] = b2

            def s3_B1(i):
                pm = bass.AP(PS, s3b.pop(i) * 512, [[4096, 128], [1, D]])
                ss = stat()
                P.act(JUNK, pm, AF.Square, accum_out=ss, name="sq_mix")
                rstd_from(ss)
                xs_ = rot("xt", 2)
                xt = XT[xs_]
                if ps_i == 0:
                    P.stt("dve", xt, pm, ss, GBC[:, 0, :], ALU.mult, ALU.mult, name="mixn")
                else:
                    P.dma("sp", xt, tiles[i][0], "xt%d" % xs_, name="xreload")
                    P.stt("dve", x1[:, i, :], pm, ss, GBC[:, 0, :], ALU.mult, ALU.mult, name="mixn")
                P.tt("pool", x1[:, i, :], x1[:, i, :], xt, ALU.add, name="x1")

            def s3_B2(i):
                s3x[i] = norm_prep(x1[:, i, :])

            def s3_C(i):
                norm_tr(s3x.pop(i), h2T, i * 128, C_GPL)

            s4_early = {"w": None, "done": set()}

            def s4_unit(wu_, f, jj, c0, n):
                bu = bank()
                for k in range(KC):
                    P.mm(PS[:, bu, 0:n], wu_.lhsT(k, jj), h2T[:, k, c0:c0 + n],
                         start=(k == 0), stop=(k == KC - 1))
                rl = RL[rot("rl", 3)][:, 0:n]
                P.act(rl, PS[:, bu, 0:n], AF.Relu, name="relu")
                P.tt("dve", actT[:, f, c0:c0 + n], rl, rl, ALU.mult, name="sqr")

            for u in range(NT + 3):
                if u < NT:
                    s3_A(u)
                if 1 <= u <= NT:
                    s3_B1(u - 1)
                if 2 <= u <= NT + 1:
                    s3_B2(u - 2)
                if u >= 3:
                    s3_C(u - 3)
                u0_ = max(7, NT)
                if u >= u0_ and u - u0_ < 4:
                    if s4_early["w"] is None:
                        s4_early["w"], = wgroup([(w_up, 0, 0)])
                    jj_ = u - u0_
                    s4_unit(s4_early["w"], jj_, jj_, tgs[0][0], tgs[0][1])
                    s4_early["done"].add((jj_, tgs[0][0]))

            for blk in range(DFF // 512):
                if blk == 0 and s4_early["w"] is not None:
                    wu_ = s4_early["w"]
                else:
                    wu_, = wgroup([(w_up, 0, blk * 512)])
                for jj in range(4):
                    f = blk * 4 + jj
                    for (c0, n, kind) in tgs:
                        if (f, c0) in s4_early["done"]:
                            continue
                        s4_unit(wu_, f, jj, c0, n)

            groups = [list(range(a, min(a + 3, NT))) for a in range(0, NT, 3)]
            if not last_pass:
                pending_s0 = s0_steps(passes[ps_i + 1])
            unit_i = 0
            bank_pool[0] = [6, 7]
            for G in groups:
                bk = {}
                for gi_, i in enumerate(G):
                    bk[i] = 2 * gi_
                od_ = s5_order(groups.index(G))
                for (nh, kg) in od_:
                    kgs_ = [kg2 for (nh2, kg2) in od_ if nh2 == nh]
                    first_k, last_k = (kg == kgs_[0]), (kg == kgs_[-1])
                    wd, = wgroup([(w_down, kg * 1024, nh * 512)])
                    if unit_i >= 1 and pending_s0:
                        pending_s0.pop(0)()
                    unit_i += 1
                    for i in G:
                        for kk in range(KC):
                            P.mm(PS[:, bk[i] + nh, :], actT[:, kg * 8 + kk, i * 128:(i + 1) * 128], wd.full(kk),
                                 start=(first_k and kk == 0), stop=(last_k and kk == KC - 1),
                                 signal=((last_k or i == G[-1]) and kk == KC - 1))
                for i in G:
                    pm = bass.AP(PS, bk[i] * 512, [[4096, 128], [1, D]])
                    ss = stat()
                    P.act(JUNK, pm, AF.Square, accum_out=ss, name="sq_mlp")
                    rstd_from(ss)
                    t4 = XT[rot("xt", 2)]
                    P.stt("dve", t4, pm, ss, GBC[:, 1, :], ALU.mult, ALU.mult, name="mlpn")
                    P.tt("dve", x1[:, i, :], x1[:, i, :], t4, ALU.add, name="y")
                    P.dma("sp", tiles[i][1], x1[:, i, :], "yo%d" % i, name="ystore", is_output=True)
            for step in pending_s0:
                step()
            pending_s0 = []
            bank_pool[0] = None

        nwait = P.emit(stack)
    return nc


_NC_CACHE = {}


def kernel(**inputs):
    f = lambda a: np.ascontiguousarray(np.asarray(a, dtype=np.float32))
    x_prompt = f(inputs["x_prompt"])
    x_sample = f(inputs["x_sample"])
    state_conv = f(inputs["state_conv"])
    state_pool = f(inputs["state_pool"])
    shared = {
        "g_pre_mix": f(inputs["g_pre_mix"]).reshape(D),
        "w_in": f(inputs["w_in"]).reshape(D, 6 * D),
        "b_gate": f(inputs["b_gate"]).reshape(2 * D),
        "w_conv": f(inputs["w_conv"]).reshape(3, D),
        "w_out_conv": f(inputs["w_out_conv"]).reshape(D, D),
        "w_pool_group": f(inputs["w_pool_group"]).reshape(4, 256, 256),
        "pool_scale": f(inputs["pool_scale"]).reshape(D),
        "w_o": f(inputs["w_o"]).reshape(D, D),
        "g_post_mix": f(inputs["g_post_mix"]).reshape(D),
        "g_pre_mlp": f(inputs["g_pre_mlp"]).reshape(D),
        "w_up": f(inputs["w_up"]).reshape(D, DFF),
        "w_down": f(inputs["w_down"]).reshape(DFF, D),
        "g_post_mlp": f(inputs["g_post_mlp"]).reshape(D),
    }
    in_maps = []
    for c in range(NCORES):
        m = dict(shared)
        m["x_p"] = x_prompt[c]
        m["x_s"] = x_sample[16 * c:16 * (c + 1)].reshape(128, D)
        m["st_conv"] = state_conv[0, 16 * c:16 * (c + 1)].reshape(32, D)
        m["st_pool"] = state_pool[0, 16 * c:16 * (c + 1)].reshape(240, D)
        in_maps.append(m)
    if "nc" not in _NC_CACHE:
        _NC_CACHE["nc"] = build_program()
    nc = _NC_CACHE["nc"]
    res = run_bass_kernel_spmd(nc, in_maps, core_ids=list(range(NCORES)))
    R = res.results
    y_prompt = np.stack([R[c]["y_p"] for c in range(NCORES)], axis=0)
    y_sample = np.concatenate([R[c]["y_s"].reshape(16, 8, D) for c in range(NCORES)], axis=0)
    ncp = np.stack([R[c]["ncp"] for c in range(NCORES)], axis=0)[None]
    npp = np.stack([R[c]["npp"] for c in range(NCORES)], axis=0)[None]
    ncs = np.concatenate([R[c]["ncs"].reshape(16, 2, D) for c in range(NCORES)], axis=0)[None]
    nps = np.concatenate([R[c]["nps"].reshape(16, 15, D) for c in range(NCORES)], axis=0)[None]
    return (y_prompt.astype(np.float32), y_sample.astype(np.float32), ncp.astype(np.float32),
            npp.astype(np.float32), ncs.astype(np.float32), nps.astype(np.float32))
```

```python
import numpy as np
from contextlib import ExitStack
import concourse.bass as bass
import concourse.mybir as mybir
from concourse.bass_utils import run_bass_kernel_spmd

F32 = mybir.dt.float32
BF16 = mybir.dt.bfloat16
ALU = mybir.AluOpType
AF = mybir.ActivationFunctionType
SZ = {F32: 4, BF16: 2}

D = 1024
KC = 8
DFF = 4096
EPS = 1e-6
NCORES = 8
POOL_W = (2, 4, 8, 16)


def _prod(xs):
    r = 1
    for x in xs:
        r *= int(x)
    return r


class _Op:
    __slots__ = ("eng", "fn", "deps", "signal", "pos", "dma", "name", "sig", "wait_val")

    def __init__(self, eng, fn, name):
        self.eng = eng
        self.fn = fn
        self.deps = set()
        self.signal = True
        self.pos = None
        self.dma = None
        self.name = name
        self.sig = None
        self.wait_val = None


class Prog:
    ENG = ("pe", "act", "dve", "pool", "sp")

    def __init__(self, nc):
        self.nc = nc
        self.engs = {"pe": nc.tensor, "act": nc.scalar, "dve": nc.vector,
                     "pool": nc.gpsimd, "sp": nc.sync}
        self.streams = {e: [] for e in self.ENG}
        self.recs = {}
        self.dma_cnt = {}
        self.out_dmas = []

    @staticmethod
    def _iv(ap):
        t = ap.tensor
        if type(t).__name__.startswith("DRam"):
            return None
        esz = SZ[ap.dtype]
        pbytes = _prod(t.shape[1:]) * SZ[t.dtype]
        off = int(ap.offset) * esz
        lo = off % pbytes
        dims = list(ap.ap)
        p0 = off // pbytes
        p1 = p0 + ((int(dims[0][1]) - 1) * abs(int(dims[0][0])) * esz) // pbytes + 1
        fd = sorted([(abs(int(st)) * esz, int(cnt)) for (st, cnt) in dims[1:] if int(cnt) > 1], key=lambda d: -d[0])

        def rec(base, ds):
            if not ds:
                return [(base, base + esz)]
            inner = esz + sum((c - 1) * s_ for s_, c in ds[1:])
            s0, c0 = ds[0]
            if s0 >= 2 * inner and c0 <= 32:
                out = []
                for i in range(c0):
                    out += rec(base + i * s0, ds[1:])
                return out
            return [(base, base + esz + sum((c - 1) * s_ for s_, c in ds))]
        ivs = rec(lo, fd)
        if len(ivs) > 64:
            ivs = [(lo, lo + esz + sum((c - 1) * s_ for s_, c in fd))]
        return t.name, ivs, p0, p1

    def _acc(self, op, ap, is_write):
        iv = self._iv(ap)
        if iv is None:
            return
        key, ivs, p0, p1 = iv
        for (lo, hi) in ivs:
            self._acc1(op, key, lo, hi, p0, p1, is_write)

    def _acc1(self, op, key, lo, hi, p0, p1, is_write):
        recs = self.recs.get(key, [])
        new = []
        for r in recs:
            rlo, rhi, rop, rw, rp0, rp1 = r
            if rhi <= lo or rlo >= hi or rp1 <= p0 or rp0 >= p1:
                new.append(r)
                continue
            if (is_write or rw) and rop is not op:
                op.deps.add(rop)
            cov = rlo >= lo and rhi <= hi and rp0 >= p0 and rp1 <= p1
            if is_write and cov:
                continue
            if (not is_write) and (not rw) and rop.dma is None and op.dma is None \
                    and rop.eng == op.eng and cov:
                continue
            new.append(r)
        new.append((lo, hi, op, is_write, p0, p1))
        self.recs[key] = new

    def add(self, eng, fn, reads=(), writes=(), signal=True, name="", dma=None):
        op = _Op(eng, fn, name)
        op.signal = signal
        op.dma = dma
        for ap in reads:
            if ap is not None and not isinstance(ap, (int, float)):
                self._acc(op, ap, False)
        for ap in writes:
            self._acc(op, ap, True)
        op.pos = len(self.streams[eng])
        self.streams[eng].append(op)
        return op

    def dma(self, q, out, in_, semkey, name="", is_output=False, deps=(), **kw):
        cnt = self.dma_cnt.get(semkey, 0) + 16
        self.dma_cnt[semkey] = cnt
        op = self.add(q, lambda e: e.dma_start(out=out, in_=in_, **kw), reads=[in_], writes=[out],
                      name=name, dma=(semkey, cnt))
        op.deps.update(deps)
        if is_output:
            self.out_dmas.append(op)
        return op

    def mm(self, out, lhsT, rhs, start, stop, signal=None):
        sig = stop if signal is None else signal
        return self.add("pe", lambda e: e.matmul(out, lhsT, rhs, start=start, stop=stop),
                        reads=[lhsT, rhs], writes=[out], signal=sig, name="mm")

    def tr(self, out, in_, ident, signal=True):
        return self.add("pe", lambda e: e.transpose(out, in_, ident), reads=[in_, ident], writes=[out],
                        signal=signal, name="tr")

    def act(self, out, in_, func, bias=None, scale=None, accum_out=None, name="act"):
        kw = {}
        if bias is not None:
            kw["bias"] = bias
        if scale is not None:
            kw["scale"] = scale
        if accum_out is not None:
            kw["accum_out"] = accum_out
        rd = [in_] + [x for x in (bias, scale) if x is not None and not isinstance(x, (int, float))]
        wr = [out] + ([accum_out] if accum_out is not None else [])
        return self.add("act", lambda e: e.activation(out, in_, func, **kw), reads=rd, writes=wr, name=name)

    def tt(self, eng, out, in0, in1, op, name="tt"):
        return self.add(eng, lambda e: e.tensor_tensor(out, in0, in1, op), reads=[in0, in1], writes=[out], name=name)

    def ts(self, eng, out, in0, s1, s2, op0, op1=None, name="ts"):
        rd = [in0] + [x for x in (s1, s2) if x is not None and not isinstance(x, (int, float))]
        if op1 is None:
            return self.add(eng, lambda e: e.tensor_scalar(out, in0, s1, s2, op0), reads=rd, writes=[out], name=name)
        return self.add(eng, lambda e: e.tensor_scalar(out, in0, s1, s2, op0, op1), reads=rd, writes=[out], name=name)

    def stt(self, eng, out, in0, scalar, in1, op0, op1, name="stt"):
        rd = [in0, in1] + ([scalar] if not isinstance(scalar, (int, float)) else [])
        return self.add(eng, lambda e: e.scalar_tensor_tensor(out, in0, scalar, in1, op0, op1),
                        reads=rd, writes=[out], name=name)

    def copy(self, eng, out, in_, name="copy"):
        if eng == "act":
            return self.act(out, in_, AF.Identity, name=name)
        return self.add(eng, lambda e: e.tensor_copy(out, in_), reads=[in_], writes=[out], name=name)

    def memset(self, eng, out, val, name="memset"):
        return self.add(eng, lambda e: e.memset(out, val), writes=[out], name=name)

    def emit(self, stack):
        nc = self.nc
        sems = {}
        for e in ("pe", "act", "dve", "pool"):
            sems[e] = stack.enter_context(nc.semaphore("s_" + e))
        for k in self.dma_cnt:
            sems[("dma", k)] = stack.enter_context(nc.semaphore("d_" + str(k)))
        fin = _Op("sp", None, "final")
        fin.deps = set(self.out_dmas)
        self.streams["sp"].append(fin)
        for e in self.ENG:
            c = 0
            for op in self.streams[e]:
                if op.dma is None and op.fn is not None and op.signal:
                    c += 1
                    op.sig = c
            nxt = None
            for op in reversed(self.streams[e]):
                if op.dma is None and op.fn is not None:
                    if op.signal:
                        nxt = op.sig
                    op.wait_val = nxt
        nwait = 0
        for e in self.ENG:
            eng = self.engs[e]
            waited = {}
            for op in self.streams[e]:
                need = {}
                for d in op.deps:
                    if d.dma is not None:
                        key = ("dma", d.dma[0])
                        val = d.dma[1]
                    else:
                        if d.eng == "pe" and e == "pe":
                            continue
                        key = d.eng
                        val = d.wait_val
                        assert val is not None, (d.name, op.name)
                    if waited.get(key, 0) >= val:
                        continue
                    if need.get(key, 0) < val:
                        need[key] = val
                for key, val in need.items():
                    waited[key] = val
                    eng.wait_ge(sems[key], val)
                    nwait += 1
                if op.fn is not None:
                    ins = op.fn(eng)
                    if op.dma is not None:
                        ins.then_inc(sems[("dma", op.dma[0])], 16)
                    elif op.signal:
                        ins.then_inc(sems[e], 1)
        return nwait


def _view(t, boff, dt, shape):
    n = _prod(shape)
    esz = SZ[dt]
    assert boff % 4 == 0
    e0 = boff // 2
    e1 = e0 + n * esz // 2
    ap = t[:, e0:e1]
    if dt != BF16:
        ap = ap.bitcast(dt)
    if len(shape) == 2:
        ap = ap.rearrange("p (a b) -> p a b", b=shape[1])
    elif len(shape) == 3:
        ap = ap.rearrange("p (a b c) -> p a b c", b=shape[1], c=shape[2])
    return ap


def build_program(ntg=2, with_sample=True, npass=2):
    nc = bass.Bass("TRN2", target_bir_lowering=False)
    P = Prog(nc)
    NPP = 512 * ntg
    NPT = NPP * npass
    NSEQ = 16
    NST = NSEQ * 8
    TMAX = NPP + (NST if with_sample else 0)

    def din(name, shape):
        return nc.dram_tensor(name, list(shape), F32, kind="ExternalInput").ap()

    def dout(name, shape):
        return nc.dram_tensor(name, list(shape), F32, kind="ExternalOutput").ap()

    x_p = din("x_p", [NPT, D])
    x_s = din("x_s", [NST, D])
    st_conv = din("st_conv", [NSEQ * 2, D])
    st_pool = din("st_pool", [NSEQ * 15, D])
    g_pre_mix = din("g_pre_mix", [D])
    w_in = din("w_in", [D, 6 * D])
    b_gate = din("b_gate", [2 * D])
    w_conv = din("w_conv", [3, D])
    w_out_conv = din("w_out_conv", [D, D])
    w_pool_group = din("w_pool_group", [4, 256, 256])
    pool_scale = din("pool_scale", [D])
    w_o = din("w_o", [D, D])
    g_post_mix = din("g_post_mix", [D])
    g_pre_mlp = din("g_pre_mlp", [D])
    w_up = din("w_up", [D, DFF])
    w_down = din("w_down", [DFF, D])
    g_post_mlp = din("g_post_mlp", [D])

    y_p = dout("y_p", [NPT, D])
    y_s = dout("y_s", [NST, D])
    ncp = dout("ncp", [2, D])
    npp = dout("npp", [15, D])
    ncs = dout("ncs", [NSEQ * 2, D])
    nps = dout("nps", [NSEQ * 15, D])

    wd_bf = nc.dram_tensor("wd_bf", [8, 128, KC * 512], BF16, kind="Internal").ap()

    stack = ExitStack()
    with stack:
        NT_MAX = TMAX // 128
        A_ACT = 0
        A_HT = 0
        A_AP = KC * TMAX * 2
        A_PL = 2 * KC * TMAX * 2
        A_MG = 3 * KC * TMAX * 2
        A_X1 = 32 * TMAX * 2
        A_H2 = A_X1 + NT_MAX * 4096
        A_END = A_H2 + KC * TMAX * 2
        AR = stack.enter_context(nc.sbuf_tensor("AR", [128, A_END // 2], BF16))
        NSLOT = 6
        WS = stack.enter_context(nc.sbuf_tensor("WS", [128, NSLOT * 4096], BF16))
        WPG = stack.enter_context(nc.sbuf_tensor("WPG", [128, 4, 2, 256], BF16))
        TM_BYTES = 20096
        TM = stack.enter_context(nc.sbuf_tensor("TM", [128, TM_BYTES // 2], BF16))
        GBC = stack.enter_context(nc.sbuf_tensor("GBC", [128, 2, D], F32))
        IDB = stack.enter_context(nc.sbuf_tensor("IDB", [128, 128], BF16))
        IDF = stack.enter_context(nc.sbuf_tensor("IDF", [128, 128], F32))
        CS = stack.enter_context(nc.sbuf_tensor("CS", [128, 96], F32))
        STT_ = stack.enter_context(nc.sbuf_tensor("STAT", [128, 64], F32))
        ZST = stack.enter_context(nc.sbuf_tensor("ZST", [128, KC, 2], F32))
        UST = stack.enter_context(nc.sbuf_tensor("UST", [128, KC, 15], F32))
        FAC = stack.enter_context(nc.sbuf_tensor("FAC", [128, 5, 16], F32))
        PS = stack.enter_context(nc.psum_tensor("PS", [128, 8, 512], F32))

        hT = _view(AR, A_H2, BF16, [KC, TMAX])
        ApT = _view(AR, A_AP, BF16, [KC, TMAX])
        plT = _view(AR, A_PL, BF16, [KC, TMAX])
        mgT = _view(AR, A_MG, BF16, [KC, TMAX])
        actT = _view(AR, A_ACT, BF16, [32, TMAX])
        x1 = _view(AR, A_X1, F32, [NT_MAX, D])
        h2T = _view(AR, A_H2, BF16, [KC, TMAX])
        US_ = _view(AR, A_X1, F32, [KC, 128])
        ZS_ = _view(AR, A_X1 + KC * 128 * 4, F32, [KC, 32])

        C_GPM = 0
        C_GPL = 8
        C_BG = 16
        C_WC = 32
        C_PS = 56
        C_EPS = 64

        def cs(col, n=1):
            return CS[:, col:col + n]

        stat_ctr = [0]

        def stat():
            i = stat_ctr[0] % 64
            stat_ctr[0] += 1
            return STT_[:, i:i + 1]

        bank_ctr = [0]
        bank_pool = [None]

        def bank():
            if bank_pool[0] is not None:
                pool_ = bank_pool[0]
                b = pool_[bank_ctr[0] % len(pool_)]
                bank_ctr[0] += 1
                return b
            b = bank_ctr[0] % 8
            bank_ctr[0] += 1
            return b

        def bank2():
            if bank_ctr[0] % 2:
                bank_ctr[0] += 1
            b = bank_ctr[0] % 8
            bank_ctr[0] += 2
            return b

        slot_ctr = [0]

        def wslot():
            s = slot_ctr[0] % NSLOT
            slot_ctr[0] += 1
            return s

        def s5_order(gi_):
            od = [(nh, kg) for nh in range(2) for kg in range(4)]
            return od if gi_ % 2 == 0 else od[::-1]

        plan = []
        for ps_i in range(npass):
            nt_ = NPP // 128 + (1 if (with_sample and ps_i == npass - 1) else 0)
            for b in range(2):
                plan.append([(w_in, 0, 0 * D + b * 512), (w_in, 0, 1 * D + b * 512), (w_in, 0, 2 * D + b * 512)])
            for b in range(2):
                plan.append([(w_in, 0, 3 * D + b * 512), (w_in, 0, 4 * D + b * 512), (w_in, 0, 5 * D + b * 512),
                             (w_out_conv, 0, b * 512)])
            plan.append([(w_o, 0, 0), (w_o, 0, 512)])
            plan.append([(w_up, 0, 0), (w_up, 0, 512)])
            for blk in range(2, DFF // 512):
                plan.append([(w_up, 0, blk * 512)])
            for gi_, a in enumerate(range(0, nt_, 3)):
                for (nh, kg) in s5_order(gi_):
                    plan.append([(w_down, kg * 1024, nh * 512)])
        entries = [e for g in plan for e in g]
        gfirst = []
        gof = []
        acc_ = 0
        for gi_, g in enumerate(plan):
            gfirst.append(acc_)
            acc_ += len(g)
            gof += [gi_] * len(g)
        keyof = [(id(W_), r_, c_) for (W_, r_, c_) in entries]
        slot_of, need_load, free_after = [], [], []
        content = [None] * NSLOT
        last_use = [-1] * NSLOT
        for e, k_ in enumerate(keyof):
            if k_ in content and e >= 3 and k_[0] == id(w_down):
                sl_ = content.index(k_)
                slot_of.append(sl_)
                need_load.append(False)
                free_after.append(-1)
            else:
                same_grp = [slot_of[e2] for e2 in range(gfirst[gof[e]], e)]
                best, best_key = None, None
                for sl_ in range(NSLOT):
                    if sl_ in same_grp:
                        continue
                    if content[sl_] is None or content[sl_][0] != id(w_down):
                        nu = 10 ** 9
                    else:
                        nu = 10 ** 9
                        for e2 in range(e + 1, min(len(keyof), e + 24)):
                            if keyof[e2] == content[sl_]:
                                nu = e2
                                break
                    cand = (nu, -last_use[sl_])
                    if best_key is None or cand > best_key:
                        best, best_key = sl_, cand
                slot_of.append(best)
                need_load.append(True)
                free_after.append(gof[last_use[best]] if last_use[best] >= 0 else -1)
                content[best] = k_
            last_use[slot_of[e]] = e
        wstate = {"g": 0, "issued": 0, "conv": 0}
        conv_ops = []

        class WV:
            def __init__(self, e, split):
                self.split = split
                if split:
                    self.v = _view(WS, slot_of[e] * 8192, BF16, [4, KC, 128])
                else:
                    self.v = _view(WS, slot_of[e] * 8192, BF16, [KC, 512])

            def lhsT(self, k, jj):
                return self.v[:, jj, k, :] if self.split else self.v[:, k, jj * 128:(jj + 1) * 128]

            def full(self, k):
                assert not self.split
                return self.v[:, k, :]

        def wslot_view(e):
            return WV(e, e < 3)

        def wgroup(expect=None, topup=False):
            if topup:
                wstate["g"] -= 1
            g = wstate["g"]
            wstate["g"] += 1
            first, cnt = gfirst[g], len(plan[g])
            if expect is not None:
                assert [(a is b_, r, c) for (a, r, c), (b_, r2, c2) in zip(plan[g], expect)] and \
                    all(a is b_ and r == r2 and c == c2 for (a, r, c), (b_, r2, c2) in zip(plan[g], expect)), g
            if 1 <= g <= 4 and not topup:
                for c in (2 * (g - 1), 2 * (g - 1) + 1):
                    kg_, nh_ = c // 2, c % 2
                    conv_ops.append(P.dma("pool", wd_bf[c, :, :].rearrange("p (k c) -> p k c", c=512),
                                          w_down[kg_ * 1024:(kg_ + 1) * 1024, nh_ * 512:(nh_ + 1) * 512].rearrange(
                                              "(k p) c -> p k c", p=128),
                                          "wdc", name="wdconv"))
                    if c == 7:
                        for op_ in conv_ops:
                            op_.dma = ("wdc", P.dma_cnt["wdc"])
            while wstate["issued"] < len(entries):
                e = wstate["issued"]
                own = e < first + cnt
                ahead_ok = (free_after[e] < g) and not (g == 0 and not topup)
                if not (own or ahead_ok):
                    break
                if own:
                    assert free_after[e] < g
                W, r0, c0 = entries[e]
                if not need_load[e]:
                    pass
                elif e < 3:
                    if e == 0:
                        for jj_ in range(4):
                            for e_ in range(3):
                                W_, r_0, c_0 = entries[e_]
                                src = W_[r_0:r_0 + KC * 128, c_0 + jj_ * 128:c_0 + (jj_ + 1) * 128].rearrange(
                                    "(k p) c -> p k c", p=128)
                                P.dma("pool", wslot_view(e_).v[:, jj_, :, :], src, "wi%d_%d" % (e_, jj_), name="wload0",
                                      deps=([xload0[min(3, len(xload0) - 1)]] if (jj_ > 0 and xload0) else []))
                elif W is w_down:
                    src = wd_bf[(r0 // 1024) * 2 + (c0 // 512), :, :].rearrange("p (k c) -> p k c", c=512)
                    P.dma("pool", wslot_view(e).v, src, "w%d" % slot_of[e], name="wload_bf", deps=conv_ops)
                else:
                    src = W[r0:r0 + KC * 128, c0:c0 + 512].rearrange("(k p) c -> p k c", p=128)
                    P.dma("pool", wslot_view(e).v, src, "w%d" % slot_of[e], name="wload")
                wstate["issued"] += 1
            return [wslot_view(first + i) for i in range(cnt)]

        CST = _view(TM, 18688, F32, [128])
        r_ = lambda v: v.rearrange("(k p) -> k p", p=128)
        P.dma("act", CST[0:8, :], r_(g_pre_mix), "const")
        P.dma("act", CST[8:16, :], r_(g_pre_mlp), "const")
        P.dma("act", CST[16:32, :], r_(b_gate), "const")
        P.dma("act", CST[32:56, :], w_conv.rearrange("w (k p) -> (w k) p", p=128), "const")
        P.dma("act", CST[56:64, :], r_(pool_scale), "const")
        tot = P.dma_cnt["const"]
        for op in P.streams["act"]:
            if op.dma is not None and op.dma[0] == "const":
                op.dma = ("const", tot)

        P.memset("pool", IDF[:, :], 0.0)
        P.add("pool", lambda e: e.affine_select(out=IDF[:, :], in_=IDF[:, :], compare_op=ALU.not_equal,
                                                 fill=1.0, base=0, pattern=[[-1, 128]], channel_multiplier=1),
              reads=[IDF[:, :]], writes=[IDF[:, :]], name="ident")
        P.copy("dve", IDB[:, :], IDF[:, :])
        P.act(STT_[:, 63:64], IDF[:, 0:1], AF.Square, name="act_table_warm")
        def emit_consts_finalize():
            bt0 = bank()
            P.tr(PS[:, bt0, 0:64], CST[0:64, :], IDF[0:64, 0:64])
            P.copy("act", CS[:, 0:64], PS[:, bt0, 0:64])
            P.add("pool", lambda e: e.iota(FAC[:, 4, :], pattern=[[1, 16]], base=1, channel_multiplier=0,
                                           allow_small_or_imprecise_dtypes=True),
                  writes=[FAC[:, 4, :]], name="iota")
            for g, w in enumerate(POOL_W):
                P.ts("dve", FAC[:, g, :], FAC[:, 4, :], float(w), None, ALU.min)
                P.add("dve", lambda e, g=g: e.reciprocal(FAC[:, g, :], FAC[:, g, :]),
                      reads=[FAC[:, g, :]], writes=[FAC[:, g, :]], name="recip")
                P.ts("dve", FAC[:, g, :], FAC[:, g, :], float(w), None, ALU.mult)
                P.ts("dve", cs(C_PS + 2 * g, 2), cs(C_PS + 2 * g, 2), 1.0 / w, None, ALU.mult)

        P.memset("dve", cs(C_EPS), EPS)
        P.memset("dve", ZST[:, :, :], 0.0)
        P.memset("dve", UST[:, :, :], 0.0)

        def tmv(off, dt, shape):
            return _view(TM, off, dt, shape)

        XT = [tmv(0 + i * 4096, F32, [D]) for i in range(2)]
        XN = [tmv(8192 + i * 2048, BF16, [D]) for i in range(2)]
        JUNK = tmv(12288, BF16, [D])
        XV = [tmv(0 + i * 2048, F32, [512]) for i in range(2)]
        ZC = [tmv(4096 + i * 2064, F32, [514]) for i in range(2)]
        TCV = [tmv(8224 + i * 2048, F32, [512]) for i in range(2)]
        ZSX = [tmv(12320 + i * 640, F32, [16, 10]) for i in range(2)]
        UC = [tmv(i * 2112, F32, [527]) for i in range(3)]
        PB = tmv(6336, F32, [527])
        QB = tmv(8448, F32, [527])
        PBP = tmv(10560, F32, [527])
        QBP = tmv(12672, F32, [527])
        TMPC = tmv(14784, F32, [16])
        USX = [_view(AR, A_X1 + 2 * 4096 + i * 1472, F32, [16, 23]) for i in range(2)]
        PSX = _view(AR, A_X1 + 2 * 4096 + 2944, F32, [16, 23])
        QSX = _view(AR, A_X1 + 2 * 4096 + 4416, F32, [16, 23])
        SPT = [_view(AR, A_X1 + (NT_MAX - 1) * 4096, F32, [KC, 120]), tmv(14848, F32, [KC, 120])]
        SCT = tmv(18688, F32, [KC, 32])
        XNF = tmv(8192, F32, [D])
        SG1 = [tmv(0 + i * 2048, F32, [512]) for i in range(2)]
        SG2 = [tmv(4096 + i * 2048, F32, [512]) for i in range(2)]
        MT1 = [tmv(8192 + i * 2048, F32, [512]) for i in range(2)]
        MT2 = [tmv(12288 + i * 2048, F32, [512]) for i in range(2)]
        RL = [tmv(i * 2048, F32, [512]) for i in range(3)]
        if NT_MAX >= 9:
            STGS = [_view(AR, A_X1 + (4 + i) * 4096, F32, [D]) for i in range(4)]
        else:
            STGS = [tmv(i * 4096, F32, [D]) for i in range(4)]

        ctr = {"xt": 0, "xn": 0, "t4k": 0, "cv": 0, "pl": 0, "sg": 0, "rl": 0, "sx": 0}

        def rot(key, n):
            i = ctr[key] % n
            ctr[key] += 1
            return i

        def rstd_from(ss):
            P.act(ss, ss, AF.Sqrt, bias=cs(C_EPS), scale=1.0 / D, name="sqrt")
            P.add("dve", lambda e: e.reciprocal(ss, ss), reads=[ss], writes=[ss], name="rstd")
            return ss

        def make_pass(ps_i):
            last_pass = ps_i == npass - 1
            has_s = with_sample and last_pass
            tiles = []
            for i in range(NPP // 128):
                r0 = ps_i * NPP + i * 128
                tiles.append((x_p[r0:r0 + 128, :], y_p[r0:r0 + 128, :]))
            if has_s:
                tiles.append((x_s[:, :], y_s[:, :]))
            tgs = [(g * 512, 512, "p") for g in range(ntg)]
            if has_s:
                tgs.append((NPP, 128, "s"))
            return dict(idx=ps_i, last_pass=last_pass, has_s=has_s, tiles=tiles, tgs=tgs, NT=len(tiles))

        passes = [make_pass(i) for i in range(npass)]

        def norm_prep(src_tok, xn_eng="act"):
            ss = stat()
            P.act(JUNK, src_tok, AF.Square, accum_out=ss, name="sq")
            rstd_from(ss)
            xn = XN[rot("xn", 2)]
            if xn_eng == "act":
                P.act(xn, src_tok, AF.Identity, scale=ss, name="xn")
            else:
                P.ts("dve", xn, src_tok, ss, None, ALU.mult, name="xn")
            return xn

        def norm_tr(xn, dstT, col0, gcol):
            b = bank()
            pb = PS[:, b, :].bitcast(BF16)
            for k in range(KC):
                P.tr(pb[:, k * 128:(k + 1) * 128], xn[:, k * 128:(k + 1) * 128], IDB[:, :], signal=(k == KC - 1))
            gb = bass.AP(CS, gcol, [[96, 128], [1, KC], [0, 128]])
            P.tt("dve", dstT[:, :, col0:col0 + 128], pb.rearrange("p (k t) -> p k t", t=128), gb, ALU.mult,
                 name="evacT")

        def s0_steps(pinfo, lo=0, hi=None, with_state=True):
            st = {}
            steps = []
            NT_ = pinfo["NT"]
            hi = NT_ if hi is None else hi

            def mk(u):
                def step():
                    if lo + 2 <= u:
                        norm_tr(st.pop(u - 2), hT, (u - 2) * 128, C_GPM)
                    if u < hi:
                        if pinfo["idx"] == 0:
                            if u == 0:
                                for i_ in range(NT_):
                                    xload0.append(P.dma("sp", x1[:, i_, :], pinfo["tiles"][i_][0], "x0_%d" % i_, name="xload0"))
                            xt = x1[:, u, :]
                        else:
                            xs_ = rot("xt", 2)
                            xt = XT[xs_]
                            P.dma("sp", xt, pinfo["tiles"][u][0], "xt%d" % xs_, name="xload")
                        st[u] = norm_prep(xt, "dve" if pinfo["idx"] == 0 else "act")
                return step
            for u in range(lo, hi + 2):
                steps.append(mk(u))
            if pinfo["has_s"] and with_state:
                def ld():
                    P.dma("sp", XT[0][0:120, :], st_pool[0:120, :], "xt0", name="stload")
                    P.dma("sp", XT[1][0:120, :], st_pool[120:240, :], "xt1", name="stload")
                    P.dma("sp", XNF[0:32, :], st_conv[:, :], "stc", name="stload")

                def trs(which):
                    def f():
                        ci = 0
                        for j in range(KC):
                            for (src, nrow, dst) in which(j):
                                bt = bank()
                                P.tr(PS[:, bt, 0:nrow], src, IDF[0:nrow, 0:nrow])
                                P.copy("dve" if ci % 2 == 0 else "act", dst, PS[:, bt, 0:nrow])
                                ci += 1
                    return f
                steps.append(ld)
                steps.append(lambda: None)
                steps.append(trs(lambda j: [(XNF[0:32, j * 128:(j + 1) * 128], 32, SCT[:, j, :]),
                                            (XT[0][0:120, j * 128:(j + 1) * 128], 120, SPT[0][:, j, :])]))
                steps.append(trs(lambda j: [(XT[1][0:120, j * 128:(j + 1) * 128], 120, SPT[1][:, j, :])]))
            return steps

        xload0 = []
        if passes[0]["NT"] >= 8:
            pending_s0 = s0_steps(passes[0], 0, 4)
            s0_rest_p0 = s0_steps(passes[0], 4, None)
        else:
            pending_s0 = s0_steps(passes[0])
            s0_rest_p0 = []

        for ps_i in range(npass):
            pinfo = passes[ps_i]
            last_pass, has_s, tiles, tgs, NT = pinfo["last_pass"], pinfo["has_s"], pinfo["tiles"], pinfo["tgs"], pinfo["NT"]
            T = NT * 128

            s0_rest = s0_rest_p0 if ps_i == 0 else []
            for si_, step in enumerate(pending_s0):
                step()
                if ps_i == 0 and si_ == 0:
                    emit_consts_finalize()
            pending_s0 = []
            if ps_i == 0:
                gd_ = [xload0[-1]] if xload0 else []
                P.dma("sp", GBC[:, 0, :], bass.AP(g_post_mix.tensor, 0, [[0, 128], [1, D]]), "gbc0", deps=gd_)
                P.dma("sp", GBC[:, 1, :], bass.AP(g_post_mlp.tensor, 0, [[0, 128], [1, D]]), "gbc1", deps=gd_)

            for b in range(2):
                wxv, wbg, wcg = wgroup([(w_in, 0, 0 * D + b * 512), (w_in, 0, 1 * D + b * 512), (w_in, 0, 2 * D + b * 512)])
                for jj in range(4):
                    j = b * 4 + jj
                    prev_zc = None
                    for (c0, n, kind) in tgs:
                        bxv, bcg, bbg = bank(), bank(), bank()
                        for (bk, wt) in ((bxv, wxv), (bcg, wcg), (bbg, wbg)):
                            for k in range(KC):
                                P.mm(PS[:, bk, 0:n], wt.lhsT(k, jj), hT[:, k, c0:c0 + n],
                                     start=(k == 0), stop=(k == KC - 1))
                        si = rot("cv", 2)
                        xv = XV[si]
                        tcv = TCV[si]
                        if kind == "p":
                            zc = ZC[si]
                            zsl = lambda a, bb, zc=zc: zc[:, a:bb]
                            pview = lambda ap: ap
                            if prev_zc is None:
                                P.copy("dve", zc[:, 0:2], ZST[:, j, :])
                            else:
                                P.copy("dve", zc[:, 0:2], prev_zc[:, 512:514])
                            nn = n
                            xv_v, tcv_v = xv[:, 0:n], tcv[:, 0:n]
                            pcg, pbg, pxv = PS[:, bcg, 0:n], PS[:, bbg, 0:n], PS[:, bxv, 0:n]
                        else:
                            zc = ZSX[rot("sx", 2)]
                            zsl = lambda a, bb, zc=zc: zc[:, :, a:bb]
                            nn = 8
                            r3 = lambda ap: ap.rearrange("p (s t) -> p s t", t=8)
                            xv_v, tcv_v = r3(xv[:, 0:n]), r3(tcv[:, 0:n])
                            pcg, pbg, pxv = r3(PS[:, bcg, 0:n]), r3(PS[:, bbg, 0:n]), r3(PS[:, bxv, 0:n])
                            P.copy("dve", zc[:, :, 0:2], SCT[:, j, :].rearrange("p (s t) -> p s t", t=2))
                        P.copy("act", xv_v, pxv, name="xv")
                        P.tt("dve", zsl(2, 2 + nn), pcg, xv_v, ALU.mult, name="z")
                        P.act(tcv_v, zsl(0, nn), AF.Identity, scale=cs(C_WC + 0 * 8 + j), name="cv0")
                        P.stt("dve", tcv_v, zsl(1, 1 + nn), cs(C_WC + 1 * 8 + j), tcv_v, ALU.mult, ALU.add, name="cv1")
                        P.stt("dve", tcv_v, zsl(2, 2 + nn), cs(C_WC + 2 * 8 + j), tcv_v, ALU.mult, ALU.add, name="cv2")
                        if kind == "p":
                            P.tt("dve", ApT[:, j, c0:c0 + n], pbg, tcv_v, ALU.mult, name="Ap")
                            prev_zc = zc
                            if c0 + n == NPP:
                                P.copy("dve", ZST[:, j, :], zc[:, 512:514])
                        else:
                            P.tt("dve", ApT[:, j, c0:c0 + n].rearrange("p (s t) -> p s t", t=8), pbg, tcv_v, ALU.mult,
                                 name="Ap")
                            P.copy("dve", ZS_[:, j, :].rearrange("p (s t) -> p s t", t=2), zc[:, :, 8:10])
                        if ps_i == 0 and j == 0 and c0 == 0:
                            for step in s0_rest:
                                step()
                            s0_rest = []
                            wgroup(topup=True)
                            P.dma("pool", WPG[:, :, :, :], w_pool_group.rearrange("g (c p) o -> p g c o", p=128), "wpg")

            deferred = []

            def flush_deferred():
                while deferred:
                    deferred.pop(0)[1]()

            def s1b_chunk(j, jj, wu):
                g = j // 2
                w = POOL_W[g]
                prev_uc = None
                for (c0, n, kind) in tgs:
                    bu = bank()
                    for k in range(KC):
                        P.mm(PS[:, bu, 0:n], wu.lhsT(k, jj), hT[:, k, c0:c0 + n],
                             start=(k == 0), stop=(k == KC - 1))
                    pe_ = "pool" if (w > 2 and ((kind == "p" and (c0 // 512) % 2 == 0) or (kind == "s" and j % 2 == 0))) else "dve"
                    if kind == "p":
                        si = rot("pl", 3)
                        uc, pb_, qb_ = UC[si], PB, QB
                        if pe_ == "pool":
                            pb_, qb_ = PBP, QBP
                        sl = lambda buf, a, bb: buf[:, a:bb]
                        nn = n
                        if prev_uc is None:
                            P.copy("dve", uc[:, 0:15], UST[:, j, :])
                        else:
                            P.copy("dve", uc[:, 0:15], prev_uc[:, 512:527])
                        pu = PS[:, bu, 0:n]
                        outv = plT[:, j, c0:c0 + n]
                    else:
                        uc, pb_, qb_ = USX[j % 2], PSX, QSX
                        sl = lambda buf, a, bb: buf[:, :, a:bb]
                        nn = 8
                        pu = PS[:, bu, 0:n].rearrange("p (s t) -> p s t", t=8)
                        outv = plT[:, j, c0:c0 + n].rearrange("p (s t) -> p s t", t=8)
                        for h in range(2):
                            P.copy("dve" if h == 0 else "act", uc[:, h * 8:(h + 1) * 8, 0:15],
                                   SPT[h][:, j, :].rearrange("p (s t) -> p s t", t=15))
                    E = 15 + nn
                    if pe_ == "pool":
                        for d_ in [d for d in deferred if d[0] == kind]:
                            deferred.remove(d_)
                            d_[1]()
                    P.copy("act", sl(uc, 15, E), pu, name="u")
                    if w == 2:
                        P.tt(pe_, outv, sl(uc, 14, E - 1), sl(uc, 15, E), ALU.subtract, name="pool2")
                        sw = None
                    else:
                        P.tt(pe_, sl(pb_, 1, E), sl(uc, 1, E), sl(uc, 0, E - 1), ALU.add, name="s2")
                        P.tt(pe_, sl(qb_, 3, E), sl(pb_, 3, E), sl(pb_, 1, E - 2), ALU.add, name="s4")
                        sw = qb_
                        if w >= 8:
                            P.tt(pe_, sl(pb_, 7, E), sl(qb_, 7, E), sl(qb_, 3, E - 4), ALU.add, name="s8")
                            sw = pb_
                        if w >= 16:
                            P.tt(pe_, sl(qb_, 15, E), sl(pb_, 15, E), sl(pb_, 7, E - 8), ALU.add, name="s16")
                            sw = qb_
                    def fin(w=w, outv=outv, uc=uc, sw=sw, sl=sl, E=E, kind=kind, c0=c0, j=j, g=g):
                        if w > 2:
                            P.stt("dve", outv, sl(uc, 15, E), -float(w), sl(sw, 15, E), ALU.mult, ALU.add, name="pooled")
                        if kind == "p" and ps_i == 0 and c0 == 0:
                            if w == 2:
                                P.memset("dve", plT[:, j, 0:1], 0.0)
                            else:
                                nfix = w - 1
                                P.tt("dve", TMPC[:, 0:nfix], sw[:, 15:15 + nfix], FAC[:, g, 0:nfix], ALU.mult)
                                P.stt("dve", plT[:, j, 0:nfix], uc[:, 15:15 + nfix], -float(w), TMPC[:, 0:nfix],
                                      ALU.mult, ALU.add)
                    if kind == "p":
                        prev_uc = uc
                        if c0 + n == NPP:
                            P.copy("act", UST[:, j, :], uc[:, 512:527])
                    else:
                        P.copy("act", US_[:, j, :].rearrange("p (s t) -> p s t", t=8), uc[:, :, 15:23])
                    if pe_ == "pool":
                        deferred.append((kind, fin))
                    else:
                        fin()
                    while len(deferred) > (1 if pe_ == "pool" else 0):
                        deferred.pop(0)[1]()

            def emit_state_outputs():
                if last_pass:
                    b2 = bank2()
                    for j in range(KC):
                        P.tr(PS[0:2, b2 + j // 4, (j % 4) * 128:(j % 4 + 1) * 128], ZST[:, j, :], IDF[:, :])
                    pz = bass.AP(PS, b2 * 512, [[4096, 2], [1, D]])
                    P.copy("dve", STGS[0][0:2, :], pz)
                    P.dma("sp", ncp[:, :], STGS[0][0:2, :], "so0", is_output=True)
                    b2 = bank2()
                    for j in range(KC):
                        P.tr(PS[0:15, b2 + j // 4, (j % 4) * 128:(j % 4 + 1) * 128], UST[:, j, :], IDF[:, :])
                    pz = bass.AP(PS, b2 * 512, [[4096, 15], [1, D]])
                    P.copy("act", STGS[1][0:15, :], pz)
                    P.dma("sp", npp[:, :], STGS[1][0:15, :], "so1", is_output=True)
                    if has_s:
                        b2 = bank2()
                        for j in range(KC):
                            P.tr(PS[0:32, b2 + j // 4, (j % 4) * 128:(j % 4 + 1) * 128], ZS_[:, j, :], IDF[:, :])
                        pz = bass.AP(PS, b2 * 512, [[4096, 32], [1, D]])
                        P.copy("dve", STGS[2][0:32, :], pz)
                        P.dma("sp", ncs[:, :], STGS[2][0:32, :], "so2", is_output=True)
                        b2 = bank2()
                        for j in range(KC):
                            P.tr(PS[:, b2 + j // 4, (j % 4) * 128:(j % 4 + 1) * 128], US_[:, j, :], IDF[:, :])
                        pz = bass.AP(PS, b2 * 512, [[4096, 128], [1, D]])
                        P.copy("act", STGS[3][:, :], pz)
                        for s in range(NSEQ):
                            P.dma("sp", nps[s * 15 + 7:s * 15 + 15, :], STGS[3][s * 8:(s + 1) * 8, :], "so3", is_output=True)
                        P.dma("sp", nps.rearrange("(s r) d -> s r d", r=15)[:, 0:7, :],
                              st_pool.rearrange("(s r) d -> s r d", r=15)[:, 8:15, :], "so4", is_output=True)


            if has_s:
                S2T = [_view(AR, A_X1 + 5120, F32, [512]), _view(AR, A_X1 + 14080, F32, [512])]
            else:
                S2T = [tmv(14848, F32, [512]), tmv(16896, F32, [512])]

            def s2_chunk(m, jj, wgc, wgp, woc):
                flush_deferred()
                g = m // 2
                for (c0, n, kind) in tgs:
                    bA, bB, bg1, bg2 = bank(), bank(), bank(), bank()
                    for k in range(KC):
                        P.mm(PS[:, bg1, 0:n], wgc.lhsT(k, jj), hT[:, k, c0:c0 + n],
                             start=(k == 0), stop=(k == KC - 1))
                    for k in range(KC):
                        P.mm(PS[:, bg2, 0:n], wgp.lhsT(k, jj), hT[:, k, c0:c0 + n],
                             start=(k == 0), stop=(k == KC - 1))
                    for k in range(KC):
                        P.mm(PS[:, bA, 0:n], woc.lhsT(k, jj), ApT[:, k, c0:c0 + n],
                             start=(k == 0), stop=(k == KC - 1))
                    for c in range(2):
                        P.mm(PS[:, bB, 0:n], WPG[:, g, c, (m % 2) * 128:(m % 2 + 1) * 128],
                             plT[:, 2 * g + c, c0:c0 + n], start=(c == 0), stop=(c == 1))
                    s1, s2 = S2T[0][:, 0:n], S2T[1][:, 0:n]
                    t1, t2 = s1, s2
                    P.act(s1, PS[:, bg1, 0:n], AF.Sigmoid, bias=cs(C_BG + m), name="sig1")
                    P.act(s2, PS[:, bg2, 0:n], AF.Sigmoid, bias=cs(C_BG + 8 + m), name="sig2")
                    P.tt("dve", t1, PS[:, bA, 0:n], s1, ALU.mult, name="mA")
                    P.stt("dve", t2, PS[:, bB, 0:n], cs(C_PS + m), s2, ALU.mult, ALU.mult, name="mB")
                    P.tt("dve", mgT[:, m, c0:c0 + n], t1, t2, ALU.add, name="merge")

            for b in range(2):
                wu, wgc, wgp, woc = wgroup([(w_in, 0, 3 * D + b * 512), (w_in, 0, 4 * D + b * 512),
                                            (w_in, 0, 5 * D + b * 512), (w_out_conv, 0, b * 512)])
                c_ = b * 4
                s1b_chunk(c_ + 0, 0, wu)
                s1b_chunk(c_ + 1, 1, wu)
                s2_chunk(c_ + 0, 0, wgc, wgp, woc)
                s1b_chunk(c_ + 2, 2, wu)
                s2_chunk(c_ + 1, 1, wgc, wgp, woc)
                s1b_chunk(c_ + 3, 3, wu)
                s2_chunk(c_ + 2, 2, wgc, wgp, woc)
                if b == 1:
                    emit_state_outputs()
                s2_chunk(c_ + 3, 3, wgc, wgp, woc)
            flush_deferred()

            wo = wgroup([(w_o, 0, 0), (w_o, 0, 512)])
            s3b, s3x = {}, {}

            bank_pool[0] = [6, 7]

            def s3_A(i):
                b2 = 2 * (i % 3)
                for nh in range(2):
                    for k in range(KC):
                        P.mm(PS[:, b2 + nh, :], mgT[:, k, i * 128:(i + 1) * 128], wo[nh].full(k),
                             start=(k == 0), stop=(k == KC - 1))
                s3b[i] = b2

            def s3_B1(i):
                pm = bass.AP(PS, s3b.pop(i) * 512, [[4096, 128], [1, D]])
                ss = stat()
                P.act(JUNK, pm, AF.Square, accum_out=ss, name="sq_mix")
                rstd_from(ss)
                xs_ = rot("xt", 2)
                xt = XT[xs_]
                if ps_i == 0:
                    P.stt("dve", xt, pm, ss, GBC[:, 0, :], ALU.mult, ALU.mult, name="mixn")
                else:
                    P.dma("sp", xt, tiles[i][0], "xt%d" % xs_, name="xreload")
                    P.stt("dve", x1[:, i, :], pm, ss, GBC[:, 0, :], ALU.mult, ALU.mult, name="mixn")
                P.tt("pool", x1[:, i, :], x1[:, i, :], xt, ALU.add, name="x1")

            def s3_B2(i):
                s3x[i] = norm_prep(x1[:, i, :])

            def s3_C(i):
                norm_tr(s3x.pop(i), h2T, i * 128, C_GPL)

            s4_early = {"w": None, "done": set()}
            b2_last_done = [False]

            def s4_unit(wu_, f, jj, c0, n):
                bu = bank()
                for k in range(KC):
                    P.mm(PS[:, bu, 0:n], wu_.lhsT(k, jj), h2T[:, k, c0:c0 + n],
                         start=(k == 0), stop=(k == KC - 1))
                rl = RL[rot("rl", 3)][:, 0:n]
                P.act(rl, PS[:, bu, 0:n], AF.Relu, name="relu")
                P.tt("dve", actT[:, f, c0:c0 + n], rl, rl, ALU.mult, name="sqr")

            for u in range(NT + 3):
                if u < NT:
                    s3_A(u)
                if 1 <= u <= NT:
                    s3_B1(u - 1)
                if 2 <= u <= NT + 1 and not (u - 2 == NT - 1 and b2_last_done[0]):
                    s3_B2(u - 2)
                u0_ = max(7, NT - 1)
                if u >= u0_:
                    if s4_early["w"] is None:
                        s4_early["w"] = wgroup([(w_up, 0, 0), (w_up, 0, 512)])
                    todo_ = [(bl_, jj_) for bl_ in range(2) for jj_ in range(4)
                             if (bl_ * 4 + jj_, tgs[0][0]) not in s4_early["done"]]
                    if u < NT + 2:
                        todo_ = todo_[:2]
                    pool_ = [0, 1, 2, 3, 4, 5]
                    if u < NT:
                        pr_ = 2 * ((NT - 1) % 3)
                        pool_ = [b_ for b_ in pool_ if b_ not in (pr_, pr_ + 1)]
                    bank_pool[0] = pool_
                    for (bl_, jj_) in todo_:
                        s4_unit(s4_early["w"][bl_], bl_ * 4 + jj_, jj_, tgs[0][0], tgs[0][1])
                        s4_early["done"].add((bl_ * 4 + jj_, tgs[0][0]))
                    bank_pool[0] = [6, 7]
                if u >= 3:
                    s3_C(u - 3)
                if u == NT and NT >= 4:
                    s3_B2(NT - 1)
                    b2_last_done[0] = True

            bank_pool[0] = None
            if s4_early["w"] is None:
                s4_early["w"] = wgroup([(w_up, 0, 0), (w_up, 0, 512)])
            for blk in range(DFF // 512):
                if blk < 2:
                    wu_ = s4_early["w"][blk]
                else:
                    wu_, = wgroup([(w_up, 0, blk * 512)])
                for jj in range(4):
                    f = blk * 4 + jj
                    for (c0, n, kind) in tgs:
                        if (f, c0) in s4_early["done"]:
                            continue
                        s4_unit(wu_, f, jj, c0, n)

            groups = [list(range(a, min(a + 3, NT))) for a in range(0, NT, 3)]
            if not last_pass:
                pending_s0 = s0_steps(passes[ps_i + 1])
            unit_i = 0
            bank_pool[0] = [6, 7]
            for G in groups:
                bk = {}
                for gi_, i in enumerate(G):
                    bk[i] = 2 * gi_
                od_ = s5_order(groups.index(G))
                for (nh, kg) in od_:
                    kgs_ = [kg2 for (nh2, kg2) in od_ if nh2 == nh]
                    first_k, last_k = (kg == kgs_[0]), (kg == kgs_[-1])
                    wd, = wgroup([(w_down, kg * 1024, nh * 512)])
                    if unit_i >= 1 and pending_s0:
                        pending_s0.pop(0)()
                    unit_i += 1
                    for i in G:
                        for kk in range(KC):
                            P.mm(PS[:, bk[i] + nh, :], actT[:, kg * 8 + kk, i * 128:(i + 1) * 128], wd.full(kk),
                                 start=(first_k and kk == 0), stop=(last_k and kk == KC - 1),
                                 signal=((last_k or i == G[-1]) and kk == KC - 1))
                for i in G:
                    pm = bass.AP(PS, bk[i] * 512, [[4096, 128], [1, D]])
                    ss = stat()
                    P.act(JUNK, pm, AF.Square, accum_out=ss, name="sq_mlp")
                    rstd_from(ss)
                    t4 = XT[rot("xt", 2)]
                    P.stt("dve", t4, pm, ss, GBC[:, 1, :], ALU.mult, ALU.mult, name="mlpn")
                    P.tt("dve", x1[:, i, :], x1[:, i, :], t4, ALU.add, name="y")
                    P.dma("sp", tiles[i][1], x1[:, i, :], "yo%d" % i, name="ystore", is_output=True)
            for step in pending_s0:
                step()
            pending_s0 = []
            bank_pool[0] = None

        nwait = P.emit(stack)
    return nc


_NC_CACHE = {}


def kernel(**inputs):
    f = lambda a: np.ascontiguousarray(np.asarray(a, dtype=np.float32))
    x_prompt = f(inputs["x_prompt"])
    x_sample = f(inputs["x_sample"])
    state_conv = f(inputs["state_conv"])
    state_pool = f(inputs["state_pool"])
    shared = {
        "g_pre_mix": f(inputs["g_pre_mix"]).reshape(D),
        "w_in": f(inputs["w_in"]).reshape(D, 6 * D),
        "b_gate": f(inputs["b_gate"]).reshape(2 * D),
        "w_conv": f(inputs["w_conv"]).reshape(3, D),
        "w_out_conv": f(inputs["w_out_conv"]).reshape(D, D),
        "w_pool_group": f(inputs["w_pool_group"]).reshape(4, 256, 256),
        "pool_scale": f(inputs["pool_scale"]).reshape(D),
        "w_o": f(inputs["w_o"]).reshape(D, D),
        "g_post_mix": f(inputs["g_post_mix"]).reshape(D),
        "g_pre_mlp": f(inputs["g_pre_mlp"]).reshape(D),
        "w_up": f(inputs["w_up"]).reshape(D, DFF),
        "w_down": f(inputs["w_down"]).reshape(DFF, D),
        "g_post_mlp": f(inputs["g_post_mlp"]).reshape(D),
    }
    in_maps = []
    for c in range(NCORES):
        m = dict(shared)
        m["x_p"] = x_prompt[c]
        m["x_s"] = x_sample[16 * c:16 * (c + 1)].reshape(128, D)
        m["st_conv"] = state_conv[0, 16 * c:16 * (c + 1)].reshape(32, D)
        m["st_pool"] = state_pool[0, 16 * c:16 * (c + 1)].reshape(240, D)
        in_maps.append(m)
    if "nc" not in _NC_CACHE:
        _NC_CACHE["nc"] = build_program()
    nc = _NC_CACHE["nc"]
    res = run_bass_kernel_spmd(nc, in_maps, core_ids=list(range(NCORES)))
    R = res.results
    y_prompt = np.stack([R[c]["y_p"] for c in range(NCORES)], axis=0)
    y_sample = np.concatenate([R[c]["y_s"].reshape(16, 8, D) for c in range(NCORES)], axis=0)
    ncp = np.stack([R[c]["ncp"] for c in range(NCORES)], axis=0)[None]
    npp = np.stack([R[c]["npp"] for c in range(NCORES)], axis=0)[None]
    ncs = np.concatenate([R[c]["ncs"].reshape(16, 2, D) for c in range(NCORES)], axis=0)[None]
    nps = np.concatenate([R[c]["nps"].reshape(16, 15, D) for c in range(NCORES)], axis=0)[None]
    return (y_prompt.astype(np.float32), y_sample.astype(np.float32), ncp.astype(np.float32),
            npp.astype(np.float32), ncs.astype(np.float32), nps.astype(np.float32))
```

```python
import numpy as np
from contextlib import ExitStack
import concourse.bass as bass
import concourse.mybir as mybir
from concourse.bass_utils import run_bass_kernel_spmd

F32 = mybir.dt.float32
BF16 = mybir.dt.bfloat16
ALU = mybir.AluOpType
AF = mybir.ActivationFunctionType
SZ = {F32: 4, BF16: 2}

D = 1024
KC = 8
DFF = 4096
EPS = 1e-6
NCORES = 8
POOL_W = (2, 4, 8, 16)


def _prod(xs):
    r = 1
    for x in xs:
        r *= int(x)
    return r


class _Op:
    __slots__ = ("eng", "fn", "deps", "signal", "pos", "dma", "name", "sig", "wait_val")

    def __init__(self, eng, fn, name):
        self.eng = eng
        self.fn = fn
        self.deps = set()
        self.signal = True
        self.pos = None
        self.dma = None
        self.name = name
        self.sig = None
        self.wait_val = None


class Prog:
    ENG = ("pe", "act", "dve", "pool", "sp")

    def __init__(self, nc):
        self.nc = nc
        self.engs = {"pe": nc.tensor, "act": nc.scalar, "dve": nc.vector,
                     "pool": nc.gpsimd, "sp": nc.sync}
        self.streams = {e: [] for e in self.ENG}
        self.recs = {}
        self.dma_cnt = {}
        self.out_dmas = []

    @staticmethod
    def _iv(ap):
        t = ap.tensor
        if type(t).__name__.startswith("DRam"):
            return None
        esz = SZ[ap.dtype]
        pbytes = _prod(t.shape[1:]) * SZ[t.dtype]
        off = int(ap.offset) * esz
        lo = off % pbytes
        dims = list(ap.ap)
        p0 = off // pbytes
        p1 = p0 + ((int(dims[0][1]) - 1) * abs(int(dims[0][0])) * esz) // pbytes + 1
        fd = sorted([(abs(int(st)) * esz, int(cnt)) for (st, cnt) in dims[1:] if int(cnt) > 1], key=lambda d: -d[0])

        def rec(base, ds):
            if not ds:
                return [(base, base + esz)]
            inner = esz + sum((c - 1) * s_ for s_, c in ds[1:])
            s0, c0 = ds[0]
            if s0 >= 2 * inner and c0 <= 32:
                out = []
                for i in range(c0):
                    out += rec(base + i * s0, ds[1:])
                return out
            return [(base, base + esz + sum((c - 1) * s_ for s_, c in ds))]
        ivs = rec(lo, fd)
        if len(ivs) > 64:
            ivs = [(lo, lo + esz + sum((c - 1) * s_ for s_, c in fd))]
        return t.name, ivs, p0, p1

    def _acc(self, op, ap, is_write):
        iv = self._iv(ap)
        if iv is None:
            return
        key, ivs, p0, p1 = iv
        for (lo, hi) in ivs:
            self._acc1(op, key, lo, hi, p0, p1, is_write)

    def _acc1(self, op, key, lo, hi, p0, p1, is_write):
        recs = self.recs.get(key, [])
        new = []
        for r in recs:
            rlo, rhi, rop, rw, rp0, rp1 = r
            if rhi <= lo or rlo >= hi or rp1 <= p0 or rp0 >= p1:
                new.append(r)
                continue
            if (is_write or rw) and rop is not op:
                op.deps.add(rop)
            cov = rlo >= lo and rhi <= hi and rp0 >= p0 and rp1 <= p1
            if is_write and cov:
                continue
            if (not is_write) and (not rw) and rop.dma is None and op.dma is None \
                    and rop.eng == op.eng and cov:
                continue
            new.append(r)
        new.append((lo, hi, op, is_write, p0, p1))
        self.recs[key] = new

    def add(self, eng, fn, reads=(), writes=(), signal=True, name="", dma=None):
        op = _Op(eng, fn, name)
        op.signal = signal
        op.dma = dma
        for ap in reads:
            if ap is not None and not isinstance(ap, (int, float)):
                self._acc(op, ap, False)
        for ap in writes:
            self._acc(op, ap, True)
        op.pos = len(self.streams[eng])
        self.streams[eng].append(op)
        return op

    def dma(self, q, out, in_, semkey, name="", is_output=False, deps=(), **kw):
        cnt = self.dma_cnt.get(semkey, 0) + 16
        self.dma_cnt[semkey] = cnt
        op = self.add(q, lambda e: e.dma_start(out=out, in_=in_, **kw), reads=[in_], writes=[out],
                      name=name, dma=(semkey, cnt))
        op.deps.update(deps)
        if is_output:
            self.out_dmas.append(op)
        return op

    def mm(self, out, lhsT, rhs, start, stop, signal=None):
        sig = stop if signal is None else signal
        return self.add("pe", lambda e: e.matmul(out, lhsT, rhs, start=start, stop=stop),
                        reads=[lhsT, rhs], writes=[out], signal=sig, name="mm")

    def tr(self, out, in_, ident, signal=True):
        return self.add("pe", lambda e: e.transpose(out, in_, ident), reads=[in_, ident], writes=[out],
                        signal=signal, name="tr")

    def act(self, out, in_, func, bias=None, scale=None, accum_out=None, name="act"):
        kw = {}
        if bias is not None:
            kw["bias"] = bias
        if scale is not None:
            kw["scale"] = scale
        if accum_out is not None:
            kw["accum_out"] = accum_out
        rd = [in_] + [x for x in (bias, scale) if x is not None and not isinstance(x, (int, float))]
        wr = [out] + ([accum_out] if accum_out is not None else [])
        return self.add("act", lambda e: e.activation(out, in_, func, **kw), reads=rd, writes=wr, name=name)

    def tt(self, eng, out, in0, in1, op, name="tt"):
        return self.add(eng, lambda e: e.tensor_tensor(out, in0, in1, op), reads=[in0, in1], writes=[out], name=name)

    def ts(self, eng, out, in0, s1, s2, op0, op1=None, name="ts"):
        rd = [in0] + [x for x in (s1, s2) if x is not None and not isinstance(x, (int, float))]
        if op1 is None:
            return self.add(eng, lambda e: e.tensor_scalar(out, in0, s1, s2, op0), reads=rd, writes=[out], name=name)
        return self.add(eng, lambda e: e.tensor_scalar(out, in0, s1, s2, op0, op1), reads=rd, writes=[out], name=name)

    def stt(self, eng, out, in0, scalar, in1, op0, op1, name="stt"):
        rd = [in0, in1] + ([scalar] if not isinstance(scalar, (int, float)) else [])
        return self.add(eng, lambda e: e.scalar_tensor_tensor(out, in0, scalar, in1, op0, op1),
                        reads=rd, writes=[out], name=name)

    def copy(self, eng, out, in_, name="copy"):
        if eng == "act":
            return self.act(out, in_, AF.Identity, name=name)
        return self.add(eng, lambda e: e.tensor_copy(out, in_), reads=[in_], writes=[out], name=name)

    def memset(self, eng, out, val, name="memset"):
        return self.add(eng, lambda e: e.memset(out, val), writes=[out], name=name)

    def emit(self, stack):
        nc = self.nc
        sems = {}
        for e in ("pe", "act", "dve", "pool"):
            sems[e] = stack.enter_context(nc.semaphore("s_" + e))
        for k in self.dma_cnt:
            sems[("dma", k)] = stack.enter_context(nc.semaphore("d_" + str(k)))
        fin = _Op("sp", None, "final")
        fin.deps = set(self.out_dmas)
        self.streams["sp"].append(fin)
        for e in self.ENG:
            c = 0
            for op in self.streams[e]:
                if op.dma is None and op.fn is not None and op.signal:
                    c += 1
                    op.sig = c
            nxt = None
            for op in reversed(self.streams[e]):
                if op.dma is None and op.fn is not None:
                    if op.signal:
                        nxt = op.sig
                    op.wait_val = nxt
        nwait = 0
        for e in self.ENG:
            eng = self.engs[e]
            waited = {}
            for op in self.streams[e]:
                need = {}
                for d in op.deps:
                    if d.dma is not None:
                        key = ("dma", d.dma[0])
                        val = d.dma[1]
                    else:
                        if d.eng == "pe" and e == "pe":
                            continue
                        key = d.eng
                        val = d.wait_val
                        assert val is not None, (d.name, op.name)
                    if waited.get(key, 0) >= val:
                        continue
                    if need.get(key, 0) < val:
                        need[key] = val
                for key, val in need.items():
                    waited[key] = val
                    eng.wait_ge(sems[key], val)
                    nwait += 1
                if op.fn is not None:
                    ins = op.fn(eng)
                    if op.dma is not None:
                        ins.then_inc(sems[("dma", op.dma[0])], 16)
                    elif op.signal:
                        ins.then_inc(sems[e], 1)
        return nwait


def _view(t, boff, dt, shape):
    n = _prod(shape)
    esz = SZ[dt]
    assert boff % 4 == 0
    e0 = boff // 2
    e1 = e0 + n * esz // 2
    ap = t[:, e0:e1]
    if dt != BF16:
        ap = ap.bitcast(dt)
    if len(shape) == 2:
        ap = ap.rearrange("p (a b) -> p a b", b=shape[1])
    elif len(shape) == 3:
        ap = ap.rearrange("p (a b c) -> p a b c", b=shape[1], c=shape[2])
    return ap


def build_program(ntg=2, with_sample=True, npass=2):
    nc = bass.Bass("TRN2", target_bir_lowering=False)
    P = Prog(nc)
    NPP = 512 * ntg
    NPT = NPP * npass
    NSEQ = 16
    NST = NSEQ * 8
    TMAX = NPP + (NST if with_sample else 0)

    def din(name, shape):
        return nc.dram_tensor(name, list(shape), F32, kind="ExternalInput").ap()

    def dout(name, shape):
        return nc.dram_tensor(name, list(shape), F32, kind="ExternalOutput").ap()

    x_p = din("x_p", [NPT, D])
    x_s = din("x_s", [NST, D])
    st_conv = din("st_conv", [NSEQ * 2, D])
    st_pool = din("st_pool", [NSEQ * 15, D])
    g_pre_mix = din("g_pre_mix", [D])
    w_in = din("w_in", [D, 6 * D])
    b_gate = din("b_gate", [2 * D])
    w_conv = din("w_conv", [3, D])
    w_out_conv = din("w_out_conv", [D, D])
    w_pool_group = din("w_pool_group", [4, 256, 256])
    pool_scale = din("pool_scale", [D])
    w_o = din("w_o", [D, D])
    g_post_mix = din("g_post_mix", [D])
    g_pre_mlp = din("g_pre_mlp", [D])
    w_up = din("w_up", [D, DFF])
    w_down = din("w_down", [DFF, D])
    g_post_mlp = din("g_post_mlp", [D])

    y_p = dout("y_p", [NPT, D])
    y_s = dout("y_s", [NST, D])
    ncp = dout("ncp", [2, D])
    npp = dout("npp", [15, D])
    ncs = dout("ncs", [NSEQ * 2, D])
    nps = dout("nps", [NSEQ * 15, D])

    wd_bf = nc.dram_tensor("wd_bf", [8, 128, KC * 512], BF16, kind="Internal").ap()

    stack = ExitStack()
    with stack:
        NT_MAX = TMAX // 128
        A_ACT = 0
        A_HT = 0
        A_AP = KC * TMAX * 2
        A_PL = 2 * KC * TMAX * 2
        A_MG = 3 * KC * TMAX * 2
        A_X1 = 32 * TMAX * 2
        A_H2 = A_X1 + NT_MAX * 4096
        A_END = A_H2 + KC * TMAX * 2
        AR = stack.enter_context(nc.sbuf_tensor("AR", [128, A_END // 2], BF16))
        NSLOT = 6
        WS = stack.enter_context(nc.sbuf_tensor("WS", [128, NSLOT * 4096], BF16))
        WPG = stack.enter_context(nc.sbuf_tensor("WPG", [128, 4, 2, 256], BF16))
        TM_BYTES = 20096
        TM = stack.enter_context(nc.sbuf_tensor("TM", [128, TM_BYTES // 2], BF16))
        GBC = stack.enter_context(nc.sbuf_tensor("GBC", [128, 2, D], F32))
        IDB = stack.enter_context(nc.sbuf_tensor("IDB", [128, 128], BF16))
        IDF = stack.enter_context(nc.sbuf_tensor("IDF", [128, 128], F32))
        CS = stack.enter_context(nc.sbuf_tensor("CS", [128, 96], F32))
        STT_ = stack.enter_context(nc.sbuf_tensor("STAT", [128, 64], F32))
        ZST = stack.enter_context(nc.sbuf_tensor("ZST", [128, KC, 2], F32))
        UST = stack.enter_context(nc.sbuf_tensor("UST", [128, KC, 15], F32))
        FAC = stack.enter_context(nc.sbuf_tensor("FAC", [128, 5, 16], F32))
        PS = stack.enter_context(nc.psum_tensor("PS", [128, 8, 512], F32))

        hT = _view(AR, A_H2, BF16, [KC, TMAX])
        ApT = _view(AR, A_AP, BF16, [KC, TMAX])
        plT = _view(AR, A_PL, BF16, [KC, TMAX])
        mgT = _view(AR, A_MG, BF16, [KC, TMAX])
        actT = _view(AR, A_ACT, BF16, [32, TMAX])
        x1 = _view(AR, A_X1, F32, [NT_MAX, D])
        h2T = _view(AR, A_H2, BF16, [KC, TMAX])
        US_ = _view(AR, A_X1, F32, [KC, 128])
        ZS_ = _view(AR, A_X1 + KC * 128 * 4, F32, [KC, 32])

        C_GPM = 0
        C_GPL = 8
        C_BG = 16
        C_WC = 32
        C_PS = 56
        C_EPS = 64

        def cs(col, n=1):
            return CS[:, col:col + n]

        stat_ctr = [0]

        def stat():
            i = stat_ctr[0] % 64
            stat_ctr[0] += 1
            return STT_[:, i:i + 1]

        bank_ctr = [0]
        bank_pool = [None]

        def bank():
            if bank_pool[0] is not None:
                pool_ = bank_pool[0]
                b = pool_[bank_ctr[0] % len(pool_)]
                bank_ctr[0] += 1
                return b
            b = bank_ctr[0] % 8
            bank_ctr[0] += 1
            return b

        def bank2():
            if bank_ctr[0] % 2:
                bank_ctr[0] += 1
            b = bank_ctr[0] % 8
            bank_ctr[0] += 2
            return b

        slot_ctr = [0]

        def wslot():
            s = slot_ctr[0] % NSLOT
            slot_ctr[0] += 1
            return s

        def s5_order(gi_):
            od = [(nh, kg) for nh in range(2) for kg in range(4)]
            return od if gi_ % 2 == 0 else od[::-1]

        plan = []
        for ps_i in range(npass):
            nt_ = NPP // 128 + (1 if (with_sample and ps_i == npass - 1) else 0)
            for b in range(2):
                plan.append([(w_in, 0, 0 * D + b * 512), (w_in, 0, 1 * D + b * 512), (w_in, 0, 2 * D + b * 512)])
            for b in range(2):
                plan.append([(w_in, 0, 3 * D + b * 512), (w_in, 0, 4 * D + b * 512), (w_in, 0, 5 * D + b * 512),
                             (w_out_conv, 0, b * 512)])
            plan.append([(w_o, 0, 0), (w_o, 0, 512)])
            for blk in range(DFF // 512):
                plan.append([(w_up, 0, blk * 512)])
            for gi_, a in enumerate(range(0, nt_, 3)):
                for (nh, kg) in s5_order(gi_):
                    plan.append([(w_down, kg * 1024, nh * 512)])
        entries = [e for g in plan for e in g]
        gfirst = []
        gof = []
        acc_ = 0
        for gi_, g in enumerate(plan):
            gfirst.append(acc_)
            acc_ += len(g)
            gof += [gi_] * len(g)
        keyof = [(id(W_), r_, c_) for (W_, r_, c_) in entries]
        slot_of, need_load, free_after = [], [], []
        content = [None] * NSLOT
        last_use = [-1] * NSLOT
        for e, k_ in enumerate(keyof):
            if k_ in content and e >= 3 and k_[0] == id(w_down):
                sl_ = content.index(k_)
                slot_of.append(sl_)
                need_load.append(False)
                free_after.append(-1)
            else:
                same_grp = [slot_of[e2] for e2 in range(gfirst[gof[e]], e)]
                best, best_key = None, None
                for sl_ in range(NSLOT):
                    if sl_ in same_grp:
                        continue
                    if content[sl_] is None or content[sl_][0] != id(w_down):
                        nu = 10 ** 9
                    else:
                        nu = 10 ** 9
                        for e2 in range(e + 1, min(len(keyof), e + 24)):
                            if keyof[e2] == content[sl_]:
                                nu = e2
                                break
                    cand = (nu, -last_use[sl_])
                    if best_key is None or cand > best_key:
                        best, best_key = sl_, cand
                slot_of.append(best)
                need_load.append(True)
                free_after.append(gof[last_use[best]] if last_use[best] >= 0 else -1)
                content[best] = k_
            last_use[slot_of[e]] = e
        wstate = {"g": 0, "issued": 0, "conv": 0}
        conv_ops = []

        class WV:
            def __init__(self, e, split):
                self.split = split
                if split:
                    self.v = _view(WS, slot_of[e] * 8192, BF16, [4, KC, 128])
                else:
                    self.v = _view(WS, slot_of[e] * 8192, BF16, [KC, 512])

            def lhsT(self, k, jj):
                return self.v[:, jj, k, :] if self.split else self.v[:, k, jj * 128:(jj + 1) * 128]

            def full(self, k):
                assert not self.split
                return self.v[:, k, :]

        def wslot_view(e):
            return WV(e, e < 3)

        def wgroup(expect=None, topup=False):
            if topup:
                wstate["g"] -= 1
            g = wstate["g"]
            wstate["g"] += 1
            first, cnt = gfirst[g], len(plan[g])
            if expect is not None:
                assert [(a is b_, r, c) for (a, r, c), (b_, r2, c2) in zip(plan[g], expect)] and \
                    all(a is b_ and r == r2 and c == c2 for (a, r, c), (b_, r2, c2) in zip(plan[g], expect)), g
            if 1 <= g <= 4 and not topup:
                for c in (2 * (g - 1), 2 * (g - 1) + 1):
                    kg_, nh_ = c // 2, c % 2
                    conv_ops.append(P.dma("pool", wd_bf[c, :, :].rearrange("p (k c) -> p k c", c=512),
                                          w_down[kg_ * 1024:(kg_ + 1) * 1024, nh_ * 512:(nh_ + 1) * 512].rearrange(
                                              "(k p) c -> p k c", p=128),
                                          "wdc", name="wdconv"))
                    if c == 7:
                        for op_ in conv_ops:
                            op_.dma = ("wdc", P.dma_cnt["wdc"])
            while wstate["issued"] < len(entries):
                e = wstate["issued"]
                own = e < first + cnt
                ahead_ok = (free_after[e] < g) and not (g == 0 and not topup)
                if not (own or ahead_ok):
                    break
                if own:
                    assert free_after[e] < g
                W, r0, c0 = entries[e]
                if not need_load[e]:
                    pass
                elif e < 3:
                    if e == 0:
                        for jj_ in range(4):
                            for e_ in range(3):
                                W_, r_0, c_0 = entries[e_]
                                src = W_[r_0:r_0 + KC * 128, c_0 + jj_ * 128:c_0 + (jj_ + 1) * 128].rearrange(
                                    "(k p) c -> p k c", p=128)
                                P.dma("pool", wslot_view(e_).v[:, jj_, :, :], src, "wi%d_%d" % (e_, jj_), name="wload0",
                                      deps=([xload0[min(3, len(xload0) - 1)]] if (jj_ > 0 and xload0) else []))
                elif W is w_down:
                    src = wd_bf[(r0 // 1024) * 2 + (c0 // 512), :, :].rearrange("p (k c) -> p k c", c=512)
                    P.dma("pool", wslot_view(e).v, src, "w%d" % slot_of[e], name="wload_bf", deps=conv_ops)
                else:
                    src = W[r0:r0 + KC * 128, c0:c0 + 512].rearrange("(k p) c -> p k c", p=128)
                    P.dma("pool", wslot_view(e).v, src, "w%d" % slot_of[e], name="wload")
                wstate["issued"] += 1
            return [wslot_view(first + i) for i in range(cnt)]

        CST = _view(TM, 8192, F32, [128])
        r_ = lambda v: v.rearrange("(k p) -> k p", p=128)
        P.dma("act", CST[0:8, :], r_(g_pre_mix), "const")
        P.dma("act", CST[8:16, :], r_(g_pre_mlp), "const")
        P.dma("act", CST[16:32, :], r_(b_gate), "const")
        P.dma("act", CST[32:56, :], w_conv.rearrange("w (k p) -> (w k) p", p=128), "const")
        P.dma("act", CST[56:64, :], r_(pool_scale), "const")
        tot = P.dma_cnt["const"]
        for op in P.streams["act"]:
            if op.dma is not None and op.dma[0] == "const":
                op.dma = ("const", tot)

        P.memset("pool", IDF[:, :], 0.0)
        P.add("pool", lambda e: e.affine_select(out=IDF[:, :], in_=IDF[:, :], compare_op=ALU.not_equal,
                                                 fill=1.0, base=0, pattern=[[-1, 128]], channel_multiplier=1),
              reads=[IDF[:, :]], writes=[IDF[:, :]], name="ident")
        P.copy("dve", IDB[:, :], IDF[:, :])
        P.act(STT_[:, 63:64], IDF[:, 0:1], AF.Square, name="act_table_warm")
        bt0 = 0
        P.tr(PS[:, bt0, 0:64], CST[0:64, :], IDF[0:64, 0:64])
        P.copy("dve", CS[:, 0:64], PS[:, bt0, 0:64])
        P.add("pool", lambda e: e.iota(FAC[:, 4, :], pattern=[[1, 16]], base=1, channel_multiplier=0,
                                       allow_small_or_imprecise_dtypes=True),
              writes=[FAC[:, 4, :]], name="iota")
        for g, w in enumerate(POOL_W):
            P.ts("dve", FAC[:, g, :], FAC[:, 4, :], float(w), None, ALU.min)
            P.add("dve", lambda e, g=g: e.reciprocal(FAC[:, g, :], FAC[:, g, :]),
                  reads=[FAC[:, g, :]], writes=[FAC[:, g, :]], name="recip")
            P.ts("dve", FAC[:, g, :], FAC[:, g, :], float(w), None, ALU.mult)
            P.ts("dve", cs(C_PS + 2 * g, 2), cs(C_PS + 2 * g, 2), 1.0 / w, None, ALU.mult)
        P.memset("dve", cs(C_EPS), EPS)
        P.memset("dve", ZST[:, :, :], 0.0)
        P.memset("dve", UST[:, :, :], 0.0)

        def tmv(off, dt, shape):
            return _view(TM, off, dt, shape)

        XT = [tmv(0 + i * 4096, F32, [D]) for i in range(2)]
        XN = [tmv(8192 + i * 2048, BF16, [D]) for i in range(2)]
        JUNK = tmv(12288, BF16, [D])
        XV = [tmv(0 + i * 2048, F32, [512]) for i in range(2)]
        ZC = [tmv(4096 + i * 2064, F32, [514]) for i in range(2)]
        TCV = [tmv(8224 + i * 2048, F32, [512]) for i in range(2)]
        ZSX = [tmv(12320 + i * 640, F32, [16, 10]) for i in range(2)]
        UC = [tmv(i * 2112, F32, [527]) for i in range(3)]
        PB = tmv(6336, F32, [527])
        QB = tmv(8448, F32, [527])
        PBP = tmv(10560, F32, [527])
        QBP = tmv(12672, F32, [527])
        TMPC = tmv(14784, F32, [16])
        USX = [_view(AR, A_X1 + 2 * 4096 + i * 1472, F32, [16, 23]) for i in range(2)]
        PSX = _view(AR, A_X1 + 2 * 4096 + 2944, F32, [16, 23])
        QSX = _view(AR, A_X1 + 2 * 4096 + 4416, F32, [16, 23])
        SPT = [_view(AR, A_X1 + (NT_MAX - 1) * 4096, F32, [KC, 120]), tmv(14848, F32, [KC, 120])]
        SCT = tmv(18688, F32, [KC, 32])
        XNF = tmv(8192, F32, [D])
        SG1 = [tmv(0 + i * 2048, F32, [512]) for i in range(2)]
        SG2 = [tmv(4096 + i * 2048, F32, [512]) for i in range(2)]
        MT1 = [tmv(8192 + i * 2048, F32, [512]) for i in range(2)]
        MT2 = [tmv(12288 + i * 2048, F32, [512]) for i in range(2)]
        RL = [tmv(i * 2048, F32, [512]) for i in range(3)]
        if NT_MAX >= 9:
            STGS = [_view(AR, A_X1 + (4 + i) * 4096, F32, [D]) for i in range(4)]
        else:
            STGS = [tmv(i * 4096, F32, [D]) for i in range(4)]

        ctr = {"xt": 0, "xn": 0, "t4k": 0, "cv": 0, "pl": 0, "sg": 0, "rl": 0, "sx": 0}

        def rot(key, n):
            i = ctr[key] % n
            ctr[key] += 1
            return i

        def rstd_from(ss):
            P.act(ss, ss, AF.Sqrt, bias=cs(C_EPS), scale=1.0 / D, name="sqrt")
            P.add("dve", lambda e: e.reciprocal(ss, ss), reads=[ss], writes=[ss], name="rstd")
            return ss

        def make_pass(ps_i):
            last_pass = ps_i == npass - 1
            has_s = with_sample and last_pass
            tiles = []
            for i in range(NPP // 128):
                r0 = ps_i * NPP + i * 128
                tiles.append((x_p[r0:r0 + 128, :], y_p[r0:r0 + 128, :]))
            if has_s:
                tiles.append((x_s[:, :], y_s[:, :]))
            tgs = [(g * 512, 512, "p") for g in range(ntg)]
            if has_s:
                tgs.append((NPP, 128, "s"))
            return dict(idx=ps_i, last_pass=last_pass, has_s=has_s, tiles=tiles, tgs=tgs, NT=len(tiles))

        passes = [make_pass(i) for i in range(npass)]

        def norm_prep(src_tok, xn_eng="act"):
            ss = stat()
            P.act(JUNK, src_tok, AF.Square, accum_out=ss, name="sq")
            rstd_from(ss)
            xn = XN[rot("xn", 2)]
            if xn_eng == "act":
                P.act(xn, src_tok, AF.Identity, scale=ss, name="xn")
            else:
                P.ts("dve", xn, src_tok, ss, None, ALU.mult, name="xn")
            return xn

        def norm_tr(xn, dstT, col0, gcol):
            b = bank()
            pb = PS[:, b, :].bitcast(BF16)
            for k in range(KC):
                P.tr(pb[:, k * 128:(k + 1) * 128], xn[:, k * 128:(k + 1) * 128], IDB[:, :], signal=(k == KC - 1))
            gb = bass.AP(CS, gcol, [[96, 128], [1, KC], [0, 128]])
            P.tt("dve", dstT[:, :, col0:col0 + 128], pb.rearrange("p (k t) -> p k t", t=128), gb, ALU.mult,
                 name="evacT")

        def s0_steps(pinfo, lo=0, hi=None, with_state=True):
            st = {}
            steps = []
            NT_ = pinfo["NT"]
            hi = NT_ if hi is None else hi

            def mk(u):
                def step():
                    if lo + 2 <= u:
                        norm_tr(st.pop(u - 2), hT, (u - 2) * 128, C_GPM)
                    if u < hi:
                        if pinfo["idx"] == 0:
                            if u == 0:
                                for i_ in range(NT_):
                                    xload0.append(P.dma("sp", x1[:, i_, :], pinfo["tiles"][i_][0], "x0_%d" % i_, name="xload0"))
                            xt = x1[:, u, :]
                        else:
                            xs_ = rot("xt", 2)
                            xt = XT[xs_]
                            P.dma("sp", xt, pinfo["tiles"][u][0], "xt%d" % xs_, name="xload")
                        st[u] = norm_prep(xt, "dve" if pinfo["idx"] == 0 else "act")
                return step
            for u in range(lo, hi + 2):
                steps.append(mk(u))
            if pinfo["has_s"] and with_state:
                def ld():
                    P.dma("sp", XT[0][0:120, :], st_pool[0:120, :], "xt0", name="stload")
                    P.dma("sp", XT[1][0:120, :], st_pool[120:240, :], "xt1", name="stload")
                    P.dma("sp", XNF[0:32, :], st_conv[:, :], "stc", name="stload")

                def trs(which):
                    def f():
                        ci = 0
                        for j in range(KC):
                            for (src, nrow, dst) in which(j):
                                bt = bank()
                                P.tr(PS[:, bt, 0:nrow], src, IDF[0:nrow, 0:nrow])
                                P.copy("dve" if ci % 2 == 0 else "act", dst, PS[:, bt, 0:nrow])
                                ci += 1
                    return f
                steps.append(ld)
                steps.append(lambda: None)
                steps.append(trs(lambda j: [(XNF[0:32, j * 128:(j + 1) * 128], 32, SCT[:, j, :]),
                                            (XT[0][0:120, j * 128:(j + 1) * 128], 120, SPT[0][:, j, :])]))
                steps.append(trs(lambda j: [(XT[1][0:120, j * 128:(j + 1) * 128], 120, SPT[1][:, j, :])]))
            return steps

        xload0 = []
        if passes[0]["NT"] >= 8:
            pending_s0 = s0_steps(passes[0], 0, 4)
            s0_rest_p0 = s0_steps(passes[0], 4, None)
        else:
            pending_s0 = s0_steps(passes[0])
            s0_rest_p0 = []

        for ps_i in range(npass):
            pinfo = passes[ps_i]
            last_pass, has_s, tiles, tgs, NT = pinfo["last_pass"], pinfo["has_s"], pinfo["tiles"], pinfo["tgs"], pinfo["NT"]
            T = NT * 128

            s0_rest = s0_rest_p0 if ps_i == 0 else []
            for step in pending_s0:
                step()
            pending_s0 = []
            if ps_i == 0:
                gd_ = [xload0[-1]] if xload0 else []
                P.dma("sp", GBC[:, 0, :], bass.AP(g_post_mix.tensor, 0, [[0, 128], [1, D]]), "gbc0", deps=gd_)
                P.dma("sp", GBC[:, 1, :], bass.AP(g_post_mlp.tensor, 0, [[0, 128], [1, D]]), "gbc1", deps=gd_)

            for b in range(2):
                wxv, wbg, wcg = wgroup([(w_in, 0, 0 * D + b * 512), (w_in, 0, 1 * D + b * 512), (w_in, 0, 2 * D + b * 512)])
                for jj in range(4):
                    j = b * 4 + jj
                    prev_zc = None
                    for (c0, n, kind) in tgs:
                        bxv, bcg, bbg = bank(), bank(), bank()
                        for (bk, wt) in ((bxv, wxv), (bcg, wcg), (bbg, wbg)):
                            for k in range(KC):
                                P.mm(PS[:, bk, 0:n], wt.lhsT(k, jj), hT[:, k, c0:c0 + n],
                                     start=(k == 0), stop=(k == KC - 1))
                        si = rot("cv", 2)
                        xv = XV[si]
                        tcv = TCV[si]
                        if kind == "p":
                            zc = ZC[si]
                            zsl = lambda a, bb, zc=zc: zc[:, a:bb]
                            pview = lambda ap: ap
                            if prev_zc is None:
                                P.copy("dve", zc[:, 0:2], ZST[:, j, :])
                            else:
                                P.copy("dve", zc[:, 0:2], prev_zc[:, 512:514])
                            nn = n
                            xv_v, tcv_v = xv[:, 0:n], tcv[:, 0:n]
                            pcg, pbg, pxv = PS[:, bcg, 0:n], PS[:, bbg, 0:n], PS[:, bxv, 0:n]
                        else:
                            zc = ZSX[rot("sx", 2)]
                            zsl = lambda a, bb, zc=zc: zc[:, :, a:bb]
                            nn = 8
                            r3 = lambda ap: ap.rearrange("p (s t) -> p s t", t=8)
                            xv_v, tcv_v = r3(xv[:, 0:n]), r3(tcv[:, 0:n])
                            pcg, pbg, pxv = r3(PS[:, bcg, 0:n]), r3(PS[:, bbg, 0:n]), r3(PS[:, bxv, 0:n])
                            P.copy("dve", zc[:, :, 0:2], SCT[:, j, :].rearrange("p (s t) -> p s t", t=2))
                        P.copy("act", xv_v, pxv, name="xv")
                        P.tt("dve", zsl(2, 2 + nn), pcg, xv_v, ALU.mult, name="z")
                        P.act(tcv_v, zsl(0, nn), AF.Identity, scale=cs(C_WC + 0 * 8 + j), name="cv0")
                        P.stt("dve", tcv_v, zsl(1, 1 + nn), cs(C_WC + 1 * 8 + j), tcv_v, ALU.mult, ALU.add, name="cv1")
                        P.stt("dve", tcv_v, zsl(2, 2 + nn), cs(C_WC + 2 * 8 + j), tcv_v, ALU.mult, ALU.add, name="cv2")
                        if kind == "p":
                            P.tt("dve", ApT[:, j, c0:c0 + n], pbg, tcv_v, ALU.mult, name="Ap")
                            prev_zc = zc
                            if c0 + n == NPP:
                                P.copy("dve", ZST[:, j, :], zc[:, 512:514])
                        else:
                            P.tt("dve", ApT[:, j, c0:c0 + n].rearrange("p (s t) -> p s t", t=8), pbg, tcv_v, ALU.mult,
                                 name="Ap")
                            P.copy("dve", ZS_[:, j, :].rearrange("p (s t) -> p s t", t=2), zc[:, :, 8:10])
                        if ps_i == 0 and j == 0 and c0 == 0:
                            for step in s0_rest:
                                step()
                            s0_rest = []
                            wgroup(topup=True)
                            P.dma("pool", WPG[:, :, :, :], w_pool_group.rearrange("g (c p) o -> p g c o", p=128), "wpg")

            deferred = []

            def flush_deferred():
                while deferred:
                    deferred.pop(0)[1]()

            def s1b_chunk(j, jj, wu):
                g = j // 2
                w = POOL_W[g]
                prev_uc = None
                for (c0, n, kind) in tgs:
                    bu = bank()
                    for k in range(KC):
                        P.mm(PS[:, bu, 0:n], wu.lhsT(k, jj), hT[:, k, c0:c0 + n],
                             start=(k == 0), stop=(k == KC - 1))
                    pe_ = "pool" if (w > 2 and ((kind == "p" and (c0 // 512) % 2 == 0) or (kind == "s" and j % 2 == 0))) else "dve"
                    if kind == "p":
                        si = rot("pl", 3)
                        uc, pb_, qb_ = UC[si], PB, QB
                        if pe_ == "pool":
                            pb_, qb_ = PBP, QBP
                        sl = lambda buf, a, bb: buf[:, a:bb]
                        nn = n
                        if prev_uc is None:
                            P.copy("dve", uc[:, 0:15], UST[:, j, :])
                        else:
                            P.copy("dve", uc[:, 0:15], prev_uc[:, 512:527])
                        pu = PS[:, bu, 0:n]
                        outv = plT[:, j, c0:c0 + n]
                    else:
                        uc, pb_, qb_ = USX[j % 2], PSX, QSX
                        sl = lambda buf, a, bb: buf[:, :, a:bb]
                        nn = 8
                        pu = PS[:, bu, 0:n].rearrange("p (s t) -> p s t", t=8)
                        outv = plT[:, j, c0:c0 + n].rearrange("p (s t) -> p s t", t=8)
                        for h in range(2):
                            P.copy("dve" if h == 0 else "act", uc[:, h * 8:(h + 1) * 8, 0:15],
                                   SPT[h][:, j, :].rearrange("p (s t) -> p s t", t=15))
                    E = 15 + nn
                    if pe_ == "pool":
                        for d_ in [d for d in deferred if d[0] == kind]:
                            deferred.remove(d_)
                            d_[1]()
                    P.copy("act", sl(uc, 15, E), pu, name="u")
                    if w == 2:
                        P.tt(pe_, outv, sl(uc, 14, E - 1), sl(uc, 15, E), ALU.subtract, name="pool2")
                        sw = None
                    else:
                        P.tt(pe_, sl(pb_, 1, E), sl(uc, 1, E), sl(uc, 0, E - 1), ALU.add, name="s2")
                        P.tt(pe_, sl(qb_, 3, E), sl(pb_, 3, E), sl(pb_, 1, E - 2), ALU.add, name="s4")
                        sw = qb_
                        if w >= 8:
                            P.tt(pe_, sl(pb_, 7, E), sl(qb_, 7, E), sl(qb_, 3, E - 4), ALU.add, name="s8")
                            sw = pb_
                        if w >= 16:
                            P.tt(pe_, sl(qb_, 15, E), sl(pb_, 15, E), sl(pb_, 7, E - 8), ALU.add, name="s16")
                            sw = qb_
                    def fin(w=w, outv=outv, uc=uc, sw=sw, sl=sl, E=E, kind=kind, c0=c0, j=j, g=g):
                        if w > 2:
                            P.stt("dve", outv, sl(uc, 15, E), -float(w), sl(sw, 15, E), ALU.mult, ALU.add, name="pooled")
                        if kind == "p" and ps_i == 0 and c0 == 0:
                            if w == 2:
                                P.memset("dve", plT[:, j, 0:1], 0.0)
                            else:
                                nfix = w - 1
                                P.tt("dve", TMPC[:, 0:nfix], sw[:, 15:15 + nfix], FAC[:, g, 0:nfix], ALU.mult)
                                P.stt("dve", plT[:, j, 0:nfix], uc[:, 15:15 + nfix], -float(w), TMPC[:, 0:nfix],
                                      ALU.mult, ALU.add)
                    if kind == "p":
                        prev_uc = uc
                        if c0 + n == NPP:
                            P.copy("act", UST[:, j, :], uc[:, 512:527])
                    else:
                        P.copy("act", US_[:, j, :].rearrange("p (s t) -> p s t", t=8), uc[:, :, 15:23])
                    if pe_ == "pool":
                        deferred.append((kind, fin))
                    else:
                        fin()
                    while len(deferred) > (1 if pe_ == "pool" else 0):
                        deferred.pop(0)[1]()

            def emit_state_outputs():
                if last_pass:
                    b2 = bank2()
                    for j in range(KC):
                        P.tr(PS[0:2, b2 + j // 4, (j % 4) * 128:(j % 4 + 1) * 128], ZST[:, j, :], IDF[:, :])
                    pz = bass.AP(PS, b2 * 512, [[4096, 2], [1, D]])
                    P.copy("dve", STGS[0][0:2, :], pz)
                    P.dma("sp", ncp[:, :], STGS[0][0:2, :], "so0", is_output=True)
                    b2 = bank2()
                    for j in range(KC):
                        P.tr(PS[0:15, b2 + j // 4, (j % 4) * 128:(j % 4 + 1) * 128], UST[:, j, :], IDF[:, :])
                    pz = bass.AP(PS, b2 * 512, [[4096, 15], [1, D]])
                    P.copy("act", STGS[1][0:15, :], pz)
                    P.dma("sp", npp[:, :], STGS[1][0:15, :], "so1", is_output=True)
                    if has_s:
                        b2 = bank2()
                        for j in range(KC):
                            P.tr(PS[0:32, b2 + j // 4, (j % 4) * 128:(j % 4 + 1) * 128], ZS_[:, j, :], IDF[:, :])
                        pz = bass.AP(PS, b2 * 512, [[4096, 32], [1, D]])
                        P.copy("dve", STGS[2][0:32, :], pz)
                        P.dma("sp", ncs[:, :], STGS[2][0:32, :], "so2", is_output=True)
                        b2 = bank2()
                        for j in range(KC):
                            P.tr(PS[:, b2 + j // 4, (j % 4) * 128:(j % 4 + 1) * 128], US_[:, j, :], IDF[:, :])
                        pz = bass.AP(PS, b2 * 512, [[4096, 128], [1, D]])
                        P.copy("act", STGS[3][:, :], pz)
                        for s in range(NSEQ):
                            P.dma("sp", nps[s * 15 + 7:s * 15 + 15, :], STGS[3][s * 8:(s + 1) * 8, :], "so3", is_output=True)
                        P.dma("sp", nps.rearrange("(s r) d -> s r d", r=15)[:, 0:7, :],
                              st_pool.rearrange("(s r) d -> s r d", r=15)[:, 8:15, :], "so4", is_output=True)


            if has_s:
                S2T = [_view(AR, A_X1 + 5120, F32, [512]), _view(AR, A_X1 + 14080, F32, [512])]
            else:
                S2T = [tmv(14848, F32, [512]), tmv(16896, F32, [512])]

            def s2_chunk(m, jj, wgc, wgp, woc):
                flush_deferred()
                g = m // 2
                for (c0, n, kind) in tgs:
                    bA, bB, bg1, bg2 = bank(), bank(), bank(), bank()
                    for k in range(KC):
                        P.mm(PS[:, bg1, 0:n], wgc.lhsT(k, jj), hT[:, k, c0:c0 + n],
                             start=(k == 0), stop=(k == KC - 1))
                    for k in range(KC):
                        P.mm(PS[:, bg2, 0:n], wgp.lhsT(k, jj), hT[:, k, c0:c0 + n],
                             start=(k == 0), stop=(k == KC - 1))
                    for k in range(KC):
                        P.mm(PS[:, bA, 0:n], woc.lhsT(k, jj), ApT[:, k, c0:c0 + n],
                             start=(k == 0), stop=(k == KC - 1))
                    for c in range(2):
                        P.mm(PS[:, bB, 0:n], WPG[:, g, c, (m % 2) * 128:(m % 2 + 1) * 128],
                             plT[:, 2 * g + c, c0:c0 + n], start=(c == 0), stop=(c == 1))
                    s1, s2 = S2T[0][:, 0:n], S2T[1][:, 0:n]
                    t1, t2 = s1, s2
                    P.act(s1, PS[:, bg1, 0:n], AF.Sigmoid, bias=cs(C_BG + m), name="sig1")
                    P.act(s2, PS[:, bg2, 0:n], AF.Sigmoid, bias=cs(C_BG + 8 + m), name="sig2")
                    P.tt("dve", t1, PS[:, bA, 0:n], s1, ALU.mult, name="mA")
                    P.stt("dve", t2, PS[:, bB, 0:n], cs(C_PS + m), s2, ALU.mult, ALU.mult, name="mB")
                    P.tt("dve", mgT[:, m, c0:c0 + n], t1, t2, ALU.add, name="merge")

            for b in range(2):
                wu, wgc, wgp, woc = wgroup([(w_in, 0, 3 * D + b * 512), (w_in, 0, 4 * D + b * 512),
                                            (w_in, 0, 5 * D + b * 512), (w_out_conv, 0, b * 512)])
                c_ = b * 4
                s1b_chunk(c_ + 0, 0, wu)
                s1b_chunk(c_ + 1, 1, wu)
                s2_chunk(c_ + 0, 0, wgc, wgp, woc)
                s1b_chunk(c_ + 2, 2, wu)
                s2_chunk(c_ + 1, 1, wgc, wgp, woc)
                s1b_chunk(c_ + 3, 3, wu)
                s2_chunk(c_ + 2, 2, wgc, wgp, woc)
                if b == 1:
                    emit_state_outputs()
                s2_chunk(c_ + 3, 3, wgc, wgp, woc)
            flush_deferred()

            wo = wgroup([(w_o, 0, 0), (w_o, 0, 512)])
            s3b, s3x = {}, {}

            def s3_A(i):
                b2 = bank2()
                for nh in range(2):
                    for k in range(KC):
                        P.mm(PS[:, b2 + nh, :], mgT[:, k, i * 128:(i + 1) * 128], wo[nh].full(k),
                             start=(k == 0), stop=(k == KC - 1))
                s3b[i] = b2

            def s3_B1(i):
                pm = bass.AP(PS, s3b.pop(i) * 512, [[4096, 128], [1, D]])
                ss = stat()
                P.act(JUNK, pm, AF.Square, accum_out=ss, name="sq_mix")
                rstd_from(ss)
                xs_ = rot("xt", 2)
                xt = XT[xs_]
                if ps_i == 0:
                    P.stt("dve", xt, pm, ss, GBC[:, 0, :], ALU.mult, ALU.mult, name="mixn")
                else:
                    P.dma("sp", xt, tiles[i][0], "xt%d" % xs_, name="xreload")
                    P.stt("dve", x1[:, i, :], pm, ss, GBC[:, 0, :], ALU.mult, ALU.mult, name="mixn")
                P.tt("pool", x1[:, i, :], x1[:, i, :], xt, ALU.add, name="x1")

            def s3_B2(i):
                s3x[i] = norm_prep(x1[:, i, :])

            def s3_C(i):
                norm_tr(s3x.pop(i), h2T, i * 128, C_GPL)

            s4_early = {"w": None, "done": set()}

            def s4_unit(wu_, f, jj, c0, n):
                bu = bank()
                for k in range(KC):
                    P.mm(PS[:, bu, 0:n], wu_.lhsT(k, jj), h2T[:, k, c0:c0 + n],
                         start=(k == 0), stop=(k == KC - 1))
                rl = RL[rot("rl", 3)][:, 0:n]
                P.act(rl, PS[:, bu, 0:n], AF.Relu, name="relu")
                P.tt("dve", actT[:, f, c0:c0 + n], rl, rl, ALU.mult, name="sqr")

            for u in range(NT + 3):
                if u < NT:
                    s3_A(u)
                if 1 <= u <= NT:
                    s3_B1(u - 1)
                if 2 <= u <= NT + 1:
                    s3_B2(u - 2)
                u0_ = max(7, NT - 1)
                if u >= u0_ and u - u0_ < 4:
                    if s4_early["w"] is None:
                        s4_early["w"], = wgroup([(w_up, 0, 0)])
                    jj_ = u - u0_
                    s4_unit(s4_early["w"], jj_, jj_, tgs[0][0], tgs[0][1])
                    s4_early["done"].add((jj_, tgs[0][0]))
                if u >= 3:
                    s3_C(u - 3)

            for blk in range(DFF // 512):
                if blk == 0 and s4_early["w"] is not None:
                    wu_ = s4_early["w"]
                else:
                    wu_, = wgroup([(w_up, 0, blk * 512)])
                for jj in range(4):
                    f = blk * 4 + jj
                    for (c0, n, kind) in tgs:
                        if (f, c0) in s4_early["done"]:
                            continue
                        s4_unit(wu_, f, jj, c0, n)

            groups = [list(range(a, min(a + 3, NT))) for a in range(0, NT, 3)]
            if not last_pass:
                pending_s0 = s0_steps(passes[ps_i + 1])
            unit_i = 0
            bank_pool[0] = [6, 7]
            for G in groups:
                bk = {}
                for gi_, i in enumerate(G):
                    bk[i] = 2 * gi_
                od_ = s5_order(groups.index(G))
                for (nh, kg) in od_:
                    kgs_ = [kg2 for (nh2, kg2) in od_ if nh2 == nh]
                    first_k, last_k = (kg == kgs_[0]), (kg == kgs_[-1])
                    wd, = wgroup([(w_down, kg * 1024, nh * 512)])
                    if unit_i >= 1 and pending_s0:
                        pending_s0.pop(0)()
                    unit_i += 1
                    for i in G:
                        for kk in range(KC):
                            P.mm(PS[:, bk[i] + nh, :], actT[:, kg * 8 + kk, i * 128:(i + 1) * 128], wd.full(kk),
                                 start=(first_k and kk == 0), stop=(last_k and kk == KC - 1),
                                 signal=((last_k or i == G[-1]) and kk == KC - 1))
                for i in G:
                    pm = bass.AP(PS, bk[i] * 512, [[4096, 128], [1, D]])
                    ss = stat()
                    P.act(JUNK, pm, AF.Square, accum_out=ss, name="sq_mlp")
                    rstd_from(ss)
                    t4 = XT[rot("xt", 2)]
                    P.stt("dve", t4, pm, ss, GBC[:, 1, :], ALU.mult, ALU.mult, name="mlpn")
                    P.tt("dve", x1[:, i, :], x1[:, i, :], t4, ALU.add, name="y")
                    P.dma("sp", tiles[i][1], x1[:, i, :], "yo%d" % i, name="ystore", is_output=True)
            for step in pending_s0:
                step()
            pending_s0 = []
            bank_pool[0] = None

        nwait = P.emit(stack)
    return nc


_NC_CACHE = {}


def kernel(**inputs):
    f = lambda a: np.ascontiguousarray(np.asarray(a, dtype=np.float32))
    x_prompt = f(inputs["x_prompt"])
    x_sample = f(inputs["x_sample"])
    state_conv = f(inputs["state_conv"])
    state_pool = f(inputs["state_pool"])
    shared = {
        "g_pre_mix": f(inputs["g_pre_mix"]).reshape(D),
        "w_in": f(inputs["w_in"]).reshape(D, 6 * D),
        "b_gate": f(inputs["b_gate"]).reshape(2 * D),
        "w_conv": f(inputs["w_conv"]).reshape(3, D),
        "w_out_conv": f(inputs["w_out_conv"]).reshape(D, D),
        "w_pool_group": f(inputs["w_pool_group"]).reshape(4, 256, 256),
        "pool_scale": f(inputs["pool_scale"]).reshape(D),
        "w_o": f(inputs["w_o"]).reshape(D, D),
        "g_post_mix": f(inputs["g_post_mix"]).reshape(D),
        "g_pre_mlp": f(inputs["g_pre_mlp"]).reshape(D),
        "w_up": f(inputs["w_up"]).reshape(D, DFF),
        "w_down": f(inputs["w_down"]).reshape(DFF, D),
        "g_post_mlp": f(inputs["g_post_mlp"]).reshape(D),
    }
    in_maps = []
    for c in range(NCORES):
        m = dict(shared)
        m["x_p"] = x_prompt[c]
        m["x_s"] = x_sample[16 * c:16 * (c + 1)].reshape(128, D)
        m["st_conv"] = state_conv[0, 16 * c:16 * (c + 1)].reshape(32, D)
        m["st_pool"] = state_pool[0, 16 * c:16 * (c + 1)].reshape(240, D)
        in_maps.append(m)
    if "nc" not in _NC_CACHE:
        _NC_CACHE["nc"] = build_program()
    nc = _NC_CACHE["nc"]
    res = run_bass_kernel_spmd(nc, in_maps, core_ids=list(range(NCORES)))
    R = res.results
    y_prompt = np.stack([R[c]["y_p"] for c in range(NCORES)], axis=0)
    y_sample = np.concatenate([R[c]["y_s"].reshape(16, 8, D) for c in range(NCORES)], axis=0)
    ncp = np.stack([R[c]["ncp"] for c in range(NCORES)], axis=0)[None]
    npp = np.stack([R[c]["npp"] for c in range(NCORES)], axis=0)[None]
    ncs = np.concatenate([R[c]["ncs"].reshape(16, 2, D) for c in range(NCORES)], axis=0)[None]
    nps = np.concatenate([R[c]["nps"].reshape(16, 15, D) for c in range(NCORES)], axis=0)[None]
    return (y_prompt.astype(np.float32), y_sample.astype(np.float32), ncp.astype(np.float32),
            npp.astype(np.float32), ncs.astype(np.float32), nps.astype(np.float32))
```

```python
import numpy as np
from contextlib import ExitStack
import concourse.bass as bass
import concourse.mybir as mybir
from concourse.bass_utils import run_bass_kernel_spmd

F32 = mybir.dt.float32
BF16 = mybir.dt.bfloat16
ALU = mybir.AluOpType
AF = mybir.ActivationFunctionType
SZ = {F32: 4, BF16: 2}

D = 1024
KC = 8
DFF = 4096
EPS = 1e-6
NCORES = 8
POOL_W = (2, 4, 8, 16)


def _prod(xs):
    r = 1
    for x in xs:
        r *= int(x)
    return r


class _Op:
    __slots__ = ("eng", "fn", "deps", "signal", "pos", "dma", "name", "sig", "wait_val")

    def __init__(self, eng, fn, name):
        self.eng = eng
        self.fn = fn
        self.deps = set()
        self.signal = True
        self.pos = None
        self.dma = None
        self.name = name
        self.sig = None
        self.wait_val = None


class Prog:
    ENG = ("pe", "act", "dve", "pool", "sp")

    def __init__(self, nc):
        self.nc = nc
        self.engs = {"pe": nc.tensor, "act": nc.scalar, "dve": nc.vector,
                     "pool": nc.gpsimd, "sp": nc.sync}
        self.streams = {e: [] for e in self.ENG}
        self.recs = {}
        self.dma_cnt = {}
        self.out_dmas = []

    @staticmethod
    def _iv(ap):
        t = ap.tensor
        if type(t).__name__.startswith("DRam"):
            return None
        esz = SZ[ap.dtype]
        pbytes = _prod(t.shape[1:]) * SZ[t.dtype]
        off = int(ap.offset) * esz
        lo = off % pbytes
        dims = list(ap.ap)
        p0 = off // pbytes
        p1 = p0 + ((int(dims[0][1]) - 1) * abs(int(dims[0][0])) * esz) // pbytes + 1
        fd = sorted([(abs(int(st)) * esz, int(cnt)) for (st, cnt) in dims[1:] if int(cnt) > 1], key=lambda d: -d[0])

        def rec(base, ds):
            if not ds:
                return [(base, base + esz)]
            inner = esz + sum((c - 1) * s_ for s_, c in ds[1:])
            s0, c0 = ds[0]
            if s0 >= 2 * inner and c0 <= 32:
                out = []
                for i in range(c0):
                    out += rec(base + i * s0, ds[1:])
                return out
            return [(base, base + esz + sum((c - 1) * s_ for s_, c in ds))]
        ivs = rec(lo, fd)
        if len(ivs) > 64:
            ivs = [(lo, lo + esz + sum((c - 1) * s_ for s_, c in fd))]
        return t.name, ivs, p0, p1

    def _acc(self, op, ap, is_write):
        iv = self._iv(ap)
        if iv is None:
            return
        key, ivs, p0, p1 = iv
        for (lo, hi) in ivs:
            self._acc1(op, key, lo, hi, p0, p1, is_write)

    def _acc1(self, op, key, lo, hi, p0, p1, is_write):
        recs = self.recs.get(key, [])
        new = []
        for r in recs:
            rlo, rhi, rop, rw, rp0, rp1 = r
            if rhi <= lo or rlo >= hi or rp1 <= p0 or rp0 >= p1:
                new.append(r)
                continue
            if (is_write or rw) and rop is not op:
                op.deps.add(rop)
            cov = rlo >= lo and rhi <= hi and rp0 >= p0 and rp1 <= p1
            if is_write and cov:
                continue
            if (not is_write) and (not rw) and rop.dma is None and op.dma is None \
                    and rop.eng == op.eng and cov:
                continue
            new.append(r)
        new.append((lo, hi, op, is_write, p0, p1))
        self.recs[key] = new

    def add(self, eng, fn, reads=(), writes=(), signal=True, name="", dma=None):
        op = _Op(eng, fn, name)
        op.signal = signal
        op.dma = dma
        for ap in reads:
            if ap is not None and not isinstance(ap, (int, float)):
                self._acc(op, ap, False)
        for ap in writes:
            self._acc(op, ap, True)
        op.pos = len(self.streams[eng])
        self.streams[eng].append(op)
        return op

    def dma(self, q, out, in_, semkey, name="", is_output=False, deps=(), **kw):
        cnt = self.dma_cnt.get(semkey, 0) + 16
        self.dma_cnt[semkey] = cnt
        op = self.add(q, lambda e: e.dma_start(out=out, in_=in_, **kw), reads=[in_], writes=[out],
                      name=name, dma=(semkey, cnt))
        op.deps.update(deps)
        if is_output:
            self.out_dmas.append(op)
        return op

    def mm(self, out, lhsT, rhs, start, stop, signal=None):
        sig = stop if signal is None else signal
        return self.add("pe", lambda e: e.matmul(out, lhsT, rhs, start=start, stop=stop),
                        reads=[lhsT, rhs], writes=[out], signal=sig, name="mm")

    def tr(self, out, in_, ident, signal=True):
        return self.add("pe", lambda e: e.transpose(out, in_, ident), reads=[in_, ident], writes=[out],
                        signal=signal, name="tr")

    def act(self, out, in_, func, bias=None, scale=None, accum_out=None, name="act"):
        kw = {}
        if bias is not None:
            kw["bias"] = bias
        if scale is not None:
            kw["scale"] = scale
        if accum_out is not None:
            kw["accum_out"] = accum_out
        rd = [in_] + [x for x in (bias, scale) if x is not None and not isinstance(x, (int, float))]
        wr = [out] + ([accum_out] if accum_out is not None else [])
        return self.add("act", lambda e: e.activation(out, in_, func, **kw), reads=rd, writes=wr, name=name)

    def tt(self, eng, out, in0, in1, op, name="tt"):
        return self.add(eng, lambda e: e.tensor_tensor(out, in0, in1, op), reads=[in0, in1], writes=[out], name=name)

    def ts(self, eng, out, in0, s1, s2, op0, op1=None, name="ts"):
        rd = [in0] + [x for x in (s1, s2) if x is not None and not isinstance(x, (int, float))]
        if op1 is None:
            return self.add(eng, lambda e: e.tensor_scalar(out, in0, s1, s2, op0), reads=rd, writes=[out], name=name)
        return self.add(eng, lambda e: e.tensor_scalar(out, in0, s1, s2, op0, op1), reads=rd, writes=[out], name=name)

    def stt(self, eng, out, in0, scalar, in1, op0, op1, name="stt"):
        rd = [in0, in1] + ([scalar] if not isinstance(scalar, (int, float)) else [])
        return self.add(eng, lambda e: e.scalar_tensor_tensor(out, in0, scalar, in1, op0, op1),
                        reads=rd, writes=[out], name=name)

    def copy(self, eng, out, in_, name="copy"):
        if eng == "act":
            return self.act(out, in_, AF.Identity, name=name)
        return self.add(eng, lambda e: e.tensor_copy(out, in_), reads=[in_], writes=[out], name=name)

    def memset(self, eng, out, val, name="memset"):
        return self.add(eng, lambda e: e.memset(out, val), writes=[out], name=name)

    def emit(self, stack):
        nc = self.nc
        sems = {}
        for e in ("pe", "act", "dve", "pool"):
            sems[e] = stack.enter_context(nc.semaphore("s_" + e))
        for k in self.dma_cnt:
            sems[("dma", k)] = stack.enter_context(nc.semaphore("d_" + str(k)))
        fin = _Op("sp", None, "final")
        fin.deps = set(self.out_dmas)
        self.streams["sp"].append(fin)
        for e in self.ENG:
            c = 0
            for op in self.streams[e]:
                if op.dma is None and op.fn is not None and op.signal:
                    c += 1
                    op.sig = c
            nxt = None
            for op in reversed(self.streams[e]):
                if op.dma is None and op.fn is not None:
                    if op.signal:
                        nxt = op.sig
                    op.wait_val = nxt
        nwait = 0
        for e in self.ENG:
            eng = self.engs[e]
            waited = {}
            for op in self.streams[e]:
                need = {}
                for d in op.deps:
                    if d.dma is not None:
                        key = ("dma", d.dma[0])
                        val = d.dma[1]
                    else:
                        if d.eng == "pe" and e == "pe":
                            continue
                        key = d.eng
                        val = d.wait_val
                        assert val is not None, (d.name, op.name)
                    if waited.get(key, 0) >= val:
                        continue
                    if need.get(key, 0) < val:
                        need[key] = val
                for key, val in need.items():
                    waited[key] = val
                    eng.wait_ge(sems[key], val)
                    nwait += 1
                if op.fn is not None:
                    ins = op.fn(eng)
                    if op.dma is not None:
                        ins.then_inc(sems[("dma", op.dma[0])], 16)
                    elif op.signal:
                        ins.then_inc(sems[e], 1)
        return nwait


def _view(t, boff, dt, shape):
    n = _prod(shape)
    esz = SZ[dt]
    assert boff % 4 == 0
    e0 = boff // 2
    e1 = e0 + n * esz // 2
    ap = t[:, e0:e1]
    if dt != BF16:
        ap = ap.bitcast(dt)
    if len(shape) == 2:
        ap = ap.rearrange("p (a b) -> p a b", b=shape[1])
    elif len(shape) == 3:
        ap = ap.rearrange("p (a b c) -> p a b c", b=shape[1], c=shape[2])
    return ap


def build_program(ntg=2, with_sample=True, npass=2):
    nc = bass.Bass("TRN2", target_bir_lowering=False)
    P = Prog(nc)
    NPP = 512 * ntg
    NPT = NPP * npass
    NSEQ = 16
    NST = NSEQ * 8
    TMAX = NPP + (NST if with_sample else 0)

    def din(name, shape):
        return nc.dram_tensor(name, list(shape), F32, kind="ExternalInput").ap()

    def dout(name, shape):
        return nc.dram_tensor(name, list(shape), F32, kind="ExternalOutput").ap()

    x_p = din("x_p", [NPT, D])
    x_s = din("x_s", [NST, D])
    st_conv = din("st_conv", [NSEQ * 2, D])
    st_pool = din("st_pool", [NSEQ * 15, D])
    g_pre_mix = din("g_pre_mix", [D])
    w_in = din("w_in", [D, 6 * D])
    b_gate = din("b_gate", [2 * D])
    w_conv = din("w_conv", [3, D])
    w_out_conv = din("w_out_conv", [D, D])
    w_pool_group = din("w_pool_group", [4, 256, 256])
    pool_scale = din("pool_scale", [D])
    w_o = din("w_o", [D, D])
    g_post_mix = din("g_post_mix", [D])
    g_pre_mlp = din("g_pre_mlp", [D])
    w_up = din("w_up", [D, DFF])
    w_down = din("w_down", [DFF, D])
    g_post_mlp = din("g_post_mlp", [D])

    y_p = dout("y_p", [NPT, D])
    y_s = dout("y_s", [NST, D])
    ncp = dout("ncp", [2, D])
    npp = dout("npp", [15, D])
    ncs = dout("ncs", [NSEQ * 2, D])
    nps = dout("nps", [NSEQ * 15, D])

    wd_bf = nc.dram_tensor("wd_bf", [8, 128, KC * 512], BF16, kind="Internal").ap()

    stack = ExitStack()
    with stack:
        NT_MAX = TMAX // 128
        A_ACT = 0
        A_HT = 0
        A_AP = KC * TMAX * 2
        A_PL = 2 * KC * TMAX * 2
        A_MG = 3 * KC * TMAX * 2
        A_X1 = 32 * TMAX * 2
        A_H2 = A_X1 + NT_MAX * 4096
        A_END = A_H2 + KC * TMAX * 2
        AR = stack.enter_context(nc.sbuf_tensor("AR", [128, A_END // 2], BF16))
        NSLOT = 6
        WS = stack.enter_context(nc.sbuf_tensor("WS", [128, NSLOT * 4096], BF16))
        WPG = stack.enter_context(nc.sbuf_tensor("WPG", [128, 4, 2, 256], BF16))
        TM_BYTES = 20096
        TM = stack.enter_context(nc.sbuf_tensor("TM", [128, TM_BYTES // 2], BF16))
        GBC = stack.enter_context(nc.sbuf_tensor("GBC", [128, 2, D], F32))
        IDB = stack.enter_context(nc.sbuf_tensor("IDB", [128, 128], BF16))
        IDF = stack.enter_context(nc.sbuf_tensor("IDF", [128, 128], F32))
        CS = stack.enter_context(nc.sbuf_tensor("CS", [128, 96], F32))
        STT_ = stack.enter_context(nc.sbuf_tensor("STAT", [128, 64], F32))
        ZST = stack.enter_context(nc.sbuf_tensor("ZST", [128, KC, 2], F32))
        UST = stack.enter_context(nc.sbuf_tensor("UST", [128, KC, 15], F32))
        FAC = stack.enter_context(nc.sbuf_tensor("FAC", [128, 5, 16], F32))
        PS = stack.enter_context(nc.psum_tensor("PS", [128, 8, 512], F32))

        hT = _view(AR, A_H2, BF16, [KC, TMAX])
        ApT = _view(AR, A_AP, BF16, [KC, TMAX])
        plT = _view(AR, A_PL, BF16, [KC, TMAX])
        mgT = _view(AR, A_MG, BF16, [KC, TMAX])
        actT = _view(AR, A_ACT, BF16, [32, TMAX])
        x1 = _view(AR, A_X1, F32, [NT_MAX, D])
        h2T = _view(AR, A_H2, BF16, [KC, TMAX])
        US_ = _view(AR, A_X1, F32, [KC, 128])
        ZS_ = _view(AR, A_X1 + KC * 128 * 4, F32, [KC, 32])

        C_GPM = 0
        C_GPL = 8
        C_BG = 16
        C_WC = 32
        C_PS = 56
        C_EPS = 64

        def cs(col, n=1):
            return CS[:, col:col + n]

        stat_ctr = [0]

        def stat():
            i = stat_ctr[0] % 64
            stat_ctr[0] += 1
            return STT_[:, i:i + 1]

        bank_ctr = [0]
        bank_pool = [None]

        def bank():
            if bank_pool[0] is not None:
                pool_ = bank_pool[0]
                b = pool_[bank_ctr[0] % len(pool_)]
                bank_ctr[0] += 1
                return b
            b = bank_ctr[0] % 8
            bank_ctr[0] += 1
            return b

        def bank2():
            if bank_ctr[0] % 2:
                bank_ctr[0] += 1
            b = bank_ctr[0] % 8
            bank_ctr[0] += 2
            return b

        slot_ctr = [0]

        def wslot():
            s = slot_ctr[0] % NSLOT
            slot_ctr[0] += 1
            return s

        def s5_order(gi_):
            od = [(nh, kg) for nh in range(2) for kg in range(4)]
            return od if gi_ % 2 == 0 else od[::-1]

        plan = []
        for ps_i in range(npass):
            nt_ = NPP // 128 + (1 if (with_sample and ps_i == npass - 1) else 0)
            for b in range(2):
                plan.append([(w_in, 0, 0 * D + b * 512), (w_in, 0, 1 * D + b * 512), (w_in, 0, 2 * D + b * 512)])
            for b in range(2):
                plan.append([(w_in, 0, 3 * D + b * 512), (w_in, 0, 4 * D + b * 512), (w_in, 0, 5 * D + b * 512),
                             (w_out_conv, 0, b * 512)])
            plan.append([(w_o, 0, 0), (w_o, 0, 512)])
            plan.append([(w_up, 0, 0), (w_up, 0, 512)])
            for blk in range(2, DFF // 512):
                plan.append([(w_up, 0, blk * 512)])
            for gi_, a in enumerate(range(0, nt_, 3)):
                for (nh, kg) in s5_order(gi_):
                    plan.append([(w_down, kg * 1024, nh * 512)])
        entries = [e for g in plan for e in g]
        gfirst = []
        gof = []
        acc_ = 0
        for gi_, g in enumerate(plan):
            gfirst.append(acc_)
            acc_ += len(g)
            gof += [gi_] * len(g)
        keyof = [(id(W_), r_, c_) for (W_, r_, c_) in entries]
        slot_of, need_load, free_after = [], [], []
        content = [None] * NSLOT
        last_use = [-1] * NSLOT
        for e, k_ in enumerate(keyof):
            if k_ in content and e >= 3 and k_[0] == id(w_down):
                sl_ = content.index(k_)
                slot_of.append(sl_)
                need_load.append(False)
                free_after.append(-1)
            else:
                same_grp = [slot_of[e2] for e2 in range(gfirst[gof[e]], e)]
                best, best_key = None, None
                for sl_ in range(NSLOT):
                    if sl_ in same_grp:
                        continue
                    if content[sl_] is None or content[sl_][0] != id(w_down):
                        nu = 10 ** 9
                    else:
                        nu = 10 ** 9
                        for e2 in range(e + 1, min(len(keyof), e + 24)):
                            if keyof[e2] == content[sl_]:
                                nu = e2
                                break
                    cand = (nu, -last_use[sl_])
                    if best_key is None or cand > best_key:
                        best, best_key = sl_, cand
                slot_of.append(best)
                need_load.append(True)
                free_after.append(gof[last_use[best]] if last_use[best] >= 0 else -1)
                content[best] = k_
            last_use[slot_of[e]] = e
        wstate = {"g": 0, "issued": 0, "conv": 0}
        conv_ops = []

        class WV:
            def __init__(self, e, split):
                self.split = split
                if split:
                    self.v = _view(WS, slot_of[e] * 8192, BF16, [4, KC, 128])
                else:
                    self.v = _view(WS, slot_of[e] * 8192, BF16, [KC, 512])

            def lhsT(self, k, jj):
                return self.v[:, jj, k, :] if self.split else self.v[:, k, jj * 128:(jj + 1) * 128]

            def full(self, k):
                assert not self.split
                return self.v[:, k, :]

        def wslot_view(e):
            return WV(e, e < 3)

        def wgroup(expect=None, topup=False):
            if topup:
                wstate["g"] -= 1
            g = wstate["g"]
            wstate["g"] += 1
            first, cnt = gfirst[g], len(plan[g])
            if expect is not None:
                assert [(a is b_, r, c) for (a, r, c), (b_, r2, c2) in zip(plan[g], expect)] and \
                    all(a is b_ and r == r2 and c == c2 for (a, r, c), (b_, r2, c2) in zip(plan[g], expect)), g
            if 1 <= g <= 4 and not topup:
                for c in (2 * (g - 1), 2 * (g - 1) + 1):
                    kg_, nh_ = c // 2, c % 2
                    conv_ops.append(P.dma("pool", wd_bf[c, :, :].rearrange("p (k c) -> p k c", c=512),
                                          w_down[kg_ * 1024:(kg_ + 1) * 1024, nh_ * 512:(nh_ + 1) * 512].rearrange(
                                              "(k p) c -> p k c", p=128),
                                          "wdc", name="wdconv"))
                    if c == 7:
                        for op_ in conv_ops:
                            op_.dma = ("wdc", P.dma_cnt["wdc"])
            while wstate["issued"] < len(entries):
                e = wstate["issued"]
                own = e < first + cnt
                ahead_ok = (free_after[e] < g) and not (g == 0 and not topup)
                if not (own or ahead_ok):
                    break
                if own:
                    assert free_after[e] < g
                W, r0, c0 = entries[e]
                if not need_load[e]:
                    pass
                elif e < 3:
                    if e == 0:
                        for jj_ in range(4):
                            for e_ in range(3):
                                W_, r_0, c_0 = entries[e_]
                                src = W_[r_0:r_0 + KC * 128, c_0 + jj_ * 128:c_0 + (jj_ + 1) * 128].rearrange(
                                    "(k p) c -> p k c", p=128)
                                P.dma("pool", wslot_view(e_).v[:, jj_, :, :], src, "wi%d_%d" % (e_, jj_), name="wload0",
                                      deps=([xload0[min(3, len(xload0) - 1)]] if (jj_ > 0 and xload0) else []))
                elif W is w_down:
                    src = wd_bf[(r0 // 1024) * 2 + (c0 // 512), :, :].rearrange("p (k c) -> p k c", c=512)
                    P.dma("pool", wslot_view(e).v, src, "w%d" % slot_of[e], name="wload_bf", deps=conv_ops)
                else:
                    src = W[r0:r0 + KC * 128, c0:c0 + 512].rearrange("(k p) c -> p k c", p=128)
                    P.dma("pool", wslot_view(e).v, src, "w%d" % slot_of[e], name="wload")
                wstate["issued"] += 1
            return [wslot_view(first + i) for i in range(cnt)]

        CST = _view(TM, 8192, F32, [128])
        r_ = lambda v: v.rearrange("(k p) -> k p", p=128)
        P.dma("act", CST[0:8, :], r_(g_pre_mix), "const")
        P.dma("act", CST[8:16, :], r_(g_pre_mlp), "const")
        P.dma("act", CST[16:32, :], r_(b_gate), "const")
        P.dma("act", CST[32:56, :], w_conv.rearrange("w (k p) -> (w k) p", p=128), "const")
        P.dma("act", CST[56:64, :], r_(pool_scale), "const")
        tot = P.dma_cnt["const"]
        for op in P.streams["act"]:
            if op.dma is not None and op.dma[0] == "const":
                op.dma = ("const", tot)

        P.memset("pool", IDF[:, :], 0.0)
        P.add("pool", lambda e: e.affine_select(out=IDF[:, :], in_=IDF[:, :], compare_op=ALU.not_equal,
                                                 fill=1.0, base=0, pattern=[[-1, 128]], channel_multiplier=1),
              reads=[IDF[:, :]], writes=[IDF[:, :]], name="ident")
        P.copy("dve", IDB[:, :], IDF[:, :])
        P.act(STT_[:, 63:64], IDF[:, 0:1], AF.Square, name="act_table_warm")
        bt0 = 0
        P.tr(PS[:, bt0, 0:64], CST[0:64, :], IDF[0:64, 0:64])
        P.copy("dve", CS[:, 0:64], PS[:, bt0, 0:64])
        P.add("pool", lambda e: e.iota(FAC[:, 4, :], pattern=[[1, 16]], base=1, channel_multiplier=0,
                                       allow_small_or_imprecise_dtypes=True),
              writes=[FAC[:, 4, :]], name="iota")
        for g, w in enumerate(POOL_W):
            P.ts("dve", FAC[:, g, :], FAC[:, 4, :], float(w), None, ALU.min)
            P.add("dve", lambda e, g=g: e.reciprocal(FAC[:, g, :], FAC[:, g, :]),
                  reads=[FAC[:, g, :]], writes=[FAC[:, g, :]], name="recip")
            P.ts("dve", FAC[:, g, :], FAC[:, g, :], float(w), None, ALU.mult)
            P.ts("dve", cs(C_PS + 2 * g, 2), cs(C_PS + 2 * g, 2), 1.0 / w, None, ALU.mult)
        P.memset("dve", cs(C_EPS), EPS)
        P.memset("dve", ZST[:, :, :], 0.0)
        P.memset("dve", UST[:, :, :], 0.0)

        def tmv(off, dt, shape):
            return _view(TM, off, dt, shape)

        XT = [tmv(0 + i * 4096, F32, [D]) for i in range(2)]
        XN = [tmv(8192 + i * 2048, BF16, [D]) for i in range(2)]
        JUNK = tmv(12288, BF16, [D])
        XV = [tmv(0 + i * 2048, F32, [512]) for i in range(2)]
        ZC = [tmv(4096 + i * 2064, F32, [514]) for i in range(2)]
        TCV = [tmv(8224 + i * 2048, F32, [512]) for i in range(2)]
        ZSX = [tmv(12320 + i * 640, F32, [16, 10]) for i in range(2)]
        UC = [tmv(i * 2112, F32, [527]) for i in range(3)]
        PB = tmv(6336, F32, [527])
        QB = tmv(8448, F32, [527])
        PBP = tmv(10560, F32, [527])
        QBP = tmv(12672, F32, [527])
        TMPC = tmv(14784, F32, [16])
        USX = [_view(AR, A_X1 + 2 * 4096 + i * 1472, F32, [16, 23]) for i in range(2)]
        PSX = _view(AR, A_X1 + 2 * 4096 + 2944, F32, [16, 23])
        QSX = _view(AR, A_X1 + 2 * 4096 + 4416, F32, [16, 23])
        SPT = [_view(AR, A_X1 + (NT_MAX - 1) * 4096, F32, [KC, 120]), tmv(14848, F32, [KC, 120])]
        SCT = tmv(18688, F32, [KC, 32])
        XNF = tmv(8192, F32, [D])
        SG1 = [tmv(0 + i * 2048, F32, [512]) for i in range(2)]
        SG2 = [tmv(4096 + i * 2048, F32, [512]) for i in range(2)]
        MT1 = [tmv(8192 + i * 2048, F32, [512]) for i in range(2)]
        MT2 = [tmv(12288 + i * 2048, F32, [512]) for i in range(2)]
        RL = [tmv(i * 2048, F32, [512]) for i in range(3)]
        if NT_MAX >= 9:
            STGS = [_view(AR, A_X1 + (4 + i) * 4096, F32, [D]) for i in range(4)]
        else:
            STGS = [tmv(i * 4096, F32, [D]) for i in range(4)]

        ctr = {"xt": 0, "xn": 0, "t4k": 0, "cv": 0, "pl": 0, "sg": 0, "rl": 0, "sx": 0}

        def rot(key, n):
            i = ctr[key] % n
            ctr[key] += 1
            return i

        def rstd_from(ss):
            P.act(ss, ss, AF.Sqrt, bias=cs(C_EPS), scale=1.0 / D, name="sqrt")
            P.add("dve", lambda e: e.reciprocal(ss, ss), reads=[ss], writes=[ss], name="rstd")
            return ss

        def make_pass(ps_i):
            last_pass = ps_i == npass - 1
            has_s = with_sample and last_pass
            tiles = []
            for i in range(NPP // 128):
                r0 = ps_i * NPP + i * 128
                tiles.append((x_p[r0:r0 + 128, :], y_p[r0:r0 + 128, :]))
            if has_s:
                tiles.append((x_s[:, :], y_s[:, :]))
            tgs = [(g * 512, 512, "p") for g in range(ntg)]
            if has_s:
                tgs.append((NPP, 128, "s"))
            return dict(idx=ps_i, last_pass=last_pass, has_s=has_s, tiles=tiles, tgs=tgs, NT=len(tiles))

        passes = [make_pass(i) for i in range(npass)]

        def norm_prep(src_tok, xn_eng="act"):
            ss = stat()
            P.act(JUNK, src_tok, AF.Square, accum_out=ss, name="sq")
            rstd_from(ss)
            xn = XN[rot("xn", 2)]
            if xn_eng == "act":
                P.act(xn, src_tok, AF.Identity, scale=ss, name="xn")
            else:
                P.ts("dve", xn, src_tok, ss, None, ALU.mult, name="xn")
            return xn

        def norm_tr(xn, dstT, col0, gcol):
            b = bank()
            pb = PS[:, b, :].bitcast(BF16)
            for k in range(KC):
                P.tr(pb[:, k * 128:(k + 1) * 128], xn[:, k * 128:(k + 1) * 128], IDB[:, :], signal=(k == KC - 1))
            gb = bass.AP(CS, gcol, [[96, 128], [1, KC], [0, 128]])
            P.tt("dve", dstT[:, :, col0:col0 + 128], pb.rearrange("p (k t) -> p k t", t=128), gb, ALU.mult,
                 name="evacT")

        def s0_steps(pinfo, lo=0, hi=None, with_state=True):
            st = {}
            steps = []
            NT_ = pinfo["NT"]
            hi = NT_ if hi is None else hi

            def mk(u):
                def step():
                    if lo + 2 <= u:
                        norm_tr(st.pop(u - 2), hT, (u - 2) * 128, C_GPM)
                    if u < hi:
                        if pinfo["idx"] == 0:
                            if u == 0:
                                for i_ in range(NT_):
                                    xload0.append(P.dma("sp", x1[:, i_, :], pinfo["tiles"][i_][0], "x0_%d" % i_, name="xload0"))
                            xt = x1[:, u, :]
                        else:
                            xs_ = rot("xt", 2)
                            xt = XT[xs_]
                            P.dma("sp", xt, pinfo["tiles"][u][0], "xt%d" % xs_, name="xload")
                        st[u] = norm_prep(xt, "dve" if pinfo["idx"] == 0 else "act")
                return step
            for u in range(lo, hi + 2):
                steps.append(mk(u))
            if pinfo["has_s"] and with_state:
                def ld():
                    P.dma("sp", XT[0][0:120, :], st_pool[0:120, :], "xt0", name="stload")
                    P.dma("sp", XT[1][0:120, :], st_pool[120:240, :], "xt1", name="stload")
                    P.dma("sp", XNF[0:32, :], st_conv[:, :], "stc", name="stload")

                def trs(which):
                    def f():
                        ci = 0
                        for j in range(KC):
                            for (src, nrow, dst) in which(j):
                                bt = bank()
                                P.tr(PS[:, bt, 0:nrow], src, IDF[0:nrow, 0:nrow])
                                P.copy("dve" if ci % 2 == 0 else "act", dst, PS[:, bt, 0:nrow])
                                ci += 1
                    return f
                steps.append(ld)
                steps.append(lambda: None)
                steps.append(trs(lambda j: [(XNF[0:32, j * 128:(j + 1) * 128], 32, SCT[:, j, :]),
                                            (XT[0][0:120, j * 128:(j + 1) * 128], 120, SPT[0][:, j, :])]))
                steps.append(trs(lambda j: [(XT[1][0:120, j * 128:(j + 1) * 128], 120, SPT[1][:, j, :])]))
            return steps

        xload0 = []
        if passes[0]["NT"] >= 8:
            pending_s0 = s0_steps(passes[0], 0, 4)
            s0_rest_p0 = s0_steps(passes[0], 4, None)
        else:
            pending_s0 = s0_steps(passes[0])
            s0_rest_p0 = []

        for ps_i in range(npass):
            pinfo = passes[ps_i]
            last_pass, has_s, tiles, tgs, NT = pinfo["last_pass"], pinfo["has_s"], pinfo["tiles"], pinfo["tgs"], pinfo["NT"]
            T = NT * 128

            s0_rest = s0_rest_p0 if ps_i == 0 else []
            for step in pending_s0:
                step()
            pending_s0 = []
            if ps_i == 0:
                gd_ = [xload0[-1]] if xload0 else []
                P.dma("sp", GBC[:, 0, :], bass.AP(g_post_mix.tensor, 0, [[0, 128], [1, D]]), "gbc0", deps=gd_)
                P.dma("sp", GBC[:, 1, :], bass.AP(g_post_mlp.tensor, 0, [[0, 128], [1, D]]), "gbc1", deps=gd_)

            for b in range(2):
                wxv, wbg, wcg = wgroup([(w_in, 0, 0 * D + b * 512), (w_in, 0, 1 * D + b * 512), (w_in, 0, 2 * D + b * 512)])
                for jj in range(4):
                    j = b * 4 + jj
                    prev_zc = None
                    for (c0, n, kind) in tgs:
                        bxv, bcg, bbg = bank(), bank(), bank()
                        for (bk, wt) in ((bxv, wxv), (bcg, wcg), (bbg, wbg)):
                            for k in range(KC):
                                P.mm(PS[:, bk, 0:n], wt.lhsT(k, jj), hT[:, k, c0:c0 + n],
                                     start=(k == 0), stop=(k == KC - 1))
                        si = rot("cv", 2)
                        xv = XV[si]
                        tcv = TCV[si]
                        if kind == "p":
                            zc = ZC[si]
                            zsl = lambda a, bb, zc=zc: zc[:, a:bb]
                            pview = lambda ap: ap
                            if prev_zc is None:
                                P.copy("dve", zc[:, 0:2], ZST[:, j, :])
                            else:
                                P.copy("dve", zc[:, 0:2], prev_zc[:, 512:514])
                            nn = n
                            xv_v, tcv_v = xv[:, 0:n], tcv[:, 0:n]
                            pcg, pbg, pxv = PS[:, bcg, 0:n], PS[:, bbg, 0:n], PS[:, bxv, 0:n]
                        else:
                            zc = ZSX[rot("sx", 2)]
                            zsl = lambda a, bb, zc=zc: zc[:, :, a:bb]
                            nn = 8
                            r3 = lambda ap: ap.rearrange("p (s t) -> p s t", t=8)
                            xv_v, tcv_v = r3(xv[:, 0:n]), r3(tcv[:, 0:n])
                            pcg, pbg, pxv = r3(PS[:, bcg, 0:n]), r3(PS[:, bbg, 0:n]), r3(PS[:, bxv, 0:n])
                            P.copy("dve", zc[:, :, 0:2], SCT[:, j, :].rearrange("p (s t) -> p s t", t=2))
                        P.copy("act", xv_v, pxv, name="xv")
                        P.tt("dve", zsl(2, 2 + nn), pcg, xv_v, ALU.mult, name="z")
                        P.act(tcv_v, zsl(0, nn), AF.Identity, scale=cs(C_WC + 0 * 8 + j), name="cv0")
                        P.stt("dve", tcv_v, zsl(1, 1 + nn), cs(C_WC + 1 * 8 + j), tcv_v, ALU.mult, ALU.add, name="cv1")
                        P.stt("dve", tcv_v, zsl(2, 2 + nn), cs(C_WC + 2 * 8 + j), tcv_v, ALU.mult, ALU.add, name="cv2")
                        if kind == "p":
                            P.tt("dve", ApT[:, j, c0:c0 + n], pbg, tcv_v, ALU.mult, name="Ap")
                            prev_zc = zc
                            if c0 + n == NPP:
                                P.copy("dve", ZST[:, j, :], zc[:, 512:514])
                        else:
                            P.tt("dve", ApT[:, j, c0:c0 + n].rearrange("p (s t) -> p s t", t=8), pbg, tcv_v, ALU.mult,
                                 name="Ap")
                            P.copy("dve", ZS_[:, j, :].rearrange("p (s t) -> p s t", t=2), zc[:, :, 8:10])
                        if ps_i == 0 and j == 0 and c0 == 0:
                            for step in s0_rest:
                                step()
                            s0_rest = []
                            wgroup(topup=True)
                            P.dma("pool", WPG[:, :, :, :], w_pool_group.rearrange("g (c p) o -> p g c o", p=128), "wpg")

            deferred = []

            def flush_deferred():
                while deferred:
                    deferred.pop(0)[1]()

            def s1b_chunk(j, jj, wu):
                g = j // 2
                w = POOL_W[g]
                prev_uc = None
                for (c0, n, kind) in tgs:
                    bu = bank()
                    for k in range(KC):
                        P.mm(PS[:, bu, 0:n], wu.lhsT(k, jj), hT[:, k, c0:c0 + n],
                             start=(k == 0), stop=(k == KC - 1))
                    pe_ = "pool" if (w > 2 and ((kind == "p" and (c0 // 512) % 2 == 0) or (kind == "s" and j % 2 == 0))) else "dve"
                    if kind == "p":
                        si = rot("pl", 3)
                        uc, pb_, qb_ = UC[si], PB, QB
                        if pe_ == "pool":
                            pb_, qb_ = PBP, QBP
                        sl = lambda buf, a, bb: buf[:, a:bb]
                        nn = n
                        if prev_uc is None:
                            P.copy("dve", uc[:, 0:15], UST[:, j, :])
                        else:
                            P.copy("dve", uc[:, 0:15], prev_uc[:, 512:527])
                        pu = PS[:, bu, 0:n]
                        outv = plT[:, j, c0:c0 + n]
                    else:
                        uc, pb_, qb_ = USX[j % 2], PSX, QSX
                        sl = lambda buf, a, bb: buf[:, :, a:bb]
                        nn = 8
                        pu = PS[:, bu, 0:n].rearrange("p (s t) -> p s t", t=8)
                        outv = plT[:, j, c0:c0 + n].rearrange("p (s t) -> p s t", t=8)
                        for h in range(2):
                            P.copy("dve" if h == 0 else "act", uc[:, h * 8:(h + 1) * 8, 0:15],
                                   SPT[h][:, j, :].rearrange("p (s t) -> p s t", t=15))
                    E = 15 + nn
                    if pe_ == "pool":
                        for d_ in [d for d in deferred if d[0] == kind]:
                            deferred.remove(d_)
                            d_[1]()
                    P.copy("act", sl(uc, 15, E), pu, name="u")
                    if w == 2:
                        P.tt(pe_, outv, sl(uc, 14, E - 1), sl(uc, 15, E), ALU.subtract, name="pool2")
                        sw = None
                    else:
                        P.tt(pe_, sl(pb_, 1, E), sl(uc, 1, E), sl(uc, 0, E - 1), ALU.add, name="s2")
                        P.tt(pe_, sl(qb_, 3, E), sl(pb_, 3, E), sl(pb_, 1, E - 2), ALU.add, name="s4")
                        sw = qb_
                        if w >= 8:
                            P.tt(pe_, sl(pb_, 7, E), sl(qb_, 7, E), sl(qb_, 3, E - 4), ALU.add, name="s8")
                            sw = pb_
                        if w >= 16:
                            P.tt(pe_, sl(qb_, 15, E), sl(pb_, 15, E), sl(pb_, 7, E - 8), ALU.add, name="s16")
                            sw = qb_
                    def fin(w=w, outv=outv, uc=uc, sw=sw, sl=sl, E=E, kind=kind, c0=c0, j=j, g=g):
                        if w > 2:
                            P.stt("dve", outv, sl(uc, 15, E), -float(w), sl(sw, 15, E), ALU.mult, ALU.add, name="pooled")
                        if kind == "p" and ps_i == 0 and c0 == 0:
                            if w == 2:
                                P.memset("dve", plT[:, j, 0:1], 0.0)
                            else:
                                nfix = w - 1
                                P.tt("dve", TMPC[:, 0:nfix], sw[:, 15:15 + nfix], FAC[:, g, 0:nfix], ALU.mult)
                                P.stt("dve", plT[:, j, 0:nfix], uc[:, 15:15 + nfix], -float(w), TMPC[:, 0:nfix],
                                      ALU.mult, ALU.add)
                    if kind == "p":
                        prev_uc = uc
                        if c0 + n == NPP:
                            P.copy("act", UST[:, j, :], uc[:, 512:527])
                    else:
                        P.copy("act", US_[:, j, :].rearrange("p (s t) -> p s t", t=8), uc[:, :, 15:23])
                    if pe_ == "pool":
                        deferred.append((kind, fin))
                    else:
                        fin()
                    while len(deferred) > (1 if pe_ == "pool" else 0):
                        deferred.pop(0)[1]()

            def emit_state_outputs():
                if last_pass:
                    b2 = bank2()
                    for j in range(KC):
                        P.tr(PS[0:2, b2 + j // 4, (j % 4) * 128:(j % 4 + 1) * 128], ZST[:, j, :], IDF[:, :])
                    pz = bass.AP(PS, b2 * 512, [[4096, 2], [1, D]])
                    P.copy("dve", STGS[0][0:2, :], pz)
                    P.dma("sp", ncp[:, :], STGS[0][0:2, :], "so0", is_output=True)
                    b2 = bank2()
                    for j in range(KC):
                        P.tr(PS[0:15, b2 + j // 4, (j % 4) * 128:(j % 4 + 1) * 128], UST[:, j, :], IDF[:, :])
                    pz = bass.AP(PS, b2 * 512, [[4096, 15], [1, D]])
                    P.copy("act", STGS[1][0:15, :], pz)
                    P.dma("sp", npp[:, :], STGS[1][0:15, :], "so1", is_output=True)
                    if has_s:
                        b2 = bank2()
                        for j in range(KC):
                            P.tr(PS[0:32, b2 + j // 4, (j % 4) * 128:(j % 4 + 1) * 128], ZS_[:, j, :], IDF[:, :])
                        pz = bass.AP(PS, b2 * 512, [[4096, 32], [1, D]])
                        P.copy("dve", STGS[2][0:32, :], pz)
                        P.dma("sp", ncs[:, :], STGS[2][0:32, :], "so2", is_output=True)
                        b2 = bank2()
                        for j in range(KC):
                            P.tr(PS[:, b2 + j // 4, (j % 4) * 128:(j % 4 + 1) * 128], US_[:, j, :], IDF[:, :])
                        pz = bass.AP(PS, b2 * 512, [[4096, 128], [1, D]])
                        P.copy("act", STGS[3][:, :], pz)
                        for s in range(NSEQ):
                            P.dma("sp", nps[s * 15 + 7:s * 15 + 15, :], STGS[3][s * 8:(s + 1) * 8, :], "so3", is_output=True)
                        P.dma("sp", nps.rearrange("(s r) d -> s r d", r=15)[:, 0:7, :],
                              st_pool.rearrange("(s r) d -> s r d", r=15)[:, 8:15, :], "so4", is_output=True)


            if has_s:
                S2T = [_view(AR, A_X1 + 5120, F32, [512]), _view(AR, A_X1 + 14080, F32, [512])]
            else:
                S2T = [tmv(14848, F32, [512]), tmv(16896, F32, [512])]

            def s2_chunk(m, jj, wgc, wgp, woc):
                flush_deferred()
                g = m // 2
                for (c0, n, kind) in tgs:
                    bA, bB, bg1, bg2 = bank(), bank(), bank(), bank()
                    for k in range(KC):
                        P.mm(PS[:, bg1, 0:n], wgc.lhsT(k, jj), hT[:, k, c0:c0 + n],
                             start=(k == 0), stop=(k == KC - 1))
                    for k in range(KC):
                        P.mm(PS[:, bg2, 0:n], wgp.lhsT(k, jj), hT[:, k, c0:c0 + n],
                             start=(k == 0), stop=(k == KC - 1))
                    for k in range(KC):
                        P.mm(PS[:, bA, 0:n], woc.lhsT(k, jj), ApT[:, k, c0:c0 + n],
                             start=(k == 0), stop=(k == KC - 1))
                    for c in range(2):
                        P.mm(PS[:, bB, 0:n], WPG[:, g, c, (m % 2) * 128:(m % 2 + 1) * 128],
                             plT[:, 2 * g + c, c0:c0 + n], start=(c == 0), stop=(c == 1))
                    s1, s2 = S2T[0][:, 0:n], S2T[1][:, 0:n]
                    t1, t2 = s1, s2
                    P.act(s1, PS[:, bg1, 0:n], AF.Sigmoid, bias=cs(C_BG + m), name="sig1")
                    P.act(s2, PS[:, bg2, 0:n], AF.Sigmoid, bias=cs(C_BG + 8 + m), name="sig2")
                    P.tt("dve", t1, PS[:, bA, 0:n], s1, ALU.mult, name="mA")
                    P.stt("dve", t2, PS[:, bB, 0:n], cs(C_PS + m), s2, ALU.mult, ALU.mult, name="mB")
                    P.tt("dve", mgT[:, m, c0:c0 + n], t1, t2, ALU.add, name="merge")

            for b in range(2):
                wu, wgc, wgp, woc = wgroup([(w_in, 0, 3 * D + b * 512), (w_in, 0, 4 * D + b * 512),
                                            (w_in, 0, 5 * D + b * 512), (w_out_conv, 0, b * 512)])
                c_ = b * 4
                s1b_chunk(c_ + 0, 0, wu)
                s1b_chunk(c_ + 1, 1, wu)
                s2_chunk(c_ + 0, 0, wgc, wgp, woc)
                s1b_chunk(c_ + 2, 2, wu)
                s2_chunk(c_ + 1, 1, wgc, wgp, woc)
                s1b_chunk(c_ + 3, 3, wu)
                s2_chunk(c_ + 2, 2, wgc, wgp, woc)
                if b == 1:
                    emit_state_outputs()
                s2_chunk(c_ + 3, 3, wgc, wgp, woc)
            flush_deferred()

            wo = wgroup([(w_o, 0, 0), (w_o, 0, 512)])
            s3b, s3x = {}, {}

            def s3_A(i):
                b2 = bank2()
                for nh in range(2):
                    for k in range(KC):
                        P.mm(PS[:, b2 + nh, :], mgT[:, k, i * 128:(i + 1) * 128], wo[nh].full(k),
                             start=(k == 0), stop=(k == KC - 1))
                s3b[i] = b2

            def s3_B1(i):
                pm = bass.AP(PS, s3b.pop(i) * 512, [[4096, 128], [1, D]])
                ss = stat()
                P.act(JUNK, pm, AF.Square, accum_out=ss, name="sq_mix")
                rstd_from(ss)
                xs_ = rot("xt", 2)
                xt = XT[xs_]
                if ps_i == 0:
                    P.stt("dve", xt, pm, ss, GBC[:, 0, :], ALU.mult, ALU.mult, name="mixn")
                else:
                    P.dma("sp", xt, tiles[i][0], "xt%d" % xs_, name="xreload")
                    P.stt("dve", x1[:, i, :], pm, ss, GBC[:, 0, :], ALU.mult, ALU.mult, name="mixn")
                P.tt("pool", x1[:, i, :], x1[:, i, :], xt, ALU.add, name="x1")

            def s3_B2(i):
                s3x[i] = norm_prep(x1[:, i, :])

            def s3_C(i):
                norm_tr(s3x.pop(i), h2T, i * 128, C_GPL)

            s4_early = {"w": None, "done": set()}

            def s4_unit(wu_, f, jj, c0, n):
                bu = bank()
                for k in range(KC):
                    P.mm(PS[:, bu, 0:n], wu_.lhsT(k, jj), h2T[:, k, c0:c0 + n],
                         start=(k == 0), stop=(k == KC - 1))
                rl = RL[rot("rl", 3)][:, 0:n]
                P.act(rl, PS[:, bu, 0:n], AF.Relu, name="relu")
                P.tt("dve", actT[:, f, c0:c0 + n], rl, rl, ALU.mult, name="sqr")

            for u in range(NT + 3):
                if u < NT:
                    s3_A(u)
                if 1 <= u <= NT:
                    s3_B1(u - 1)
                if 2 <= u <= NT + 1:
                    s3_B2(u - 2)
                if u >= 3:
                    s3_C(u - 3)
                u0_ = max(7, NT)
                if u >= u0_:
                    if s4_early["w"] is None:
                        s4_early["w"] = wgroup([(w_up, 0, 0), (w_up, 0, 512)])
                    todo_ = [(bl_, jj_) for bl_ in range(2) for jj_ in range(4)
                             if (bl_ * 4 + jj_, tgs[0][0]) not in s4_early["done"]]
                    if u < NT + 2:
                        todo_ = todo_[:2]
                    for (bl_, jj_) in todo_:
                        s4_unit(s4_early["w"][bl_], bl_ * 4 + jj_, jj_, tgs[0][0], tgs[0][1])
                        s4_early["done"].add((bl_ * 4 + jj_, tgs[0][0]))

            if s4_early["w"] is None:
                s4_early["w"] = wgroup([(w_up, 0, 0), (w_up, 0, 512)])
            for blk in range(DFF // 512):
                if blk < 2:
                    wu_ = s4_early["w"][blk]
                else:
                    wu_, = wgroup([(w_up, 0, blk * 512)])
                for jj in range(4):
                    f = blk * 4 + jj
                    for (c0, n, kind) in tgs:
                        if (f, c0) in s4_early["done"]:
                            continue
                        s4_unit(wu_, f, jj, c0, n)

            groups = [list(range(a, min(a + 3, NT))) for a in range(0, NT, 3)]
            if not last_pass:
                pending_s0 = s0_steps(passes[ps_i + 1])
            unit_i = 0
            bank_pool[0] = [6, 7]
            for G in groups:
                bk = {}
                for gi_, i in enumerate(G):
                    bk[i] = 2 * gi_
                od_ = s5_order(groups.index(G))
                for (nh, kg) in od_:
                    kgs_ = [kg2 for (nh2, kg2) in od_ if nh2 == nh]
                    first_k, last_k = (kg == kgs_[0]), (kg == kgs_[-1])
                    wd, = wgroup([(w_down, kg * 1024, nh * 512)])
                    if unit_i >= 1 and pending_s0:
                        pending_s0.pop(0)()
                    unit_i += 1
                    for i in G:
                        for kk in range(KC):
                            P.mm(PS[:, bk[i] + nh, :], actT[:, kg * 8 + kk, i * 128:(i + 1) * 128], wd.full(kk),
                                 start=(first_k and kk == 0), stop=(last_k and kk == KC - 1),
                                 signal=((last_k or i == G[-1]) and kk == KC - 1))
                for i in G:
                    pm = bass.AP(PS, bk[i] * 512, [[4096, 128], [1, D]])
                    ss = stat()
                    P.act(JUNK, pm, AF.Square, accum_out=ss, name="sq_mlp")
                    rstd_from(ss)
                    t4 = XT[rot("xt", 2)]
                    P.stt("dve", t4, pm, ss, GBC[:, 1, :], ALU.mult, ALU.mult, name="mlpn")
                    P.tt("dve", x1[:, i, :], x1[:, i, :], t4, ALU.add, name="y")
                    P.dma("sp", tiles[i][1], x1[:, i, :], "yo%d" % i, name="ystore", is_output=True)
            for step in pending_s0:
                step()
            pending_s0 = []
            bank_pool[0] = None

        nwait = P.emit(stack)
    return nc


_NC_CACHE = {}


def kernel(**inputs):
    f = lambda a: np.ascontiguousarray(np.asarray(a, dtype=np.float32))
    x_prompt = f(inputs["x_prompt"])
    x_sample = f(inputs["x_sample"])
    state_conv = f(inputs["state_conv"])
    state_pool = f(inputs["state_pool"])
    shared = {
        "g_pre_mix": f(inputs["g_pre_mix"]).reshape(D),
        "w_in": f(inputs["w_in"]).reshape(D, 6 * D),
        "b_gate": f(inputs["b_gate"]).reshape(2 * D),
        "w_conv": f(inputs["w_conv"]).reshape(3, D),
        "w_out_conv": f(inputs["w_out_conv"]).reshape(D, D),
        "w_pool_group": f(inputs["w_pool_group"]).reshape(4, 256, 256),
        "pool_scale": f(inputs["pool_scale"]).reshape(D),
        "w_o": f(inputs["w_o"]).reshape(D, D),
        "g_post_mix": f(inputs["g_post_mix"]).reshape(D),
        "g_pre_mlp": f(inputs["g_pre_mlp"]).reshape(D),
        "w_up": f(inputs["w_up"]).reshape(D, DFF),
        "w_down": f(inputs["w_down"]).reshape(DFF, D),
        "g_post_mlp": f(inputs["g_post_mlp"]).reshape(D),
    }
    in_maps = []
    for c in range(NCORES):
        m = dict(shared)
        m["x_p"] = x_prompt[c]
        m["x_s"] = x_sample[16 * c:16 * (c + 1)].reshape(128, D)
        m["st_conv"] = state_conv[0, 16 * c:16 * (c + 1)].reshape(32, D)
        m["st_pool"] = state_pool[0, 16 * c:16 * (c + 1)].reshape(240, D)
        in_maps.append(m)
    if "nc" not in _NC_CACHE:
        _NC_CACHE["nc"] = build_program()
    nc = _NC_CACHE["nc"]
    res = run_bass_kernel_spmd(nc, in_maps, core_ids=list(range(NCORES)))
    R = res.results
    y_prompt = np.stack([R[c]["y_p"] for c in range(NCORES)], axis=0)
    y_sample = np.concatenate([R[c]["y_s"].reshape(16, 8, D) for c in range(NCORES)], axis=0)
    ncp = np.stack([R[c]["ncp"] for c in range(NCORES)], axis=0)[None]
    npp = np.stack([R[c]["npp"] for c in range(NCORES)], axis=0)[None]
    ncs = np.concatenate([R[c]["ncs"].reshape(16, 2, D) for c in range(NCORES)], axis=0)[None]
    nps = np.concatenate([R[c]["nps"].reshape(16, 15, D) for c in range(NCORES)], axis=0)[None]
    return (y_prompt.astype(np.float32), y_sample.astype(np.float32), ncp.astype(np.float32),
            npp.astype(np.float32), ncs.astype(np.float32), nps.astype(np.float32))
```

```python
import numpy as np
from contextlib import ExitStack
import concourse.bass as bass
import concourse.mybir as mybir
from concourse.bass_utils import run_bass_kernel_spmd

F32 = mybir.dt.float32
BF16 = mybir.dt.bfloat16
ALU = mybir.AluOpType
AF = mybir.ActivationFunctionType
SZ = {F32: 4, BF16: 2}

D = 1024
KC = 8
DFF = 4096
EPS = 1e-6
NCORES = 8
POOL_W = (2, 4, 8, 16)


def _prod(xs):
    r = 1
    for x in xs:
        r *= int(x)
    return r


class _Op:
    __slots__ = ("eng", "fn", "deps", "signal", "pos", "dma", "name", "sig", "wait_val")

    def __init__(self, eng, fn, name):
        self.eng = eng
        self.fn = fn
        self.deps = set()
        self.signal = True
        self.pos = None
        self.dma = None
        self.name = name
        self.sig = None
        self.wait_val = None


class Prog:
    ENG = ("pe", "act", "dve", "pool", "sp")

    def __init__(self, nc):
        self.nc = nc
        self.engs = {"pe": nc.tensor, "act": nc.scalar, "dve": nc.vector,
                     "pool": nc.gpsimd, "sp": nc.sync}
        self.streams = {e: [] for e in self.ENG}
        self.recs = {}
        self.dma_cnt = {}
        self.out_dmas = []

    @staticmethod
    def _iv(ap):
        t = ap.tensor
        if type(t).__name__.startswith("DRam"):
            return None
        esz = SZ[ap.dtype]
        pbytes = _prod(t.shape[1:]) * SZ[t.dtype]
        off = int(ap.offset) * esz
        lo = off % pbytes
        dims = list(ap.ap)
        p0 = off // pbytes
        p1 = p0 + ((int(dims[0][1]) - 1) * abs(int(dims[0][0])) * esz) // pbytes + 1
        fd = sorted([(abs(int(st)) * esz, int(cnt)) for (st, cnt) in dims[1:] if int(cnt) > 1], key=lambda d: -d[0])

        def rec(base, ds):
            if not ds:
                return [(base, base + esz)]
            inner = esz + sum((c - 1) * s_ for s_, c in ds[1:])
            s0, c0 = ds[0]
            if s0 >= 2 * inner and c0 <= 32:
                out = []
                for i in range(c0):
                    out += rec(base + i * s0, ds[1:])
                return out
            return [(base, base + esz + sum((c - 1) * s_ for s_, c in ds))]
        ivs = rec(lo, fd)
        if len(ivs) > 64:
            ivs = [(lo, lo + esz + sum((c - 1) * s_ for s_, c in fd))]
        return t.name, ivs, p0, p1

    def _acc(self, op, ap, is_write):
        iv = self._iv(ap)
        if iv is None:
            return
        key, ivs, p0, p1 = iv
        for (lo, hi) in ivs:
            self._acc1(op, key, lo, hi, p0, p1, is_write)

    def _acc1(self, op, key, lo, hi, p0, p1, is_write):
        recs = self.recs.get(key, [])
        new = []
        for r in recs:
            rlo, rhi, rop, rw, rp0, rp1 = r
            if rhi <= lo or rlo >= hi or rp1 <= p0 or rp0 >= p1:
                new.append(r)
                continue
            if (is_write or rw) and rop is not op:
                op.deps.add(rop)
            cov = rlo >= lo and rhi <= hi and rp0 >= p0 and rp1 <= p1
            if is_write and cov:
                continue
            if (not is_write) and (not rw) and rop.dma is None and op.dma is None \
                    and rop.eng == op.eng and cov:
                continue
            new.append(r)
        new.append((lo, hi, op, is_write, p0, p1))
        self.recs[key] = new

    def add(self, eng, fn, reads=(), writes=(), signal=True, name="", dma=None):
        op = _Op(eng, fn, name)
        op.signal = signal
        op.dma = dma
        for ap in reads:
            if ap is not None and not isinstance(ap, (int, float)):
                self._acc(op, ap, False)
        for ap in writes:
            self._acc(op, ap, True)
        op.pos = len(self.streams[eng])
        self.streams[eng].append(op)
        return op

    def dma(self, q, out, in_, semkey, name="", is_output=False, deps=(), **kw):
        cnt = self.dma_cnt.get(semkey, 0) + 16
        self.dma_cnt[semkey] = cnt
        op = self.add(q, lambda e: e.dma_start(out=out, in_=in_, **kw), reads=[in_], writes=[out],
                      name=name, dma=(semkey, cnt))
        op.deps.update(deps)
        if is_output:
            self.out_dmas.append(op)
        return op

    def mm(self, out, lhsT, rhs, start, stop, signal=None):
        sig = stop if signal is None else signal
        return self.add("pe", lambda e: e.matmul(out, lhsT, rhs, start=start, stop=stop),
                        reads=[lhsT, rhs], writes=[out], signal=sig, name="mm")

    def tr(self, out, in_, ident, signal=True):
        return self.add("pe", lambda e: e.transpose(out, in_, ident), reads=[in_, ident], writes=[out],
                        signal=signal, name="tr")

    def act(self, out, in_, func, bias=None, scale=None, accum_out=None, name="act"):
        kw = {}
        if bias is not None:
            kw["bias"] = bias
        if scale is not None:
            kw["scale"] = scale
        if accum_out is not None:
            kw["accum_out"] = accum_out
        rd = [in_] + [x for x in (bias, scale) if x is not None and not isinstance(x, (int, float))]
        wr = [out] + ([accum_out] if accum_out is not None else [])
        return self.add("act", lambda e: e.activation(out, in_, func, **kw), reads=rd, writes=wr, name=name)

    def tt(self, eng, out, in0, in1, op, name="tt"):
        return self.add(eng, lambda e: e.tensor_tensor(out, in0, in1, op), reads=[in0, in1], writes=[out], name=name)

    def ts(self, eng, out, in0, s1, s2, op0, op1=None, name="ts"):
        rd = [in0] + [x for x in (s1, s2) if x is not None and not isinstance(x, (int, float))]
        if op1 is None:
            return self.add(eng, lambda e: e.tensor_scalar(out, in0, s1, s2, op0), reads=rd, writes=[out], name=name)
        return self.add(eng, lambda e: e.tensor_scalar(out, in0, s1, s2, op0, op1), reads=rd, writes=[out], name=name)

    def stt(self, eng, out, in0, scalar, in1, op0, op1, name="stt"):
        rd = [in0, in1] + ([scalar] if not isinstance(scalar, (int, float)) else [])
        return self.add(eng, lambda e: e.scalar_tensor_tensor(out, in0, scalar, in1, op0, op1),
                        reads=rd, writes=[out], name=name)

    def copy(self, eng, out, in_, name="copy"):
        if eng == "act":
            return self.act(out, in_, AF.Identity, name=name)
        return self.add(eng, lambda e: e.tensor_copy(out, in_), reads=[in_], writes=[out], name=name)

    def memset(self, eng, out, val, name="memset"):
        return self.add(eng, lambda e: e.memset(out, val), writes=[out], name=name)

    def emit(self, stack):
        nc = self.nc
        sems = {}
        for e in ("pe", "act", "dve", "pool"):
            sems[e] = stack.enter_context(nc.semaphore("s_" + e))
        for k in self.dma_cnt:
            sems[("dma", k)] = stack.enter_context(nc.semaphore("d_" + str(k)))
        fin = _Op("sp", None, "final")
        fin.deps = set(self.out_dmas)
        self.streams["sp"].append(fin)
        for e in self.ENG:
            c = 0
            for op in self.streams[e]:
                if op.dma is None and op.fn is not None and op.signal:
                    c += 1
                    op.sig = c
            nxt = None
            for op in reversed(self.streams[e]):
                if op.dma is None and op.fn is not None:
                    if op.signal:
                        nxt = op.sig
                    op.wait_val = nxt
        nwait = 0
        for e in self.ENG:
            eng = self.engs[e]
            waited = {}
            for op in self.streams[e]:
                need = {}
                for d in op.deps:
                    if d.dma is not None:
                        key = ("dma", d.dma[0])
                        val = d.dma[1]
                    else:
                        if d.eng == "pe" and e == "pe":
                            continue
                        key = d.eng
                        val = d.wait_val
                        assert val is not None, (d.name, op.name)
                    if waited.get(key, 0) >= val:
                        continue
                    if need.get(key, 0) < val:
                        need[key] = val
                for key, val in need.items():
                    waited[key] = val
                    eng.wait_ge(sems[key], val)
                    nwait += 1
                if op.fn is not None:
                    ins = op.fn(eng)
                    if op.dma is not None:
                        ins.then_inc(sems[("dma", op.dma[0])], 16)
                    elif op.signal:
                        ins.then_inc(sems[e], 1)
        return nwait


def _view(t, boff, dt, shape):
    n = _prod(shape)
    esz = SZ[dt]
    assert boff % 4 == 0
    e0 = boff // 2
    e1 = e0 + n * esz // 2
    ap = t[:, e0:e1]
    if dt != BF16:
        ap = ap.bitcast(dt)
    if len(shape) == 2:
        ap = ap.rearrange("p (a b) -> p a b", b=shape[1])
    elif len(shape) == 3:
        ap = ap.rearrange("p (a b c) -> p a b c", b=shape[1], c=shape[2])
    return ap


def build_program(ntg=2, with_sample=True, npass=2):
    nc = bass.Bass("TRN2", target_bir_lowering=False)
    P = Prog(nc)
    NPP = 512 * ntg
    NPT = NPP * npass
    NSEQ = 16
    NST = NSEQ * 8
    TMAX = NPP + (NST if with_sample else 0)

    def din(name, shape):
        return nc.dram_tensor(name, list(shape), F32, kind="ExternalInput").ap()

    def dout(name, shape):
        return nc.dram_tensor(name, list(shape), F32, kind="ExternalOutput").ap()

    x_p = din("x_p", [NPT, D])
    x_s = din("x_s", [NST, D])
    st_conv = din("st_conv", [NSEQ * 2, D])
    st_pool = din("st_pool", [NSEQ * 15, D])
    g_pre_mix = din("g_pre_mix", [D])
    w_in = din("w_in", [D, 6 * D])
    b_gate = din("b_gate", [2 * D])
    w_conv = din("w_conv", [3, D])
    w_out_conv = din("w_out_conv", [D, D])
    w_pool_group = din("w_pool_group", [4, 256, 256])
    pool_scale = din("pool_scale", [D])
    w_o = din("w_o", [D, D])
    g_post_mix = din("g_post_mix", [D])
    g_pre_mlp = din("g_pre_mlp", [D])
    w_up = din("w_up", [D, DFF])
    w_down = din("w_down", [DFF, D])
    g_post_mlp = din("g_post_mlp", [D])

    y_p = dout("y_p", [NPT, D])
    y_s = dout("y_s", [NST, D])
    ncp = dout("ncp", [2, D])
    npp = dout("npp", [15, D])
    ncs = dout("ncs", [NSEQ * 2, D])
    nps = dout("nps", [NSEQ * 15, D])

    wd_bf = nc.dram_tensor("wd_bf", [8, 128, KC * 512], BF16, kind="Internal").ap()

    stack = ExitStack()
    with stack:
        NT_MAX = TMAX // 128
        A_ACT = 0
        A_HT = 0
        A_AP = KC * TMAX * 2
        A_PL = 2 * KC * TMAX * 2
        A_MG = 3 * KC * TMAX * 2
        A_X1 = 32 * TMAX * 2
        A_H2 = A_X1 + NT_MAX * 4096
        A_END = A_H2 + KC * TMAX * 2
        AR = stack.enter_context(nc.sbuf_tensor("AR", [128, A_END // 2], BF16))
        NSLOT = 6
        WS = stack.enter_context(nc.sbuf_tensor("WS", [128, NSLOT * 4096], BF16))
        WPG = stack.enter_context(nc.sbuf_tensor("WPG", [128, 4, 2, 256], BF16))
        TM_BYTES = 20096
        TM = stack.enter_context(nc.sbuf_tensor("TM", [128, TM_BYTES // 2], BF16))
        GBC = stack.enter_context(nc.sbuf_tensor("GBC", [128, 2, D], F32))
        IDB = stack.enter_context(nc.sbuf_tensor("IDB", [128, 128], BF16))
        IDF = stack.enter_context(nc.sbuf_tensor("IDF", [128, 128], F32))
        CS = stack.enter_context(nc.sbuf_tensor("CS", [128, 96], F32))
        STT_ = stack.enter_context(nc.sbuf_tensor("STAT", [128, 64], F32))
        ZST = stack.enter_context(nc.sbuf_tensor("ZST", [128, KC, 2], F32))
        UST = stack.enter_context(nc.sbuf_tensor("UST", [128, KC, 15], F32))
        FAC = stack.enter_context(nc.sbuf_tensor("FAC", [128, 5, 16], F32))
        PS = stack.enter_context(nc.psum_tensor("PS", [128, 8, 512], F32))

        hT = _view(AR, A_H2, BF16, [KC, TMAX])
        ApT = _view(AR, A_AP, BF16, [KC, TMAX])
        plT = _view(AR, A_PL, BF16, [KC, TMAX])
        mgT = _view(AR, A_MG, BF16, [KC, TMAX])
        actT = _view(AR, A_ACT, BF16, [32, TMAX])
        x1 = _view(AR, A_X1, F32, [NT_MAX, D])
        h2T = _view(AR, A_H2, BF16, [KC, TMAX])
        US_ = _view(AR, A_X1, F32, [KC, 128])
        ZS_ = _view(AR, A_X1 + KC * 128 * 4, F32, [KC, 32])

        C_GPM = 0
        C_GPL = 8
        C_BG = 16
        C_WC = 32
        C_PS = 56
        C_EPS = 64

        def cs(col, n=1):
            return CS[:, col:col + n]

        stat_ctr = [0]

        def stat():
            i = stat_ctr[0] % 64
            stat_ctr[0] += 1
            return STT_[:, i:i + 1]

        bank_ctr = [0]
        bank_pool = [None]

        def bank():
            if bank_pool[0] is not None:
                pool_ = bank_pool[0]
                b = pool_[bank_ctr[0] % len(pool_)]
                bank_ctr[0] += 1
                return b
            b = bank_ctr[0] % 8
            bank_ctr[0] += 1
            return b

        def bank2():
            if bank_ctr[0] % 2:
                bank_ctr[0] += 1
            b = bank_ctr[0] % 8
            bank_ctr[0] += 2
            return b

        slot_ctr = [0]

        def wslot():
            s = slot_ctr[0] % NSLOT
            slot_ctr[0] += 1
            return s

        def s5_order(gi_):
            od = [(nh, kg) for nh in range(2) for kg in range(4)]
            return od if gi_ % 2 == 0 else od[::-1]

        plan = []
        for ps_i in range(npass):
            nt_ = NPP // 128 + (1 if (with_sample and ps_i == npass - 1) else 0)
            for b in range(2):
                plan.append([(w_in, 0, 0 * D + b * 512), (w_in, 0, 1 * D + b * 512), (w_in, 0, 2 * D + b * 512)])
            for b in range(2):
                plan.append([(w_in, 0, 3 * D + b * 512), (w_in, 0, 4 * D + b * 512), (w_in, 0, 5 * D + b * 512),
                             (w_out_conv, 0, b * 512)])
            plan.append([(w_o, 0, 0), (w_o, 0, 512)])
            plan.append([(w_up, 0, 0), (w_up, 0, 512)])
            for blk in range(2, DFF // 512):
                plan.append([(w_up, 0, blk * 512)])
            for gi_, a in enumerate(range(0, nt_, 3)):
                for (nh, kg) in s5_order(gi_):
                    plan.append([(w_down, kg * 1024, nh * 512)])
        entries = [e for g in plan for e in g]
        gfirst = []
        gof = []
        acc_ = 0
        for gi_, g in enumerate(plan):
            gfirst.append(acc_)
            acc_ += len(g)
            gof += [gi_] * len(g)
        keyof = [(id(W_), r_, c_) for (W_, r_, c_) in entries]
        slot_of, need_load, free_after = [], [], []
        content = [None] * NSLOT
        last_use = [-1] * NSLOT
        for e, k_ in enumerate(keyof):
            if k_ in content and e >= 3 and k_[0] == id(w_down):
                sl_ = content.index(k_)
                slot_of.append(sl_)
                need_load.append(False)
                free_after.append(-1)
            else:
                same_grp = [slot_of[e2] for e2 in range(gfirst[gof[e]], e)]
                best, best_key = None, None
                for sl_ in range(NSLOT):
                    if sl_ in same_grp:
                        continue
                    if content[sl_] is None or content[sl_][0] != id(w_down):
                        nu = 10 ** 9
                    else:
                        nu = 10 ** 9
                        for e2 in range(e + 1, min(len(keyof), e + 24)):
                            if keyof[e2] == content[sl_]:
                                nu = e2
                                break
                    cand = (nu, -last_use[sl_])
                    if best_key is None or cand > best_key:
                        best, best_key = sl_, cand
                slot_of.append(best)
                need_load.append(True)
                free_after.append(gof[last_use[best]] if last_use[best] >= 0 else -1)
                content[best] = k_
            last_use[slot_of[e]] = e
        wstate = {"g": 0, "issued": 0, "conv": 0}
        conv_ops = []

        class WV:
            def __init__(self, e, split):
                self.split = split
                if split:
                    self.v = _view(WS, slot_of[e] * 8192, BF16, [4, KC, 128])
                else:
                    self.v = _view(WS, slot_of[e] * 8192, BF16, [KC, 512])

            def lhsT(self, k, jj):
                return self.v[:, jj, k, :] if self.split else self.v[:, k, jj * 128:(jj + 1) * 128]

            def full(self, k):
                assert not self.split
                return self.v[:, k, :]

        def wslot_view(e):
            return WV(e, e < 3)

        def wgroup(expect=None, topup=False):
            if topup:
                wstate["g"] -= 1
            g = wstate["g"]
            wstate["g"] += 1
            first, cnt = gfirst[g], len(plan[g])
            if expect is not None:
                assert [(a is b_, r, c) for (a, r, c), (b_, r2, c2) in zip(plan[g], expect)] and \
                    all(a is b_ and r == r2 and c == c2 for (a, r, c), (b_, r2, c2) in zip(plan[g], expect)), g
            if 1 <= g <= 4 and not topup:
                for c in (2 * (g - 1), 2 * (g - 1) + 1):
                    kg_, nh_ = c // 2, c % 2
                    conv_ops.append(P.dma("pool", wd_bf[c, :, :].rearrange("p (k c) -> p k c", c=512),
                                          w_down[kg_ * 1024:(kg_ + 1) * 1024, nh_ * 512:(nh_ + 1) * 512].rearrange(
                                              "(k p) c -> p k c", p=128),
                                          "wdc", name="wdconv"))
                    if c == 7:
                        for op_ in conv_ops:
                            op_.dma = ("wdc", P.dma_cnt["wdc"])
            while wstate["issued"] < len(entries):
                e = wstate["issued"]
                own = e < first + cnt
                ahead_ok = (free_after[e] < g) and not (g == 0 and not topup)
                if not (own or ahead_ok):
                    break
                if own:
                    assert free_after[e] < g
                W, r0, c0 = entries[e]
                if not need_load[e]:
                    pass
                elif e < 3:
                    if e == 0:
                        for jj_ in range(4):
                            for e_ in range(3):
                                W_, r_0, c_0 = entries[e_]
                                src = W_[r_0:r_0 + KC * 128, c_0 + jj_ * 128:c_0 + (jj_ + 1) * 128].rearrange(
                                    "(k p) c -> p k c", p=128)
                                P.dma("pool", wslot_view(e_).v[:, jj_, :, :], src, "wi%d_%d" % (e_, jj_), name="wload0",
                                      deps=([xload0[min(3, len(xload0) - 1)]] if (jj_ > 0 and xload0) else []))
                elif W is w_down:
                    src = wd_bf[(r0 // 1024) * 2 + (c0 // 512), :, :].rearrange("p (k c) -> p k c", c=512)
                    P.dma("pool", wslot_view(e).v, src, "w%d" % slot_of[e], name="wload_bf", deps=conv_ops)
                else:
                    src = W[r0:r0 + KC * 128, c0:c0 + 512].rearrange("(k p) c -> p k c", p=128)
                    P.dma("pool", wslot_view(e).v, src, "w%d" % slot_of[e], name="wload")
                wstate["issued"] += 1
            return [wslot_view(first + i) for i in range(cnt)]

        CST = _view(TM, 8192, F32, [128])
        r_ = lambda v: v.rearrange("(k p) -> k p", p=128)
        P.dma("act", CST[0:8, :], r_(g_pre_mix), "const")
        P.dma("act", CST[8:16, :], r_(g_pre_mlp), "const")
        P.dma("act", CST[16:32, :], r_(b_gate), "const")
        P.dma("act", CST[32:56, :], w_conv.rearrange("w (k p) -> (w k) p", p=128), "const")
        P.dma("act", CST[56:64, :], r_(pool_scale), "const")
        tot = P.dma_cnt["const"]
        for op in P.streams["act"]:
            if op.dma is not None and op.dma[0] == "const":
                op.dma = ("const", tot)

        P.memset("pool", IDF[:, :], 0.0)
        P.add("pool", lambda e: e.affine_select(out=IDF[:, :], in_=IDF[:, :], compare_op=ALU.not_equal,
                                                 fill=1.0, base=0, pattern=[[-1, 128]], channel_multiplier=1),
              reads=[IDF[:, :]], writes=[IDF[:, :]], name="ident")
        P.copy("dve", IDB[:, :], IDF[:, :])
        P.act(STT_[:, 63:64], IDF[:, 0:1], AF.Square, name="act_table_warm")
        bt0 = 0
        P.tr(PS[:, bt0, 0:64], CST[0:64, :], IDF[0:64, 0:64])
        P.copy("dve", CS[:, 0:64], PS[:, bt0, 0:64])
        P.add("pool", lambda e: e.iota(FAC[:, 4, :], pattern=[[1, 16]], base=1, channel_multiplier=0,
                                       allow_small_or_imprecise_dtypes=True),
              writes=[FAC[:, 4, :]], name="iota")
        for g, w in enumerate(POOL_W):
            P.ts("dve", FAC[:, g, :], FAC[:, 4, :], float(w), None, ALU.min)
            P.add("dve", lambda e, g=g: e.reciprocal(FAC[:, g, :], FAC[:, g, :]),
                  reads=[FAC[:, g, :]], writes=[FAC[:, g, :]], name="recip")
            P.ts("dve", FAC[:, g, :], FAC[:, g, :], float(w), None, ALU.mult)
            P.ts("dve", cs(C_PS + 2 * g, 2), cs(C_PS + 2 * g, 2), 1.0 / w, None, ALU.mult)
        P.memset("dve", cs(C_EPS), EPS)
        P.memset("dve", ZST[:, :, :], 0.0)
        P.memset("dve", UST[:, :, :], 0.0)

        def tmv(off, dt, shape):
            return _view(TM, off, dt, shape)

        XT = [tmv(0 + i * 4096, F32, [D]) for i in range(2)]
        XN = [tmv(8192 + i * 2048, BF16, [D]) for i in range(2)]
        JUNK = tmv(12288, BF16, [D])
        XV = [tmv(0 + i * 2048, F32, [512]) for i in range(2)]
        ZC = [tmv(4096 + i * 2064, F32, [514]) for i in range(2)]
        TCV = [tmv(8224 + i * 2048, F32, [512]) for i in range(2)]
        ZSX = [tmv(12320 + i * 640, F32, [16, 10]) for i in range(2)]
        UC = [tmv(i * 2112, F32, [527]) for i in range(3)]
        PB = tmv(6336, F32, [527])
        QB = tmv(8448, F32, [527])
        PBP = tmv(10560, F32, [527])
        QBP = tmv(12672, F32, [527])
        TMPC = tmv(14784, F32, [16])
        USX = [_view(AR, A_X1 + 2 * 4096 + i * 1472, F32, [16, 23]) for i in range(2)]
        PSX = _view(AR, A_X1 + 2 * 4096 + 2944, F32, [16, 23])
        QSX = _view(AR, A_X1 + 2 * 4096 + 4416, F32, [16, 23])
        SPT = [_view(AR, A_X1 + (NT_MAX - 1) * 4096, F32, [KC, 120]), tmv(14848, F32, [KC, 120])]
        SCT = tmv(18688, F32, [KC, 32])
        XNF = tmv(8192, F32, [D])
        SG1 = [tmv(0 + i * 2048, F32, [512]) for i in range(2)]
        SG2 = [tmv(4096 + i * 2048, F32, [512]) for i in range(2)]
        MT1 = [tmv(8192 + i * 2048, F32, [512]) for i in range(2)]
        MT2 = [tmv(12288 + i * 2048, F32, [512]) for i in range(2)]
        RL = [tmv(i * 2048, F32, [512]) for i in range(3)]
        if NT_MAX >= 9:
            STGS = [_view(AR, A_X1 + (4 + i) * 4096, F32, [D]) for i in range(4)]
        else:
            STGS = [tmv(i * 4096, F32, [D]) for i in range(4)]

        ctr = {"xt": 0, "xn": 0, "t4k": 0, "cv": 0, "pl": 0, "sg": 0, "rl": 0, "sx": 0}

        def rot(key, n):
            i = ctr[key] % n
            ctr[key] += 1
            return i

        def rstd_from(ss):
            P.act(ss, ss, AF.Sqrt, bias=cs(C_EPS), scale=1.0 / D, name="sqrt")
            P.add("dve", lambda e: e.reciprocal(ss, ss), reads=[ss], writes=[ss], name="rstd")
            return ss

        def make_pass(ps_i):
            last_pass = ps_i == npass - 1
            has_s = with_sample and last_pass
            tiles = []
            for i in range(NPP // 128):
                r0 = ps_i * NPP + i * 128
                tiles.append((x_p[r0:r0 + 128, :], y_p[r0:r0 + 128, :]))
            if has_s:
                tiles.append((x_s[:, :], y_s[:, :]))
            tgs = [(g * 512, 512, "p") for g in range(ntg)]
            if has_s:
                tgs.append((NPP, 128, "s"))
            return dict(idx=ps_i, last_pass=last_pass, has_s=has_s, tiles=tiles, tgs=tgs, NT=len(tiles))

        passes = [make_pass(i) for i in range(npass)]

        def norm_prep(src_tok, xn_eng="act"):
            ss = stat()
            P.act(JUNK, src_tok, AF.Square, accum_out=ss, name="sq")
            rstd_from(ss)
            xn = XN[rot("xn", 2)]
            if xn_eng == "act":
                P.act(xn, src_tok, AF.Identity, scale=ss, name="xn")
            else:
                P.ts("dve", xn, src_tok, ss, None, ALU.mult, name="xn")
            return xn

        def norm_tr(xn, dstT, col0, gcol):
            b = bank()
            pb = PS[:, b, :].bitcast(BF16)
            for k in range(KC):
                P.tr(pb[:, k * 128:(k + 1) * 128], xn[:, k * 128:(k + 1) * 128], IDB[:, :], signal=(k == KC - 1))
            gb = bass.AP(CS, gcol, [[96, 128], [1, KC], [0, 128]])
            P.tt("dve", dstT[:, :, col0:col0 + 128], pb.rearrange("p (k t) -> p k t", t=128), gb, ALU.mult,
                 name="evacT")

        def s0_steps(pinfo, lo=0, hi=None, with_state=True):
            st = {}
            steps = []
            NT_ = pinfo["NT"]
            hi = NT_ if hi is None else hi

            def mk(u):
                def step():
                    if lo + 2 <= u:
                        norm_tr(st.pop(u - 2), hT, (u - 2) * 128, C_GPM)
                    if u < hi:
                        if pinfo["idx"] == 0:
                            if u == 0:
                                for i_ in range(NT_):
                                    xload0.append(P.dma("sp", x1[:, i_, :], pinfo["tiles"][i_][0], "x0_%d" % i_, name="xload0"))
                            xt = x1[:, u, :]
                        else:
                            xs_ = rot("xt", 2)
                            xt = XT[xs_]
                            P.dma("sp", xt, pinfo["tiles"][u][0], "xt%d" % xs_, name="xload")
                        st[u] = norm_prep(xt, "dve" if pinfo["idx"] == 0 else "act")
                return step
            for u in range(lo, hi + 2):
                steps.append(mk(u))
            if pinfo["has_s"] and with_state:
                def ld():
                    P.dma("sp", XT[0][0:120, :], st_pool[0:120, :], "xt0", name="stload")
                    P.dma("sp", XT[1][0:120, :], st_pool[120:240, :], "xt1", name="stload")
                    P.dma("sp", XNF[0:32, :], st_conv[:, :], "stc", name="stload")

                def trs(which):
                    def f():
                        ci = 0
                        for j in range(KC):
                            for (src, nrow, dst) in which(j):
                                bt = bank()
                                P.tr(PS[:, bt, 0:nrow], src, IDF[0:nrow, 0:nrow])
                                P.copy("dve" if ci % 2 == 0 else "act", dst, PS[:, bt, 0:nrow])
                                ci += 1
                    return f
                steps.append(ld)
                steps.append(lambda: None)
                steps.append(trs(lambda j: [(XNF[0:32, j * 128:(j + 1) * 128], 32, SCT[:, j, :]),
                                            (XT[0][0:120, j * 128:(j + 1) * 128], 120, SPT[0][:, j, :])]))
                steps.append(trs(lambda j: [(XT[1][0:120, j * 128:(j + 1) * 128], 120, SPT[1][:, j, :])]))
            return steps

        xload0 = []
        if passes[0]["NT"] >= 8:
            pending_s0 = s0_steps(passes[0], 0, 4)
            s0_rest_p0 = s0_steps(passes[0], 4, None)
        else:
            pending_s0 = s0_steps(passes[0])
            s0_rest_p0 = []

        for ps_i in range(npass):
            pinfo = passes[ps_i]
            last_pass, has_s, tiles, tgs, NT = pinfo["last_pass"], pinfo["has_s"], pinfo["tiles"], pinfo["tgs"], pinfo["NT"]
            T = NT * 128

            s0_rest = s0_rest_p0 if ps_i == 0 else []
            for step in pending_s0:
                step()
            pending_s0 = []
            if ps_i == 0:
                gd_ = [xload0[-1]] if xload0 else []
                P.dma("sp", GBC[:, 0, :], bass.AP(g_post_mix.tensor, 0, [[0, 128], [1, D]]), "gbc0", deps=gd_)
                P.dma("sp", GBC[:, 1, :], bass.AP(g_post_mlp.tensor, 0, [[0, 128], [1, D]]), "gbc1", deps=gd_)

            for b in range(2):
                wxv, wbg, wcg = wgroup([(w_in, 0, 0 * D + b * 512), (w_in, 0, 1 * D + b * 512), (w_in, 0, 2 * D + b * 512)])
                for jj in range(4):
                    j = b * 4 + jj
                    prev_zc = None
                    for (c0, n, kind) in tgs:
                        bxv, bcg, bbg = bank(), bank(), bank()
                        for (bk, wt) in ((bxv, wxv), (bcg, wcg), (bbg, wbg)):
                            for k in range(KC):
                                P.mm(PS[:, bk, 0:n], wt.lhsT(k, jj), hT[:, k, c0:c0 + n],
                                     start=(k == 0), stop=(k == KC - 1))
                        si = rot("cv", 2)
                        xv = XV[si]
                        tcv = TCV[si]
                        if kind == "p":
                            zc = ZC[si]
                            zsl = lambda a, bb, zc=zc: zc[:, a:bb]
                            pview = lambda ap: ap
                            if prev_zc is None:
                                P.copy("dve", zc[:, 0:2], ZST[:, j, :])
                            else:
                                P.copy("dve", zc[:, 0:2], prev_zc[:, 512:514])
                            nn = n
                            xv_v, tcv_v = xv[:, 0:n], tcv[:, 0:n]
                            pcg, pbg, pxv = PS[:, bcg, 0:n], PS[:, bbg, 0:n], PS[:, bxv, 0:n]
                        else:
                            zc = ZSX[rot("sx", 2)]
                            zsl = lambda a, bb, zc=zc: zc[:, :, a:bb]
                            nn = 8
                            r3 = lambda ap: ap.rearrange("p (s t) -> p s t", t=8)
                            xv_v, tcv_v = r3(xv[:, 0:n]), r3(tcv[:, 0:n])
                            pcg, pbg, pxv = r3(PS[:, bcg, 0:n]), r3(PS[:, bbg, 0:n]), r3(PS[:, bxv, 0:n])
                            P.copy("dve", zc[:, :, 0:2], SCT[:, j, :].rearrange("p (s t) -> p s t", t=2))
                        P.copy("act", xv_v, pxv, name="xv")
                        P.tt("dve", zsl(2, 2 + nn), pcg, xv_v, ALU.mult, name="z")
                        P.act(tcv_v, zsl(0, nn), AF.Identity, scale=cs(C_WC + 0 * 8 + j), name="cv0")
                        P.stt("dve", tcv_v, zsl(1, 1 + nn), cs(C_WC + 1 * 8 + j), tcv_v, ALU.mult, ALU.add, name="cv1")
                        P.stt("dve", tcv_v, zsl(2, 2 + nn), cs(C_WC + 2 * 8 + j), tcv_v, ALU.mult, ALU.add, name="cv2")
                        if kind == "p":
                            P.tt("dve", ApT[:, j, c0:c0 + n], pbg, tcv_v, ALU.mult, name="Ap")
                            prev_zc = zc
                            if c0 + n == NPP:
                                P.copy("dve", ZST[:, j, :], zc[:, 512:514])
                        else:
                            P.tt("dve", ApT[:, j, c0:c0 + n].rearrange("p (s t) -> p s t", t=8), pbg, tcv_v, ALU.mult,
                                 name="Ap")
                            P.copy("dve", ZS_[:, j, :].rearrange("p (s t) -> p s t", t=2), zc[:, :, 8:10])
                        if ps_i == 0 and j == 0 and c0 == 0:
                            for step in s0_rest:
                                step()
                            s0_rest = []
                            wgroup(topup=True)
                            P.dma("pool", WPG[:, :, :, :], w_pool_group.rearrange("g (c p) o -> p g c o", p=128), "wpg")

            deferred = []

            def flush_deferred():
                while deferred:
                    deferred.pop(0)[1]()

            def s1b_chunk(j, jj, wu):
                g = j // 2
                w = POOL_W[g]
                prev_uc = None
                for (c0, n, kind) in tgs:
                    bu = bank()
                    for k in range(KC):
                        P.mm(PS[:, bu, 0:n], wu.lhsT(k, jj), hT[:, k, c0:c0 + n],
                             start=(k == 0), stop=(k == KC - 1))
                    pe_ = "pool" if (w > 2 and ((kind == "p" and (c0 // 512) % 2 == 0) or (kind == "s" and j % 2 == 0))) else "dve"
                    if kind == "p":
                        si = rot("pl", 3)
                        uc, pb_, qb_ = UC[si], PB, QB
                        if pe_ == "pool":
                            pb_, qb_ = PBP, QBP
                        sl = lambda buf, a, bb: buf[:, a:bb]
                        nn = n
                        if prev_uc is None:
                            P.copy("dve", uc[:, 0:15], UST[:, j, :])
                        else:
                            P.copy("dve", uc[:, 0:15], prev_uc[:, 512:527])
                        pu = PS[:, bu, 0:n]
                        outv = plT[:, j, c0:c0 + n]
                    else:
                        uc, pb_, qb_ = USX[j % 2], PSX, QSX
                        sl = lambda buf, a, bb: buf[:, :, a:bb]
                        nn = 8
                        pu = PS[:, bu, 0:n].rearrange("p (s t) -> p s t", t=8)
                        outv = plT[:, j, c0:c0 + n].rearrange("p (s t) -> p s t", t=8)
                        for h in range(2):
                            P.copy("dve" if h == 0 else "act", uc[:, h * 8:(h + 1) * 8, 0:15],
                                   SPT[h][:, j, :].rearrange("p (s t) -> p s t", t=15))
                    E = 15 + nn
                    if pe_ == "pool":
                        for d_ in [d for d in deferred if d[0] == kind]:
                            deferred.remove(d_)
                            d_[1]()
                    P.copy("act", sl(uc, 15, E), pu, name="u")
                    if w == 2:
                        P.tt(pe_, outv, sl(uc, 14, E - 1), sl(uc, 15, E), ALU.subtract, name="pool2")
                        sw = None
                    else:
                        P.tt(pe_, sl(pb_, 1, E), sl(uc, 1, E), sl(uc, 0, E - 1), ALU.add, name="s2")
                        P.tt(pe_, sl(qb_, 3, E), sl(pb_, 3, E), sl(pb_, 1, E - 2), ALU.add, name="s4")
                        sw = qb_
                        if w >= 8:
                            P.tt(pe_, sl(pb_, 7, E), sl(qb_, 7, E), sl(qb_, 3, E - 4), ALU.add, name="s8")
                            sw = pb_
                        if w >= 16:
                            P.tt(pe_, sl(qb_, 15, E), sl(pb_, 15, E), sl(pb_, 7, E - 8), ALU.add, name="s16")
                            sw = qb_
                    def fin(w=w, outv=outv, uc=uc, sw=sw, sl=sl, E=E, kind=kind, c0=c0, j=j, g=g):
                        if w > 2:
                            P.stt("dve", outv, sl(uc, 15, E), -float(w), sl(sw, 15, E), ALU.mult, ALU.add, name="pooled")
                        if kind == "p" and ps_i == 0 and c0 == 0:
                            if w == 2:
                                P.memset("dve", plT[:, j, 0:1], 0.0)
                            else:
                                nfix = w - 1
                                P.tt("dve", TMPC[:, 0:nfix], sw[:, 15:15 + nfix], FAC[:, g, 0:nfix], ALU.mult)
                                P.stt("dve", plT[:, j, 0:nfix], uc[:, 15:15 + nfix], -float(w), TMPC[:, 0:nfix],
                                      ALU.mult, ALU.add)
                    if kind == "p":
                        prev_uc = uc
                        if c0 + n == NPP:
                            P.copy("act", UST[:, j, :], uc[:, 512:527])
                    else:
                        P.copy("act", US_[:, j, :].rearrange("p (s t) -> p s t", t=8), uc[:, :, 15:23])
                    if pe_ == "pool":
                        deferred.append((kind, fin))
                    else:
                        fin()
                    while len(deferred) > (1 if pe_ == "pool" else 0):
                        deferred.pop(0)[1]()

            def emit_state_outputs():
                if last_pass:
                    b2 = bank2()
                    for j in range(KC):
                        P.tr(PS[0:2, b2 + j // 4, (j % 4) * 128:(j % 4 + 1) * 128], ZST[:, j, :], IDF[:, :])
                    pz = bass.AP(PS, b2 * 512, [[4096, 2], [1, D]])
                    P.copy("dve", STGS[0][0:2, :], pz)
                    P.dma("sp", ncp[:, :], STGS[0][0:2, :], "so0", is_output=True)
                    b2 = bank2()
                    for j in range(KC):
                        P.tr(PS[0:15, b2 + j // 4, (j % 4) * 128:(j % 4 + 1) * 128], UST[:, j, :], IDF[:, :])
                    pz = bass.AP(PS, b2 * 512, [[4096, 15], [1, D]])
                    P.copy("act", STGS[1][0:15, :], pz)
                    P.dma("sp", npp[:, :], STGS[1][0:15, :], "so1", is_output=True)
                    if has_s:
                        b2 = bank2()
                        for j in range(KC):
                            P.tr(PS[0:32, b2 + j // 4, (j % 4) * 128:(j % 4 + 1) * 128], ZS_[:, j, :], IDF[:, :])
                        pz = bass.AP(PS, b2 * 512, [[4096, 32], [1, D]])
                        P.copy("dve", STGS[2][0:32, :], pz)
                        P.dma("sp", ncs[:, :], STGS[2][0:32, :], "so2", is_output=True)
                        b2 = bank2()
                        for j in range(KC):
                            P.tr(PS[:, b2 + j // 4, (j % 4) * 128:(j % 4 + 1) * 128], US_[:, j, :], IDF[:, :])
                        pz = bass.AP(PS, b2 * 512, [[4096, 128], [1, D]])
                        P.copy("act", STGS[3][:, :], pz)
                        for s in range(NSEQ):
                            P.dma("sp", nps[s * 15 + 7:s * 15 + 15, :], STGS[3][s * 8:(s + 1) * 8, :], "so3", is_output=True)
                        P.dma("sp", nps.rearrange("(s r) d -> s r d", r=15)[:, 0:7, :],
                              st_pool.rearrange("(s r) d -> s r d", r=15)[:, 8:15, :], "so4", is_output=True)


            if has_s:
                S2T = [_view(AR, A_X1 + 5120, F32, [512]), _view(AR, A_X1 + 14080, F32, [512])]
            else:
                S2T = [tmv(14848, F32, [512]), tmv(16896, F32, [512])]

            def s2_chunk(m, jj, wgc, wgp, woc):
                flush_deferred()
                g = m // 2
                for (c0, n, kind) in tgs:
                    bA, bB, bg1, bg2 = bank(), bank(), bank(), bank()
                    for k in range(KC):
                        P.mm(PS[:, bg1, 0:n], wgc.lhsT(k, jj), hT[:, k, c0:c0 + n],
                             start=(k == 0), stop=(k == KC - 1))
                    for k in range(KC):
                        P.mm(PS[:, bg2, 0:n], wgp.lhsT(k, jj), hT[:, k, c0:c0 + n],
                             start=(k == 0), stop=(k == KC - 1))
                    for k in range(KC):
                        P.mm(PS[:, bA, 0:n], woc.lhsT(k, jj), ApT[:, k, c0:c0 + n],
                             start=(k == 0), stop=(k == KC - 1))
                    for c in range(2):
                        P.mm(PS[:, bB, 0:n], WPG[:, g, c, (m % 2) * 128:(m % 2 + 1) * 128],
                             plT[:, 2 * g + c, c0:c0 + n], start=(c == 0), stop=(c == 1))
                    s1, s2 = S2T[0][:, 0:n], S2T[1][:, 0:n]
                    t1, t2 = s1, s2
                    P.act(s1, PS[:, bg1, 0:n], AF.Sigmoid, bias=cs(C_BG + m), name="sig1")
                    P.act(s2, PS[:, bg2, 0:n], AF.Sigmoid, bias=cs(C_BG + 8 + m), name="sig2")
                    P.tt("dve", t1, PS[:, bA, 0:n], s1, ALU.mult, name="mA")
                    P.stt("dve", t2, PS[:, bB, 0:n], cs(C_PS + m), s2, ALU.mult, ALU.mult, name="mB")
                    P.tt("dve", mgT[:, m, c0:c0 + n], t1, t2, ALU.add, name="merge")

            for b in range(2):
                wu, wgc, wgp, woc = wgroup([(w_in, 0, 3 * D + b * 512), (w_in, 0, 4 * D + b * 512),
                                            (w_in, 0, 5 * D + b * 512), (w_out_conv, 0, b * 512)])
                c_ = b * 4
                s1b_chunk(c_ + 0, 0, wu)
                s1b_chunk(c_ + 1, 1, wu)
                s1b_chunk(c_ + 2, 2, wu)
                s2_chunk(c_ + 0, 0, wgc, wgp, woc)
                s1b_chunk(c_ + 3, 3, wu)
                s2_chunk(c_ + 1, 1, wgc, wgp, woc)
                s2_chunk(c_ + 2, 2, wgc, wgp, woc)
                if b == 1:
                    emit_state_outputs()
                s2_chunk(c_ + 3, 3, wgc, wgp, woc)
            flush_deferred()

            wo = wgroup([(w_o, 0, 0), (w_o, 0, 512)])
            s3b, s3x = {}, {}

            def s3_A(i):
                b2 = bank2()
                for nh in range(2):
                    for k in range(KC):
                        P.mm(PS[:, b2 + nh, :], mgT[:, k, i * 128:(i + 1) * 128], wo[nh].full(k),
                             start=(k == 0), stop=(k == KC - 1))
                s3b[i] = b2

            def s3_B1(i):
                pm = bass.AP(PS, s3b.pop(i) * 512, [[4096, 128], [1, D]])
                ss = stat()
                P.act(JUNK, pm, AF.Square, accum_out=ss, name="sq_mix")
                rstd_from(ss)
                xs_ = rot("xt", 2)
                xt = XT[xs_]
                if ps_i == 0:
                    P.stt("dve", xt, pm, ss, GBC[:, 0, :], ALU.mult, ALU.mult, name="mixn")
                else:
                    P.dma("sp", xt, tiles[i][0], "xt%d" % xs_, name="xreload")
                    P.stt("dve", x1[:, i, :], pm, ss, GBC[:, 0, :], ALU.mult, ALU.mult, name="mixn")
                P.tt("pool", x1[:, i, :], x1[:, i, :], xt, ALU.add, name="x1")

            def s3_B2(i):
                s3x[i] = norm_prep(x1[:, i, :])

            def s3_C(i):
                norm_tr(s3x.pop(i), h2T, i * 128, C_GPL)

            s4_early = {"w": None, "done": set()}

            def s4_unit(wu_, f, jj, c0, n):
                bu = bank()
                for k in range(KC):
                    P.mm(PS[:, bu, 0:n], wu_.lhsT(k, jj), h2T[:, k, c0:c0 + n],
                         start=(k == 0), stop=(k == KC - 1))
                rl = RL[rot("rl", 3)][:, 0:n]
                P.act(rl, PS[:, bu, 0:n], AF.Relu, name="relu")
                P.tt("dve", actT[:, f, c0:c0 + n], rl, rl, ALU.mult, name="sqr")

            for u in range(NT + 3):
                if u < NT:
                    s3_A(u)
                if 1 <= u <= NT:
                    s3_B1(u - 1)
                if 2 <= u <= NT + 1:
                    s3_B2(u - 2)
                if u >= 3:
                    s3_C(u - 3)
                u0_ = max(7, NT)
                if u >= u0_:
                    if s4_early["w"] is None:
                        s4_early["w"] = wgroup([(w_up, 0, 0), (w_up, 0, 512)])
                    todo_ = [(bl_, jj_) for bl_ in range(2) for jj_ in range(4)
                             if (bl_ * 4 + jj_, tgs[0][0]) not in s4_early["done"]]
                    if u < NT + 2:
                        todo_ = todo_[:2]
                    for (bl_, jj_) in todo_:
                        s4_unit(s4_early["w"][bl_], bl_ * 4 + jj_, jj_, tgs[0][0], tgs[0][1])
                        s4_early["done"].add((bl_ * 4 + jj_, tgs[0][0]))

            if s4_early["w"] is None:
                s4_early["w"] = wgroup([(w_up, 0, 0), (w_up, 0, 512)])
            for blk in range(DFF // 512):
                if blk < 2:
                    wu_ = s4_early["w"][blk]
                else:
                    wu_, = wgroup([(w_up, 0, blk * 512)])
                for jj in range(4):
                    f = blk * 4 + jj
                    for (c0, n, kind) in tgs:
                        if (f, c0) in s4_early["done"]:
                            continue
                        s4_unit(wu_, f, jj, c0, n)

            groups = [list(range(a, min(a + 3, NT))) for a in range(0, NT, 3)]
            if not last_pass:
                pending_s0 = s0_steps(passes[ps_i + 1])
            unit_i = 0
            bank_pool[0] = [6, 7]
            for G in groups:
                bk = {}
                for gi_, i in enumerate(G):
                    bk[i] = 2 * gi_
                od_ = s5_order(groups.index(G))
                for (nh, kg) in od_:
                    kgs_ = [kg2 for (nh2, kg2) in od_ if nh2 == nh]
                    first_k, last_k = (kg == kgs_[0]), (kg == kgs_[-1])
                    wd, = wgroup([(w_down, kg * 1024, nh * 512)])
                    if unit_i >= 1 and pending_s0:
                        pending_s0.pop(0)()
                    unit_i += 1
                    for i in G:
                        for kk in range(KC):
                            P.mm(PS[:, bk[i] + nh, :], actT[:, kg * 8 + kk, i * 128:(i + 1) * 128], wd.full(kk),
                                 start=(first_k and kk == 0), stop=(last_k and kk == KC - 1),
                                 signal=((last_k or i == G[-1]) and kk == KC - 1))
                for i in G:
                    pm = bass.AP(PS, bk[i] * 512, [[4096, 128], [1, D]])
                    ss = stat()
                    P.act(JUNK, pm, AF.Square, accum_out=ss, name="sq_mlp")
                    rstd_from(ss)
                    t4 = XT[rot("xt", 2)]
                    P.stt("dve", t4, pm, ss, GBC[:, 1, :], ALU.mult, ALU.mult, name="mlpn")
                    P.tt("dve", x1[:, i, :], x1[:, i, :], t4, ALU.add, name="y")
                    P.dma("sp", tiles[i][1], x1[:, i, :], "yo%d" % i, name="ystore", is_output=True)
            for step in pending_s0:
                step()
            pending_s0 = []
            bank_pool[0] = None

        nwait = P.emit(stack)
    return nc


_NC_CACHE = {}


def kernel(**inputs):
    f = lambda a: np.ascontiguousarray(np.asarray(a, dtype=np.float32))
    x_prompt = f(inputs["x_prompt"])
    x_sample = f(inputs["x_sample"])
    state_conv = f(inputs["state_conv"])
    state_pool = f(inputs["state_pool"])
    shared = {
        "g_pre_mix": f(inputs["g_pre_mix"]).reshape(D),
        "w_in": f(inputs["w_in"]).reshape(D, 6 * D),
        "b_gate": f(inputs["b_gate"]).reshape(2 * D),
        "w_conv": f(inputs["w_conv"]).reshape(3, D),
        "w_out_conv": f(inputs["w_out_conv"]).reshape(D, D),
        "w_pool_group": f(inputs["w_pool_group"]).reshape(4, 256, 256),
        "pool_scale": f(inputs["pool_scale"]).reshape(D),
        "w_o": f(inputs["w_o"]).reshape(D, D),
        "g_post_mix": f(inputs["g_post_mix"]).reshape(D),
        "g_pre_mlp": f(inputs["g_pre_mlp"]).reshape(D),
        "w_up": f(inputs["w_up"]).reshape(D, DFF),
        "w_down": f(inputs["w_down"]).reshape(DFF, D),
        "g_post_mlp": f(inputs["g_post_mlp"]).reshape(D),
    }
    in_maps = []
    for c in range(NCORES):
        m = dict(shared)
        m["x_p"] = x_prompt[c]
        m["x_s"] = x_sample[16 * c:16 * (c + 1)].reshape(128, D)
        m["st_conv"] = state_conv[0, 16 * c:16 * (c + 1)].reshape(32, D)
        m["st_pool"] = state_pool[0, 16 * c:16 * (c + 1)].reshape(240, D)
        in_maps.append(m)
    if "nc" not in _NC_CACHE:
        _NC_CACHE["nc"] = build_program()
    nc = _NC_CACHE["nc"]
    res = run_bass_kernel_spmd(nc, in_maps, core_ids=list(range(NCORES)))
    R = res.results
    y_prompt = np.stack([R[c]["y_p"] for c in range(NCORES)], axis=0)
    y_sample = np.concatenate([R[c]["y_s"].reshape(16, 8, D) for c in range(NCORES)], axis=0)
    ncp = np.stack([R[c]["ncp"] for c in range(NCORES)], axis=0)[None]
    npp = np.stack([R[c]["npp"] for c in range(NCORES)], axis=0)[None]
    ncs = np.concatenate([R[c]["ncs"].reshape(16, 2, D) for c in range(NCORES)], axis=0)[None]
    nps = np.concatenate([R[c]["nps"].reshape(16, 15, D) for c in range(NCORES)], axis=0)[None]
    return (y_prompt.astype(np.float32), y_sample.astype(np.float32), ncp.astype(np.float32),
            npp.astype(np.float32), ncs.astype(np.float32), nps.astype(np.float32))
```

```python
import numpy as np
from contextlib import ExitStack
import concourse.bass as bass
import concourse.mybir as mybir
from concourse.bass_utils import run_bass_kernel_spmd

F32 = mybir.dt.float32
BF16 = mybir.dt.bfloat16
ALU = mybir.AluOpType
AF = mybir.ActivationFunctionType
SZ = {F32: 4, BF16: 2}

D = 1024
KC = 8
DFF = 4096
EPS = 1e-6
NCORES = 8
POOL_W = (2, 4, 8, 16)


def _prod(xs):
    r = 1
    for x in xs:
        r *= int(x)
    return r


class _Op:
    __slots__ = ("eng", "fn", "deps", "signal", "pos", "dma", "name", "sig", "wait_val")

    def __init__(self, eng, fn, name):
        self.eng = eng
        self.fn = fn
        self.deps = set()
        self.signal = True
        self.pos = None
        self.dma = None
        self.name = name
        self.sig = None
        self.wait_val = None


class Prog:
    ENG = ("pe", "act", "dve", "pool", "sp")

    def __init__(self, nc):
        self.nc = nc
        self.engs = {"pe": nc.tensor, "act": nc.scalar, "dve": nc.vector,
                     "pool": nc.gpsimd, "sp": nc.sync}
        self.streams = {e: [] for e in self.ENG}
        self.recs = {}
        self.dma_cnt = {}
        self.out_dmas = []

    @staticmethod
    def _iv(ap):
        t = ap.tensor
        if type(t).__name__.startswith("DRam"):
            return None
        esz = SZ[ap.dtype]
        pbytes = _prod(t.shape[1:]) * SZ[t.dtype]
        off = int(ap.offset) * esz
        lo = off % pbytes
        dims = list(ap.ap)
        p0 = off // pbytes
        p1 = p0 + ((int(dims[0][1]) - 1) * abs(int(dims[0][0])) * esz) // pbytes + 1
        fd = sorted([(abs(int(st)) * esz, int(cnt)) for (st, cnt) in dims[1:] if int(cnt) > 1], key=lambda d: -d[0])

        def rec(base, ds):
            if not ds:
                return [(base, base + esz)]
            inner = esz + sum((c - 1) * s_ for s_, c in ds[1:])
            s0, c0 = ds[0]
            if s0 >= 2 * inner and c0 <= 32:
                out = []
                for i in range(c0):
                    out += rec(base + i * s0, ds[1:])
                return out
            return [(base, base + esz + sum((c - 1) * s_ for s_, c in ds))]
        ivs = rec(lo, fd)
        if len(ivs) > 64:
            ivs = [(lo, lo + esz + sum((c - 1) * s_ for s_, c in fd))]
        return t.name, ivs, p0, p1

    def _acc(self, op, ap, is_write):
        iv = self._iv(ap)
        if iv is None:
            return
        key, ivs, p0, p1 = iv
        for (lo, hi) in ivs:
            self._acc1(op, key, lo, hi, p0, p1, is_write)

    def _acc1(self, op, key, lo, hi, p0, p1, is_write):
        recs = self.recs.get(key, [])
        new = []
        for r in recs:
            rlo, rhi, rop, rw, rp0, rp1 = r
            if rhi <= lo or rlo >= hi or rp1 <= p0 or rp0 >= p1:
                new.append(r)
                continue
            if (is_write or rw) and rop is not op:
                op.deps.add(rop)
            cov = rlo >= lo and rhi <= hi and rp0 >= p0 and rp1 <= p1
            if is_write and cov:
                continue
            if (not is_write) and (not rw) and rop.dma is None and op.dma is None \
                    and rop.eng == op.eng and cov:
                continue
            new.append(r)
        new.append((lo, hi, op, is_write, p0, p1))
        self.recs[key] = new

    def add(self, eng, fn, reads=(), writes=(), signal=True, name="", dma=None):
        op = _Op(eng, fn, name)
        op.signal = signal
        op.dma = dma
        for ap in reads:
            if ap is not None and not isinstance(ap, (int, float)):
                self._acc(op, ap, False)
        for ap in writes:
            self._acc(op, ap, True)
        op.pos = len(self.streams[eng])
        self.streams[eng].append(op)
        return op

    def dma(self, q, out, in_, semkey, name="", is_output=False, deps=(), **kw):
        cnt = self.dma_cnt.get(semkey, 0) + 16
        self.dma_cnt[semkey] = cnt
        op = self.add(q, lambda e: e.dma_start(out=out, in_=in_, **kw), reads=[in_], writes=[out],
                      name=name, dma=(semkey, cnt))
        op.deps.update(deps)
        if is_output:
            self.out_dmas.append(op)
        return op

    def mm(self, out, lhsT, rhs, start, stop, signal=None):
        sig = stop if signal is None else signal
        return self.add("pe", lambda e: e.matmul(out, lhsT, rhs, start=start, stop=stop),
                        reads=[lhsT, rhs], writes=[out], signal=sig, name="mm")

    def tr(self, out, in_, ident, signal=True):
        return self.add("pe", lambda e: e.transpose(out, in_, ident), reads=[in_, ident], writes=[out],
                        signal=signal, name="tr")

    def act(self, out, in_, func, bias=None, scale=None, accum_out=None, name="act"):
        kw = {}
        if bias is not None:
            kw["bias"] = bias
        if scale is not None:
            kw["scale"] = scale
        if accum_out is not None:
            kw["accum_out"] = accum_out
        rd = [in_] + [x for x in (bias, scale) if x is not None and not isinstance(x, (int, float))]
        wr = [out] + ([accum_out] if accum_out is not None else [])
        return self.add("act", lambda e: e.activation(out, in_, func, **kw), reads=rd, writes=wr, name=name)

    def tt(self, eng, out, in0, in1, op, name="tt"):
        return self.add(eng, lambda e: e.tensor_tensor(out, in0, in1, op), reads=[in0, in1], writes=[out], name=name)

    def ts(self, eng, out, in0, s1, s2, op0, op1=None, name="ts"):
        rd = [in0] + [x for x in (s1, s2) if x is not None and not isinstance(x, (int, float))]
        if op1 is None:
            return self.add(eng, lambda e: e.tensor_scalar(out, in0, s1, s2, op0), reads=rd, writes=[out], name=name)
        return self.add(eng, lambda e: e.tensor_scalar(out, in0, s1, s2, op0, op1), reads=rd, writes=[out], name=name)

    def stt(self, eng, out, in0, scalar, in1, op0, op1, name="stt"):
        rd = [in0, in1] + ([scalar] if not isinstance(scalar, (int, float)) else [])
        return self.add(eng, lambda e: e.scalar_tensor_tensor(out, in0, scalar, in1, op0, op1),
                        reads=rd, writes=[out], name=name)

    def copy(self, eng, out, in_, name="copy"):
        if eng == "act":
            return self.act(out, in_, AF.Identity, name=name)
        return self.add(eng, lambda e: e.tensor_copy(out, in_), reads=[in_], writes=[out], name=name)

    def memset(self, eng, out, val, name="memset"):
        return self.add(eng, lambda e: e.memset(out, val), writes=[out], name=name)

    def emit(self, stack):
        nc = self.nc
        sems = {}
        for e in ("pe", "act", "dve", "pool"):
            sems[e] = stack.enter_context(nc.semaphore("s_" + e))
        for k in self.dma_cnt:
            sems[("dma", k)] = stack.enter_context(nc.semaphore("d_" + str(k)))
        fin = _Op("sp", None, "final")
        fin.deps = set(self.out_dmas)
        self.streams["sp"].append(fin)
        for e in self.ENG:
            c = 0
            for op in self.streams[e]:
                if op.dma is None and op.fn is not None and op.signal:
                    c += 1
                    op.sig = c
            nxt = None
            for op in reversed(self.streams[e]):
                if op.dma is None and op.fn is not None:
                    if op.signal:
                        nxt = op.sig
                    op.wait_val = nxt
        nwait = 0
        for e in self.ENG:
            eng = self.engs[e]
            waited = {}
            for op in self.streams[e]:
                need = {}
                for d in op.deps:
                    if d.dma is not None:
                        key = ("dma", d.dma[0])
                        val = d.dma[1]
                    else:
                        if d.eng == "pe" and e == "pe":
                            continue
                        key = d.eng
                        val = d.wait_val
                        assert val is not None, (d.name, op.name)
                    if waited.get(key, 0) >= val:
                        continue
                    if need.get(key, 0) < val:
                        need[key] = val
                for key, val in need.items():
                    waited[key] = val
                    eng.wait_ge(sems[key], val)
                    nwait += 1
                if op.fn is not None:
                    ins = op.fn(eng)
                    if op.dma is not None:
                        ins.then_inc(sems[("dma", op.dma[0])], 16)
                    elif op.signal:
                        ins.then_inc(sems[e], 1)
        return nwait


def _view(t, boff, dt, shape):
    n = _prod(shape)
    esz = SZ[dt]
    assert boff % 4 == 0
    e0 = boff // 2
    e1 = e0 + n * esz // 2
    ap = t[:, e0:e1]
    if dt != BF16:
        ap = ap.bitcast(dt)
    if len(shape) == 2:
        ap = ap.rearrange("p (a b) -> p a b", b=shape[1])
    elif len(shape) == 3:
        ap = ap.rearrange("p (a b c) -> p a b c", b=shape[1], c=shape[2])
    return ap


def build_program(ntg=2, with_sample=True, npass=2):
    nc = bass.Bass("TRN2", target_bir_lowering=False)
    P = Prog(nc)
    NPP = 512 * ntg
    NPT = NPP * npass
    NSEQ = 16
    NST = NSEQ * 8
    TMAX = NPP + (NST if with_sample else 0)

    def din(name, shape):
        return nc.dram_tensor(name, list(shape), F32, kind="ExternalInput").ap()

    def dout(name, shape):
        return nc.dram_tensor(name, list(shape), F32, kind="ExternalOutput").ap()

    x_p = din("x_p", [NPT, D])
    x_s = din("x_s", [NST, D])
    st_conv = din("st_conv", [NSEQ * 2, D])
    st_pool = din("st_pool", [NSEQ * 15, D])
    g_pre_mix = din("g_pre_mix", [D])
    w_in = din("w_in", [D, 6 * D])
    b_gate = din("b_gate", [2 * D])
    w_conv = din("w_conv", [3, D])
    w_out_conv = din("w_out_conv", [D, D])
    w_pool_group = din("w_pool_group", [4, 256, 256])
    pool_scale = din("pool_scale", [D])
    w_o = din("w_o", [D, D])
    g_post_mix = din("g_post_mix", [D])
    g_pre_mlp = din("g_pre_mlp", [D])
    w_up = din("w_up", [D, DFF])
    w_down = din("w_down", [DFF, D])
    g_post_mlp = din("g_post_mlp", [D])

    y_p = dout("y_p", [NPT, D])
    y_s = dout("y_s", [NST, D])
    ncp = dout("ncp", [2, D])
    npp = dout("npp", [15, D])
    ncs = dout("ncs", [NSEQ * 2, D])
    nps = dout("nps", [NSEQ * 15, D])

    wd_bf = nc.dram_tensor("wd_bf", [8, 128, KC * 512], BF16, kind="Internal").ap()

    stack = ExitStack()
    with stack:
        NT_MAX = TMAX // 128
        A_ACT = 0
        A_HT = 0
        A_AP = KC * TMAX * 2
        A_PL = 2 * KC * TMAX * 2
        A_MG = 3 * KC * TMAX * 2
        A_X1 = 32 * TMAX * 2
        A_H2 = A_X1 + NT_MAX * 4096
        A_END = A_H2 + KC * TMAX * 2
        AR = stack.enter_context(nc.sbuf_tensor("AR", [128, A_END // 2], BF16))
        NSLOT = 6
        WS = stack.enter_context(nc.sbuf_tensor("WS", [128, NSLOT * 4096], BF16))
        WPG = stack.enter_context(nc.sbuf_tensor("WPG", [128, 4, 2, 256], BF16))
        TM_BYTES = 20096
        TM = stack.enter_context(nc.sbuf_tensor("TM", [128, TM_BYTES // 2], BF16))
        GBC = stack.enter_context(nc.sbuf_tensor("GBC", [128, 2, D], F32))
        IDB = stack.enter_context(nc.sbuf_tensor("IDB", [128, 128], BF16))
        IDF = stack.enter_context(nc.sbuf_tensor("IDF", [128, 128], F32))
        CS = stack.enter_context(nc.sbuf_tensor("CS", [128, 96], F32))
        STT_ = stack.enter_context(nc.sbuf_tensor("STAT", [128, 64], F32))
        ZST = stack.enter_context(nc.sbuf_tensor("ZST", [128, KC, 2], F32))
        UST = stack.enter_context(nc.sbuf_tensor("UST", [128, KC, 15], F32))
        FAC = stack.enter_context(nc.sbuf_tensor("FAC", [128, 5, 16], F32))
        PS = stack.enter_context(nc.psum_tensor("PS", [128, 8, 512], F32))

        hT = _view(AR, A_H2, BF16, [KC, TMAX])
        ApT = _view(AR, A_AP, BF16, [KC, TMAX])
        plT = _view(AR, A_PL, BF16, [KC, TMAX])
        mgT = _view(AR, A_MG, BF16, [KC, TMAX])
        actT = _view(AR, A_ACT, BF16, [32, TMAX])
        x1 = _view(AR, A_X1, F32, [NT_MAX, D])
        h2T = _view(AR, A_H2, BF16, [KC, TMAX])
        US_ = _view(AR, A_X1, F32, [KC, 128])
        ZS_ = _view(AR, A_X1 + KC * 128 * 4, F32, [KC, 32])

        C_GPM = 0
        C_GPL = 8
        C_BG = 16
        C_WC = 32
        C_PS = 56
        C_EPS = 64

        def cs(col, n=1):
            return CS[:, col:col + n]

        stat_ctr = [0]

        def stat():
            i = stat_ctr[0] % 64
            stat_ctr[0] += 1
            return STT_[:, i:i + 1]

        bank_ctr = [0]
        bank_pool = [None]

        def bank():
            if bank_pool[0] is not None:
                pool_ = bank_pool[0]
                b = pool_[bank_ctr[0] % len(pool_)]
                bank_ctr[0] += 1
                return b
            b = bank_ctr[0] % 8
            bank_ctr[0] += 1
            return b

        def bank2():
            if bank_ctr[0] % 2:
                bank_ctr[0] += 1
            b = bank_ctr[0] % 8
            bank_ctr[0] += 2
            return b

        slot_ctr = [0]

        def wslot():
            s = slot_ctr[0] % NSLOT
            slot_ctr[0] += 1
            return s

        def s5_order(gi_):
            od = [(nh, kg) for nh in range(2) for kg in range(4)]
            return od if gi_ % 2 == 0 else od[::-1]

        plan = []
        for ps_i in range(npass):
            nt_ = NPP // 128 + (1 if (with_sample and ps_i == npass - 1) else 0)
            for b in range(2):
                plan.append([(w_in, 0, 0 * D + b * 512), (w_in, 0, 1 * D + b * 512), (w_in, 0, 2 * D + b * 512)])
            for b in range(2):
                plan.append([(w_in, 0, 3 * D + b * 512), (w_in, 0, 4 * D + b * 512), (w_in, 0, 5 * D + b * 512),
                             (w_out_conv, 0, b * 512)])
            plan.append([(w_o, 0, 0), (w_o, 0, 512)])
            plan.append([(w_up, 0, 0), (w_up, 0, 512)])
            for blk in range(2, DFF // 512):
                plan.append([(w_up, 0, blk * 512)])
            for gi_, a in enumerate(range(0, nt_, 3)):
                for (nh, kg) in s5_order(gi_):
                    plan.append([(w_down, kg * 1024, nh * 512)])
        entries = [e for g in plan for e in g]
        gfirst = []
        gof = []
        acc_ = 0
        for gi_, g in enumerate(plan):
            gfirst.append(acc_)
            acc_ += len(g)
            gof += [gi_] * len(g)
        keyof = [(id(W_), r_, c_) for (W_, r_, c_) in entries]
        slot_of, need_load, free_after = [], [], []
        content = [None] * NSLOT
        last_use = [-1] * NSLOT
        for e, k_ in enumerate(keyof):
            if k_ in content and e >= 3 and k_[0] == id(w_down):
                sl_ = content.index(k_)
                slot_of.append(sl_)
                need_load.append(False)
                free_after.append(-1)
            else:
                same_grp = [slot_of[e2] for e2 in range(gfirst[gof[e]], e)]
                best, best_key = None, None
                for sl_ in range(NSLOT):
                    if sl_ in same_grp:
                        continue
                    if content[sl_] is None or content[sl_][0] != id(w_down):
                        nu = 10 ** 9
                    else:
                        nu = 10 ** 9
                        for e2 in range(e + 1, min(len(keyof), e + 24)):
                            if keyof[e2] == content[sl_]:
                                nu = e2
                                break
                    cand = (nu, -last_use[sl_])
                    if best_key is None or cand > best_key:
                        best, best_key = sl_, cand
                slot_of.append(best)
                need_load.append(True)
                free_after.append(gof[last_use[best]] if last_use[best] >= 0 else -1)
                content[best] = k_
            last_use[slot_of[e]] = e
        wstate = {"g": 0, "issued": 0, "conv": 0}
        conv_ops = []

        class WV:
            def __init__(self, e, split):
                self.split = split
                if split:
                    self.v = _view(WS, slot_of[e] * 8192, BF16, [4, KC, 128])
                else:
                    self.v = _view(WS, slot_of[e] * 8192, BF16, [KC, 512])

            def lhsT(self, k, jj):
                return self.v[:, jj, k, :] if self.split else self.v[:, k, jj * 128:(jj + 1) * 128]

            def full(self, k):
                assert not self.split
                return self.v[:, k, :]

        def wslot_view(e):
            return WV(e, e < 3)

        def wgroup(expect=None, topup=False):
            if topup:
                wstate["g"] -= 1
            g = wstate["g"]
            wstate["g"] += 1
            first, cnt = gfirst[g], len(plan[g])
            if expect is not None:
                assert [(a is b_, r, c) for (a, r, c), (b_, r2, c2) in zip(plan[g], expect)] and \
                    all(a is b_ and r == r2 and c == c2 for (a, r, c), (b_, r2, c2) in zip(plan[g], expect)), g
            if 1 <= g <= 4 and not topup:
                for c in (2 * (g - 1), 2 * (g - 1) + 1):
                    kg_, nh_ = c // 2, c % 2
                    conv_ops.append(P.dma("pool", wd_bf[c, :, :].rearrange("p (k c) -> p k c", c=512),
                                          w_down[kg_ * 1024:(kg_ + 1) * 1024, nh_ * 512:(nh_ + 1) * 512].rearrange(
                                              "(k p) c -> p k c", p=128),
                                          "wdc", name="wdconv"))
                    if c == 7:
                        for op_ in conv_ops:
                            op_.dma = ("wdc", P.dma_cnt["wdc"])
            while wstate["issued"] < len(entries):
                e = wstate["issued"]
                own = e < first + cnt
                ahead_ok = (free_after[e] < g) and not (g == 0 and not topup)
                if not (own or ahead_ok):
                    break
                if own:
                    assert free_after[e] < g
                W, r0, c0 = entries[e]
                if not need_load[e]:
                    pass
                elif e < 3:
                    if e == 0:
                        for jj_ in range(4):
                            for e_ in range(3):
                                W_, r_0, c_0 = entries[e_]
                                src = W_[r_0:r_0 + KC * 128, c_0 + jj_ * 128:c_0 + (jj_ + 1) * 128].rearrange(
                                    "(k p) c -> p k c", p=128)
                                P.dma("pool", wslot_view(e_).v[:, jj_, :, :], src, "wi%d_%d" % (e_, jj_), name="wload0",
                                      deps=([xload0[min(3, len(xload0) - 1)]] if (jj_ > 0 and xload0) else []))
                elif W is w_down:
                    src = wd_bf[(r0 // 1024) * 2 + (c0 // 512), :, :].rearrange("p (k c) -> p k c", c=512)
                    P.dma("pool", wslot_view(e).v, src, "w%d" % slot_of[e], name="wload_bf", deps=conv_ops)
                else:
                    src = W[r0:r0 + KC * 128, c0:c0 + 512].rearrange("(k p) c -> p k c", p=128)
                    P.dma("pool", wslot_view(e).v, src, "w%d" % slot_of[e], name="wload")
                wstate["issued"] += 1
            return [wslot_view(first + i) for i in range(cnt)]

        CST = _view(TM, 8192, F32, [128])
        r_ = lambda v: v.rearrange("(k p) -> k p", p=128)
        P.dma("act", CST[0:8, :], r_(g_pre_mix), "const")
        P.dma("act", CST[8:16, :], r_(g_pre_mlp), "const")
        P.dma("act", CST[16:32, :], r_(b_gate), "const")
        P.dma("act", CST[32:56, :], w_conv.rearrange("w (k p) -> (w k) p", p=128), "const")
        P.dma("act", CST[56:64, :], r_(pool_scale), "const")
        tot = P.dma_cnt["const"]
        for op in P.streams["act"]:
            if op.dma is not None and op.dma[0] == "const":
                op.dma = ("const", tot)

        P.memset("pool", IDF[:, :], 0.0)
        P.add("pool", lambda e: e.affine_select(out=IDF[:, :], in_=IDF[:, :], compare_op=ALU.not_equal,
                                                 fill=1.0, base=0, pattern=[[-1, 128]], channel_multiplier=1),
              reads=[IDF[:, :]], writes=[IDF[:, :]], name="ident")
        P.copy("dve", IDB[:, :], IDF[:, :])
        P.act(STT_[:, 63:64], IDF[:, 0:1], AF.Square, name="act_table_warm")
        bt0 = 0
        P.tr(PS[:, bt0, 0:64], CST[0:64, :], IDF[0:64, 0:64])
        P.copy("dve", CS[:, 0:64], PS[:, bt0, 0:64])
        P.add("pool", lambda e: e.iota(FAC[:, 4, :], pattern=[[1, 16]], base=1, channel_multiplier=0,
                                       allow_small_or_imprecise_dtypes=True),
              writes=[FAC[:, 4, :]], name="iota")
        for g, w in enumerate(POOL_W):
            P.ts("dve", FAC[:, g, :], FAC[:, 4, :], float(w), None, ALU.min)
            P.add("dve", lambda e, g=g: e.reciprocal(FAC[:, g, :], FAC[:, g, :]),
                  reads=[FAC[:, g, :]], writes=[FAC[:, g, :]], name="recip")
            P.ts("dve", FAC[:, g, :], FAC[:, g, :], float(w), None, ALU.mult)
            P.ts("dve", cs(C_PS + 2 * g, 2), cs(C_PS + 2 * g, 2), 1.0 / w, None, ALU.mult)
        P.memset("dve", cs(C_EPS), EPS)
        P.memset("dve", ZST[:, :, :], 0.0)
        P.memset("dve", UST[:, :, :], 0.0)

        def tmv(off, dt, shape):
            return _view(TM, off, dt, shape)

        XT = [tmv(0 + i * 4096, F32, [D]) for i in range(2)]
        XN = [tmv(8192 + i * 2048, BF16, [D]) for i in range(2)]
        JUNK = tmv(12288, BF16, [D])
        XV = [tmv(0 + i * 2048, F32, [512]) for i in range(2)]
        ZC = [tmv(4096 + i * 2064, F32, [514]) for i in range(2)]
        TCV = [tmv(8224 + i * 2048, F32, [512]) for i in range(2)]
        ZSX = [tmv(12320 + i * 640, F32, [16, 10]) for i in range(2)]
        UC = [tmv(i * 2112, F32, [527]) for i in range(3)]
        PB = tmv(6336, F32, [527])
        QB = tmv(8448, F32, [527])
        PBP = tmv(10560, F32, [527])
        QBP = tmv(12672, F32, [527])
        TMPC = tmv(14784, F32, [16])
        USX = [_view(AR, A_X1 + 2 * 4096 + i * 1472, F32, [16, 23]) for i in range(2)]
        PSX = _view(AR, A_X1 + 2 * 4096 + 2944, F32, [16, 23])
        QSX = _view(AR, A_X1 + 2 * 4096 + 4416, F32, [16, 23])
        SPT = [_view(AR, A_X1 + (NT_MAX - 1) * 4096, F32, [KC, 120]), tmv(14848, F32, [KC, 120])]
        SCT = tmv(18688, F32, [KC, 32])
        XNF = tmv(8192, F32, [D])
        SG1 = [tmv(0 + i * 2048, F32, [512]) for i in range(2)]
        SG2 = [tmv(4096 + i * 2048, F32, [512]) for i in range(2)]
        MT1 = [tmv(8192 + i * 2048, F32, [512]) for i in range(2)]
        MT2 = [tmv(12288 + i * 2048, F32, [512]) for i in range(2)]
        RL = [tmv(i * 2048, F32, [512]) for i in range(3)]
        if NT_MAX >= 9:
            STGS = [_view(AR, A_X1 + (4 + i) * 4096, F32, [D]) for i in range(4)]
        else:
            STGS = [tmv(i * 4096, F32, [D]) for i in range(4)]

        ctr = {"xt": 0, "xn": 0, "t4k": 0, "cv": 0, "pl": 0, "sg": 0, "rl": 0, "sx": 0}

        def rot(key, n):
            i = ctr[key] % n
            ctr[key] += 1
            return i

        def rstd_from(ss):
            P.act(ss, ss, AF.Sqrt, bias=cs(C_EPS), scale=1.0 / D, name="sqrt")
            P.add("dve", lambda e: e.reciprocal(ss, ss), reads=[ss], writes=[ss], name="rstd")
            return ss

        def make_pass(ps_i):
            last_pass = ps_i == npass - 1
            has_s = with_sample and last_pass
            tiles = []
            for i in range(NPP // 128):
                r0 = ps_i * NPP + i * 128
                tiles.append((x_p[r0:r0 + 128, :], y_p[r0:r0 + 128, :]))
            if has_s:
                tiles.append((x_s[:, :], y_s[:, :]))
            tgs = [(g * 512, 512, "p") for g in range(ntg)]
            if has_s:
                tgs.append((NPP, 128, "s"))
            return dict(idx=ps_i, last_pass=last_pass, has_s=has_s, tiles=tiles, tgs=tgs, NT=len(tiles))

        passes = [make_pass(i) for i in range(npass)]

        def norm_prep(src_tok, xn_eng="act"):
            ss = stat()
            P.act(JUNK, src_tok, AF.Square, accum_out=ss, name="sq")
            rstd_from(ss)
            xn = XN[rot("xn", 2)]
            if xn_eng == "act":
                P.act(xn, src_tok, AF.Identity, scale=ss, name="xn")
            else:
                P.ts("dve", xn, src_tok, ss, None, ALU.mult, name="xn")
            return xn

        def norm_tr(xn, dstT, col0, gcol):
            b = bank()
            pb = PS[:, b, :].bitcast(BF16)
            for k in range(KC):
                P.tr(pb[:, k * 128:(k + 1) * 128], xn[:, k * 128:(k + 1) * 128], IDB[:, :], signal=(k == KC - 1))
            gb = bass.AP(CS, gcol, [[96, 128], [1, KC], [0, 128]])
            P.tt("dve", dstT[:, :, col0:col0 + 128], pb.rearrange("p (k t) -> p k t", t=128), gb, ALU.mult,
                 name="evacT")

        def s0_steps(pinfo, lo=0, hi=None, with_state=True):
            st = {}
            steps = []
            NT_ = pinfo["NT"]
            hi = NT_ if hi is None else hi

            def mk(u):
                def step():
                    if lo + 2 <= u:
                        norm_tr(st.pop(u - 2), hT, (u - 2) * 128, C_GPM)
                    if u < hi:
                        if pinfo["idx"] == 0:
                            if u == 0:
                                for i_ in range(NT_):
                                    xload0.append(P.dma("sp", x1[:, i_, :], pinfo["tiles"][i_][0], "x0_%d" % i_, name="xload0"))
                            xt = x1[:, u, :]
                        else:
                            xs_ = rot("xt", 2)
                            xt = XT[xs_]
                            P.dma("sp", xt, pinfo["tiles"][u][0], "xt%d" % xs_, name="xload")
                        st[u] = norm_prep(xt, "dve" if pinfo["idx"] == 0 else "act")
                return step
            for u in range(lo, hi + 2):
                steps.append(mk(u))
            if pinfo["has_s"] and with_state:
                def ld():
                    P.dma("sp", XT[0][0:120, :], st_pool[0:120, :], "xt0", name="stload")
                    P.dma("sp", XT[1][0:120, :], st_pool[120:240, :], "xt1", name="stload")
                    P.dma("sp", XNF[0:32, :], st_conv[:, :], "stc", name="stload")

                def trs(which):
                    def f():
                        ci = 0
                        for j in range(KC):
                            for (src, nrow, dst) in which(j):
                                bt = bank()
                                P.tr(PS[:, bt, 0:nrow], src, IDF[0:nrow, 0:nrow])
                                P.copy("dve" if ci % 2 == 0 else "act", dst, PS[:, bt, 0:nrow])
                                ci += 1
                    return f
                steps.append(ld)
                steps.append(lambda: None)
                steps.append(trs(lambda j: [(XNF[0:32, j * 128:(j + 1) * 128], 32, SCT[:, j, :]),
                                            (XT[0][0:120, j * 128:(j + 1) * 128], 120, SPT[0][:, j, :])]))
                steps.append(trs(lambda j: [(XT[1][0:120, j * 128:(j + 1) * 128], 120, SPT[1][:, j, :])]))
            return steps

        xload0 = []
        if passes[0]["NT"] >= 8:
            pending_s0 = s0_steps(passes[0], 0, 4)
            s0_rest_p0 = s0_steps(passes[0], 4, None)
        else:
            pending_s0 = s0_steps(passes[0])
            s0_rest_p0 = []

        for ps_i in range(npass):
            pinfo = passes[ps_i]
            last_pass, has_s, tiles, tgs, NT = pinfo["last_pass"], pinfo["has_s"], pinfo["tiles"], pinfo["tgs"], pinfo["NT"]
            T = NT * 128

            s0_rest = s0_rest_p0 if ps_i == 0 else []
            for step in pending_s0:
                step()
            pending_s0 = []
            if ps_i == 0:
                gd_ = [xload0[-1]] if xload0 else []
                P.dma("sp", GBC[:, 0, :], bass.AP(g_post_mix.tensor, 0, [[0, 128], [1, D]]), "gbc0", deps=gd_)
                P.dma("sp", GBC[:, 1, :], bass.AP(g_post_mlp.tensor, 0, [[0, 128], [1, D]]), "gbc1", deps=gd_)

            for b in range(2):
                wxv, wbg, wcg = wgroup([(w_in, 0, 0 * D + b * 512), (w_in, 0, 1 * D + b * 512), (w_in, 0, 2 * D + b * 512)])
                for jj in range(4):
                    j = b * 4 + jj
                    prev_zc = None
                    for (c0, n, kind) in tgs:
                        bxv, bcg, bbg = bank(), bank(), bank()
                        for (bk, wt) in ((bxv, wxv), (bcg, wcg), (bbg, wbg)):
                            for k in range(KC):
                                P.mm(PS[:, bk, 0:n], wt.lhsT(k, jj), hT[:, k, c0:c0 + n],
                                     start=(k == 0), stop=(k == KC - 1))
                        si = rot("cv", 2)
                        xv = XV[si]
                        tcv = TCV[si]
                        if kind == "p":
                            zc = ZC[si]
                            zsl = lambda a, bb, zc=zc: zc[:, a:bb]
                            pview = lambda ap: ap
                            if prev_zc is None:
                                P.copy("dve", zc[:, 0:2], ZST[:, j, :])
                            else:
                                P.copy("dve", zc[:, 0:2], prev_zc[:, 512:514])
                            nn = n
                            xv_v, tcv_v = xv[:, 0:n], tcv[:, 0:n]
                            pcg, pbg, pxv = PS[:, bcg, 0:n], PS[:, bbg, 0:n], PS[:, bxv, 0:n]
                        else:
                            zc = ZSX[rot("sx", 2)]
                            zsl = lambda a, bb, zc=zc: zc[:, :, a:bb]
                            nn = 8
                            r3 = lambda ap: ap.rearrange("p (s t) -> p s t", t=8)
                            xv_v, tcv_v = r3(xv[:, 0:n]), r3(tcv[:, 0:n])
                            pcg, pbg, pxv = r3(PS[:, bcg, 0:n]), r3(PS[:, bbg, 0:n]), r3(PS[:, bxv, 0:n])
                            P.copy("dve", zc[:, :, 0:2], SCT[:, j, :].rearrange("p (s t) -> p s t", t=2))
                        P.copy("act", xv_v, pxv, name="xv")
                        P.tt("dve", zsl(2, 2 + nn), pcg, xv_v, ALU.mult, name="z")
                        P.act(tcv_v, zsl(0, nn), AF.Identity, scale=cs(C_WC + 0 * 8 + j), name="cv0")
                        P.stt("dve", tcv_v, zsl(1, 1 + nn), cs(C_WC + 1 * 8 + j), tcv_v, ALU.mult, ALU.add, name="cv1")
                        P.stt("dve", tcv_v, zsl(2, 2 + nn), cs(C_WC + 2 * 8 + j), tcv_v, ALU.mult, ALU.add, name="cv2")
                        if kind == "p":
                            P.tt("dve", ApT[:, j, c0:c0 + n], pbg, tcv_v, ALU.mult, name="Ap")
                            prev_zc = zc
                            if c0 + n == NPP:
                                P.copy("dve", ZST[:, j, :], zc[:, 512:514])
                        else:
                            P.tt("dve", ApT[:, j, c0:c0 + n].rearrange("p (s t) -> p s t", t=8), pbg, tcv_v, ALU.mult,
                                 name="Ap")
                            P.copy("dve", ZS_[:, j, :].rearrange("p (s t) -> p s t", t=2), zc[:, :, 8:10])
                        if ps_i == 0 and j == 0 and c0 == 0:
                            for step in s0_rest:
                                step()
                            s0_rest = []
                            wgroup(topup=True)
                            P.dma("pool", WPG[:, :, :, :], w_pool_group.rearrange("g (c p) o -> p g c o", p=128), "wpg")

            deferred = []

            def flush_deferred():
                while deferred:
                    deferred.pop(0)[1]()

            def s1b_chunk(j, jj, wu):
                g = j // 2
                w = POOL_W[g]
                prev_uc = None
                for (c0, n, kind) in tgs:
                    bu = bank()
                    for k in range(KC):
                        P.mm(PS[:, bu, 0:n], wu.lhsT(k, jj), hT[:, k, c0:c0 + n],
                             start=(k == 0), stop=(k == KC - 1))
                    pe_ = "pool" if (w > 2 and ((kind == "p" and (c0 // 512) % 2 == 0) or (kind == "s" and j % 2 == 0))) else "dve"
                    if kind == "p":
                        si = rot("pl", 3)
                        uc, pb_, qb_ = UC[si], PB, QB
                        if pe_ == "pool":
                            pb_, qb_ = PBP, QBP
                        sl = lambda buf, a, bb: buf[:, a:bb]
                        nn = n
                        if prev_uc is None:
                            P.copy("dve", uc[:, 0:15], UST[:, j, :])
                        else:
                            P.copy("dve", uc[:, 0:15], prev_uc[:, 512:527])
                        pu = PS[:, bu, 0:n]
                        outv = plT[:, j, c0:c0 + n]
                    else:
                        uc, pb_, qb_ = USX[j % 2], PSX, QSX
                        sl = lambda buf, a, bb: buf[:, :, a:bb]
                        nn = 8
                        pu = PS[:, bu, 0:n].rearrange("p (s t) -> p s t", t=8)
                        outv = plT[:, j, c0:c0 + n].rearrange("p (s t) -> p s t", t=8)
                        for h in range(2):
                            P.copy("dve" if h == 0 else "act", uc[:, h * 8:(h + 1) * 8, 0:15],
                                   SPT[h][:, j, :].rearrange("p (s t) -> p s t", t=15))
                    E = 15 + nn
                    if pe_ == "pool":
                        for d_ in [d for d in deferred if d[0] == kind]:
                            deferred.remove(d_)
                            d_[1]()
                    P.copy("act", sl(uc, 15, E), pu, name="u")
                    if w == 2:
                        P.tt(pe_, outv, sl(uc, 14, E - 1), sl(uc, 15, E), ALU.subtract, name="pool2")
                        sw = None
                    else:
                        P.tt(pe_, sl(pb_, 1, E), sl(uc, 1, E), sl(uc, 0, E - 1), ALU.add, name="s2")
                        P.tt(pe_, sl(qb_, 3, E), sl(pb_, 3, E), sl(pb_, 1, E - 2), ALU.add, name="s4")
                        sw = qb_
                        if w >= 8:
                            P.tt(pe_, sl(pb_, 7, E), sl(qb_, 7, E), sl(qb_, 3, E - 4), ALU.add, name="s8")
                            sw = pb_
                        if w >= 16:
                            P.tt(pe_, sl(qb_, 15, E), sl(pb_, 15, E), sl(pb_, 7, E - 8), ALU.add, name="s16")
                            sw = qb_
                    def fin(w=w, outv=outv, uc=uc, sw=sw, sl=sl, E=E, kind=kind, c0=c0, j=j, g=g):
                        if w > 2:
                            P.stt("dve", outv, sl(uc, 15, E), -float(w), sl(sw, 15, E), ALU.mult, ALU.add, name="pooled")
                        if kind == "p" and ps_i == 0 and c0 == 0:
                            if w == 2:
                                P.memset("dve", plT[:, j, 0:1], 0.0)
                            else:
                                nfix = w - 1
                                P.tt("dve", TMPC[:, 0:nfix], sw[:, 15:15 + nfix], FAC[:, g, 0:nfix], ALU.mult)
                                P.stt("dve", plT[:, j, 0:nfix], uc[:, 15:15 + nfix], -float(w), TMPC[:, 0:nfix],
                                      ALU.mult, ALU.add)
                    if kind == "p":
                        prev_uc = uc
                        if c0 + n == NPP:
                            P.copy("act", UST[:, j, :], uc[:, 512:527])
                    else:
                        P.copy("act", US_[:, j, :].rearrange("p (s t) -> p s t", t=8), uc[:, :, 15:23])
                    if pe_ == "pool":
                        deferred.append((kind, fin))
                    else:
                        fin()
                    while len(deferred) > (1 if pe_ == "pool" else 0):
                        deferred.pop(0)[1]()

            def emit_state_outputs():
                if last_pass:
                    b2 = bank2()
                    for j in range(KC):
                        P.tr(PS[0:2, b2 + j // 4, (j % 4) * 128:(j % 4 + 1) * 128], ZST[:, j, :], IDF[:, :])
                    pz = bass.AP(PS, b2 * 512, [[4096, 2], [1, D]])
                    P.copy("dve", STGS[0][0:2, :], pz)
                    P.dma("sp", ncp[:, :], STGS[0][0:2, :], "so0", is_output=True)
                    b2 = bank2()
                    for j in range(KC):
                        P.tr(PS[0:15, b2 + j // 4, (j % 4) * 128:(j % 4 + 1) * 128], UST[:, j, :], IDF[:, :])
                    pz = bass.AP(PS, b2 * 512, [[4096, 15], [1, D]])
                    P.copy("act", STGS[1][0:15, :], pz)
                    P.dma("sp", npp[:, :], STGS[1][0:15, :], "so1", is_output=True)
                    if has_s:
                        b2 = bank2()
                        for j in range(KC):
                            P.tr(PS[0:32, b2 + j // 4, (j % 4) * 128:(j % 4 + 1) * 128], ZS_[:, j, :], IDF[:, :])
                        pz = bass.AP(PS, b2 * 512, [[4096, 32], [1, D]])
                        P.copy("dve", STGS[2][0:32, :], pz)
                        P.dma("sp", ncs[:, :], STGS[2][0:32, :], "so2", is_output=True)
                        b2 = bank2()
                        for j in range(KC):
                            P.tr(PS[:, b2 + j // 4, (j % 4) * 128:(j % 4 + 1) * 128], US_[:, j, :], IDF[:, :])
                        pz = bass.AP(PS, b2 * 512, [[4096, 128], [1, D]])
                        P.copy("act", STGS[3][:, :], pz)
                        for s in range(NSEQ):
                            P.dma("sp", nps[s * 15 + 7:s * 15 + 15, :], STGS[3][s * 8:(s + 1) * 8, :], "so3", is_output=True)
                        P.dma("sp", nps.rearrange("(s r) d -> s r d", r=15)[:, 0:7, :],
                              st_pool.rearrange("(s r) d -> s r d", r=15)[:, 8:15, :], "so4", is_output=True)


            if has_s:
                S2T = [_view(AR, A_X1 + 5120, F32, [512]), _view(AR, A_X1 + 14080, F32, [512])]
            else:
                S2T = [tmv(14848, F32, [512]), tmv(16896, F32, [512])]

            def s2_chunk(m, jj, wgc, wgp, woc):
                flush_deferred()
                g = m // 2
                for (c0, n, kind) in tgs:
                    bA, bB, bg1, bg2 = bank(), bank(), bank(), bank()
                    for k in range(KC):
                        P.mm(PS[:, bg1, 0:n], wgc.lhsT(k, jj), hT[:, k, c0:c0 + n],
                             start=(k == 0), stop=(k == KC - 1))
                    for k in range(KC):
                        P.mm(PS[:, bg2, 0:n], wgp.lhsT(k, jj), hT[:, k, c0:c0 + n],
                             start=(k == 0), stop=(k == KC - 1))
                    for k in range(KC):
                        P.mm(PS[:, bA, 0:n], woc.lhsT(k, jj), ApT[:, k, c0:c0 + n],
                             start=(k == 0), stop=(k == KC - 1))
                    for c in range(2):
                        P.mm(PS[:, bB, 0:n], WPG[:, g, c, (m % 2) * 128:(m % 2 + 1) * 128],
                             plT[:, 2 * g + c, c0:c0 + n], start=(c == 0), stop=(c == 1))
                    s1, s2 = S2T[0][:, 0:n], S2T[1][:, 0:n]
                    t1, t2 = s1, s2
                    P.act(s1, PS[:, bg1, 0:n], AF.Sigmoid, bias=cs(C_BG + m), name="sig1")
                    P.act(s2, PS[:, bg2, 0:n], AF.Sigmoid, bias=cs(C_BG + 8 + m), name="sig2")
                    P.tt("dve", t1, PS[:, bA, 0:n], s1, ALU.mult, name="mA")
                    P.stt("dve", t2, PS[:, bB, 0:n], cs(C_PS + m), s2, ALU.mult, ALU.mult, name="mB")
                    P.tt("dve", mgT[:, m, c0:c0 + n], t1, t2, ALU.add, name="merge")

            for b in range(2):
                wu, wgc, wgp, woc = wgroup([(w_in, 0, 3 * D + b * 512), (w_in, 0, 4 * D + b * 512),
                                            (w_in, 0, 5 * D + b * 512), (w_out_conv, 0, b * 512)])
                c_ = b * 4
                s1b_chunk(c_ + 0, 0, wu)
                s1b_chunk(c_ + 1, 1, wu)
                s1b_chunk(c_ + 2, 2, wu)
                s2_chunk(c_ + 0, 0, wgc, wgp, woc)
                s1b_chunk(c_ + 3, 3, wu)
                s2_chunk(c_ + 1, 1, wgc, wgp, woc)
                s2_chunk(c_ + 2, 2, wgc, wgp, woc)
                if b == 1:
                    emit_state_outputs()
                s2_chunk(c_ + 3, 3, wgc, wgp, woc)
            flush_deferred()

            wo = wgroup([(w_o, 0, 0), (w_o, 0, 512)])
            s3b, s3x = {}, {}

            def s3_A(i):
                b2 = bank2()
                for nh in range(2):
                    for k in range(KC):
                        P.mm(PS[:, b2 + nh, :], mgT[:, k, i * 128:(i + 1) * 128], wo[nh].full(k),
                             start=(k == 0), stop=(k == KC - 1))
                s3b[i] = b2

            def s3_B1(i):
                pm = bass.AP(PS, s3b.pop(i) * 512, [[4096, 128], [1, D]])
                ss = stat()
                P.act(JUNK, pm, AF.Square, accum_out=ss, name="sq_mix")
                rstd_from(ss)
                xs_ = rot("xt", 2)
                xt = XT[xs_]
                if ps_i == 0:
                    P.stt("dve", xt, pm, ss, GBC[:, 0, :], ALU.mult, ALU.mult, name="mixn")
                else:
                    P.dma("sp", xt, tiles[i][0], "xt%d" % xs_, name="xreload")
                    P.stt("dve", x1[:, i, :], pm, ss, GBC[:, 0, :], ALU.mult, ALU.mult, name="mixn")
                P.tt("pool", x1[:, i, :], x1[:, i, :], xt, ALU.add, name="x1")

            def s3_B2(i):
                s3x[i] = norm_prep(x1[:, i, :])

            def s3_C(i):
                norm_tr(s3x.pop(i), h2T, i * 128, C_GPL)

            s4_early = {"w": None, "done": set()}

            def s4_unit(wu_, f, jj, c0, n):
                bu = bank()
                for k in range(KC):
                    P.mm(PS[:, bu, 0:n], wu_.lhsT(k, jj), h2T[:, k, c0:c0 + n],
                         start=(k == 0), stop=(k == KC - 1))
                rl = RL[rot("rl", 3)][:, 0:n]
                P.act(rl, PS[:, bu, 0:n], AF.Relu, name="relu")
                P.tt("dve", actT[:, f, c0:c0 + n], rl, rl, ALU.mult, name="sqr")

            for u in range(NT + 3):
                if u < NT:
                    s3_A(u)
                if 1 <= u <= NT:
                    s3_B1(u - 1)
                if 2 <= u <= NT + 1:
                    s3_B2(u - 2)
                if u >= 3:
                    s3_C(u - 3)
                u0_ = max(7, NT)
                if u >= u0_:
                    if s4_early["w"] is None:
                        s4_early["w"] = wgroup([(w_up, 0, 0), (w_up, 0, 512)])
                    todo_ = [(bl_, jj_) for bl_ in range(2) for jj_ in range(4)
                             if (bl_ * 4 + jj_, tgs[0][0]) not in s4_early["done"]]
                    if u < NT + 2:
                        todo_ = todo_[:3]
                    for (bl_, jj_) in todo_:
                        s4_unit(s4_early["w"][bl_], bl_ * 4 + jj_, jj_, tgs[0][0], tgs[0][1])
                        s4_early["done"].add((bl_ * 4 + jj_, tgs[0][0]))

            if s4_early["w"] is None:
                s4_early["w"] = wgroup([(w_up, 0, 0), (w_up, 0, 512)])
            for blk in range(DFF // 512):
                if blk < 2:
                    wu_ = s4_early["w"][blk]
                else:
                    wu_, = wgroup([(w_up, 0, blk * 512)])
                for jj in range(4):
                    f = blk * 4 + jj
                    for (c0, n, kind) in tgs:
                        if (f, c0) in s4_early["done"]:
                            continue
                        s4_unit(wu_, f, jj, c0, n)

            groups = [list(range(a, min(a + 3, NT))) for a in range(0, NT, 3)]
            if not last_pass:
                pending_s0 = s0_steps(passes[ps_i + 1])
            unit_i = 0
            bank_pool[0] = [6, 7]
            for G in groups:
                bk = {}
                for gi_, i in enumerate(G):
                    bk[i] = 2 * gi_
                od_ = s5_order(groups.index(G))
                for (nh, kg) in od_:
                    kgs_ = [kg2 for (nh2, kg2) in od_ if nh2 == nh]
                    first_k, last_k = (kg == kgs_[0]), (kg == kgs_[-1])
                    wd, = wgroup([(w_down, kg * 1024, nh * 512)])
                    if unit_i >= 1 and pending_s0:
                        pending_s0.pop(0)()
                    unit_i += 1
                    for i in G:
                        for kk in range(KC):
                            P.mm(PS[:, bk[i] + nh, :], actT[:, kg * 8 + kk, i * 128:(i + 1) * 128], wd.full(kk),
                                 start=(first_k and kk == 0), stop=(last_k and kk == KC - 1),
                                 signal=((last_k or i == G[-1]) and kk == KC - 1))
                for i in G:
                    pm = bass.AP(PS, bk[i] * 512, [[4096, 128], [1, D]])
                    ss = stat()
                    P.act(JUNK, pm, AF.Square, accum_out=ss, name="sq_mlp")
                    rstd_from(ss)
                    t4 = XT[rot("xt", 2)]
                    P.stt("dve", t4, pm, ss, GBC[:, 1, :], ALU.mult, ALU.mult, name="mlpn")
                    P.tt("dve", x1[:, i, :], x1[:, i, :], t4, ALU.add, name="y")
                    P.dma("sp", tiles[i][1], x1[:, i, :], "yo%d" % i, name="ystore", is_output=True)
            for step in pending_s0:
                step()
            pending_s0 = []
            bank_pool[0] = None

        nwait = P.emit(stack)
    return nc


_NC_CACHE = {}


def kernel(**inputs):
    f = lambda a: np.ascontiguousarray(np.asarray(a, dtype=np.float32))
    x_prompt = f(inputs["x_prompt"])
    x_sample = f(inputs["x_sample"])
    state_conv = f(inputs["state_conv"])
    state_pool = f(inputs["state_pool"])
    shared = {
        "g_pre_mix": f(inputs["g_pre_mix"]).reshape(D),
        "w_in": f(inputs["w_in"]).reshape(D, 6 * D),
        "b_gate": f(inputs["b_gate"]).reshape(2 * D),
        "w_conv": f(inputs["w_conv"]).reshape(3, D),
        "w_out_conv": f(inputs["w_out_conv"]).reshape(D, D),
        "w_pool_group": f(inputs["w_pool_group"]).reshape(4, 256, 256),
        "pool_scale": f(inputs["pool_scale"]).reshape(D),
        "w_o": f(inputs["w_o"]).reshape(D, D),
        "g_post_mix": f(inputs["g_post_mix"]).reshape(D),
        "g_pre_mlp": f(inputs["g_pre_mlp"]).reshape(D),
        "w_up": f(inputs["w_up"]).reshape(D, DFF),
        "w_down": f(inputs["w_down"]).reshape(DFF, D),
        "g_post_mlp": f(inputs["g_post_mlp"]).reshape(D),
    }
    in_maps = []
    for c in range(NCORES):
        m = dict(shared)
        m["x_p"] = x_prompt[c]
        m["x_s"] = x_sample[16 * c:16 * (c + 1)].reshape(128, D)
        m["st_conv"] = state_conv[0, 16 * c:16 * (c + 1)].reshape(32, D)
        m["st_pool"] = state_pool[0, 16 * c:16 * (c + 1)].reshape(240, D)
        in_maps.append(m)
    if "nc" not in _NC_CACHE:
        _NC_CACHE["nc"] = build_program()
    nc = _NC_CACHE["nc"]
    res = run_bass_kernel_spmd(nc, in_maps, core_ids=list(range(NCORES)))
    R = res.results
    y_prompt = np.stack([R[c]["y_p"] for c in range(NCORES)], axis=0)
    y_sample = np.concatenate([R[c]["y_s"].reshape(16, 8, D) for c in range(NCORES)], axis=0)
    ncp = np.stack([R[c]["ncp"] for c in range(NCORES)], axis=0)[None]
    npp = np.stack([R[c]["npp"] for c in range(NCORES)], axis=0)[None]
    ncs = np.concatenate([R[c]["ncs"].reshape(16, 2, D) for c in range(NCORES)], axis=0)[None]
    nps = np.concatenate([R[c]["nps"].reshape(16, 15, D) for c in range(NCORES)], axis=0)[None]
    return (y_prompt.astype(np.float32), y_sample.astype(np.float32), ncp.astype(np.float32),
            npp.astype(np.float32), ncs.astype(np.float32), nps.astype(np.float32))
```

```python
import numpy as np
from contextlib import ExitStack
import concourse.bass as bass
import concourse.mybir as mybir
from concourse.bass_utils import run_bass_kernel_spmd

F32 = mybir.dt.float32
BF16 = mybir.dt.bfloat16
ALU = mybir.AluOpType
AF = mybir.ActivationFunctionType
SZ = {F32: 4, BF16: 2}

D = 1024
KC = 8
DFF = 4096
EPS = 1e-6
NCORES = 8
POOL_W = (2, 4, 8, 16)


def _prod(xs):
    r = 1
    for x in xs:
        r *= int(x)
    return r


class _Op:
    __slots__ = ("eng", "fn", "deps", "signal", "pos", "dma", "name", "sig", "wait_val")

    def __init__(self, eng, fn, name):
        self.eng = eng
        self.fn = fn
        self.deps = set()
        self.signal = True
        self.pos = None
        self.dma = None
        self.name = name
        self.sig = None
        self.wait_val = None


class Prog:
    ENG = ("pe", "act", "dve", "pool", "sp")

    def __init__(self, nc):
        self.nc = nc
        self.engs = {"pe": nc.tensor, "act": nc.scalar, "dve": nc.vector,
                     "pool": nc.gpsimd, "sp": nc.sync}
        self.streams = {e: [] for e in self.ENG}
        self.recs = {}
        self.dma_cnt = {}
        self.out_dmas = []

    @staticmethod
    def _iv(ap):
        t = ap.tensor
        if type(t).__name__.startswith("DRam"):
            return None
        esz = SZ[ap.dtype]
        pbytes = _prod(t.shape[1:]) * SZ[t.dtype]
        off = int(ap.offset) * esz
        lo = off % pbytes
        dims = list(ap.ap)
        p0 = off // pbytes
        p1 = p0 + ((int(dims[0][1]) - 1) * abs(int(dims[0][0])) * esz) // pbytes + 1
        fd = sorted([(abs(int(st)) * esz, int(cnt)) for (st, cnt) in dims[1:] if int(cnt) > 1], key=lambda d: -d[0])

        def rec(base, ds):
            if not ds:
                return [(base, base + esz)]
            inner = esz + sum((c - 1) * s_ for s_, c in ds[1:])
            s0, c0 = ds[0]
            if s0 >= 2 * inner and c0 <= 32:
                out = []
                for i in range(c0):
                    out += rec(base + i * s0, ds[1:])
                return out
            return [(base, base + esz + sum((c - 1) * s_ for s_, c in ds))]
        ivs = rec(lo, fd)
        if len(ivs) > 64:
            ivs = [(lo, lo + esz + sum((c - 1) * s_ for s_, c in fd))]
        return t.name, ivs, p0, p1

    def _acc(self, op, ap, is_write):
        iv = self._iv(ap)
        if iv is None:
            return
        key, ivs, p0, p1 = iv
        for (lo, hi) in ivs:
            self._acc1(op, key, lo, hi, p0, p1, is_write)

    def _acc1(self, op, key, lo, hi, p0, p1, is_write):
        recs = self.recs.get(key, [])
        new = []
        for r in recs:
            rlo, rhi, rop, rw, rp0, rp1 = r
            if rhi <= lo or rlo >= hi or rp1 <= p0 or rp0 >= p1:
                new.append(r)
                continue
            if (is_write or rw) and rop is not op:
                op.deps.add(rop)
            cov = rlo >= lo and rhi <= hi and rp0 >= p0 and rp1 <= p1
            if is_write and cov:
                continue
            if (not is_write) and (not rw) and rop.dma is None and op.dma is None \
                    and rop.eng == op.eng and cov:
                continue
            new.append(r)
        new.append((lo, hi, op, is_write, p0, p1))
        self.recs[key] = new

    def add(self, eng, fn, reads=(), writes=(), signal=True, name="", dma=None):
        op = _Op(eng, fn, name)
        op.signal = signal
        op.dma = dma
        for ap in reads:
            if ap is not None and not isinstance(ap, (int, float)):
                self._acc(op, ap, False)
        for ap in writes:
            self._acc(op, ap, True)
        op.pos = len(self.streams[eng])
        self.streams[eng].append(op)
        return op

    def dma(self, q, out, in_, semkey, name="", is_output=False, deps=(), **kw):
        cnt = self.dma_cnt.get(semkey, 0) + 16
        self.dma_cnt[semkey] = cnt
        op = self.add(q, lambda e: e.dma_start(out=out, in_=in_, **kw), reads=[in_], writes=[out],
                      name=name, dma=(semkey, cnt))
        op.deps.update(deps)
        if is_output:
            self.out_dmas.append(op)
        return op

    def mm(self, out, lhsT, rhs, start, stop, signal=None):
        sig = stop if signal is None else signal
        return self.add("pe", lambda e: e.matmul(out, lhsT, rhs, start=start, stop=stop),
                        reads=[lhsT, rhs], writes=[out], signal=sig, name="mm")

    def tr(self, out, in_, ident, signal=True):
        return self.add("pe", lambda e: e.transpose(out, in_, ident), reads=[in_, ident], writes=[out],
                        signal=signal, name="tr")

    def act(self, out, in_, func, bias=None, scale=None, accum_out=None, name="act"):
        kw = {}
        if bias is not None:
            kw["bias"] = bias
        if scale is not None:
            kw["scale"] = scale
        if accum_out is not None:
            kw["accum_out"] = accum_out
        rd = [in_] + [x for x in (bias, scale) if x is not None and not isinstance(x, (int, float))]
        wr = [out] + ([accum_out] if accum_out is not None else [])
        return self.add("act", lambda e: e.activation(out, in_, func, **kw), reads=rd, writes=wr, name=name)

    def tt(self, eng, out, in0, in1, op, name="tt"):
        return self.add(eng, lambda e: e.tensor_tensor(out, in0, in1, op), reads=[in0, in1], writes=[out], name=name)

    def ts(self, eng, out, in0, s1, s2, op0, op1=None, name="ts"):
        rd = [in0] + [x for x in (s1, s2) if x is not None and not isinstance(x, (int, float))]
        if op1 is None:
            return self.add(eng, lambda e: e.tensor_scalar(out, in0, s1, s2, op0), reads=rd, writes=[out], name=name)
        return self.add(eng, lambda e: e.tensor_scalar(out, in0, s1, s2, op0, op1), reads=rd, writes=[out], name=name)

    def stt(self, eng, out, in0, scalar, in1, op0, op1, name="stt"):
        rd = [in0, in1] + ([scalar] if not isinstance(scalar, (int, float)) else [])
        return self.add(eng, lambda e: e.scalar_tensor_tensor(out, in0, scalar, in1, op0, op1),
                        reads=rd, writes=[out], name=name)

    def copy(self, eng, out, in_, name="copy"):
        if eng == "act":
            return self.act(out, in_, AF.Identity, name=name)
        return self.add(eng, lambda e: e.tensor_copy(out, in_), reads=[in_], writes=[out], name=name)

    def memset(self, eng, out, val, name="memset"):
        return self.add(eng, lambda e: e.memset(out, val), writes=[out], name=name)

    def emit(self, stack):
        nc = self.nc
        sems = {}
        for e in ("pe", "act", "dve", "pool"):
            sems[e] = stack.enter_context(nc.semaphore("s_" + e))
        for k in self.dma_cnt:
            sems[("dma", k)] = stack.enter_context(nc.semaphore("d_" + str(k)))
        fin = _Op("sp", None, "final")
        fin.deps = set(self.out_dmas)
        self.streams["sp"].append(fin)
        for e in self.ENG:
            c = 0
            for op in self.streams[e]:
                if op.dma is None and op.fn is not None and op.signal:
                    c += 1
                    op.sig = c
            nxt = None
            for op in reversed(self.streams[e]):
                if op.dma is None and op.fn is not None:
                    if op.signal:
                        nxt = op.sig
                    op.wait_val = nxt
        nwait = 0
        for e in self.ENG:
            eng = self.engs[e]
            waited = {}
            for op in self.streams[e]:
                need = {}
                for d in op.deps:
                    if d.dma is not None:
                        key = ("dma", d.dma[0])
                        val = d.dma[1]
                    else:
                        if d.eng == "pe" and e == "pe":
                            continue
                        key = d.eng
                        val = d.wait_val
                        assert val is not None, (d.name, op.name)
                    if waited.get(key, 0) >= val:
                        continue
                    if need.get(key, 0) < val:
                        need[key] = val
                for key, val in need.items():
                    waited[key] = val
                    eng.wait_ge(sems[key], val)
                    nwait += 1
                if op.fn is not None:
                    ins = op.fn(eng)
                    if op.dma is not None:
                        ins.then_inc(sems[("dma", op.dma[0])], 16)
                    elif op.signal:
                        ins.then_inc(sems[e], 1)
        return nwait


def _view(t, boff, dt, shape):
    n = _prod(shape)
    esz = SZ[dt]
    assert boff % 4 == 0
    e0 = boff // 2
    e1 = e0 + n * esz // 2
    ap = t[:, e0:e1]
    if dt != BF16:
        ap = ap.bitcast(dt)
    if len(shape) == 2:
        ap = ap.rearrange("p (a b) -> p a b", b=shape[1])
    elif len(shape) == 3:
        ap = ap.rearrange("p (a b c) -> p a b c", b=shape[1], c=shape[2])
    return ap


def build_program(ntg=2, with_sample=True, npass=2):
    nc = bass.Bass("TRN2", target_bir_lowering=False)
    P = Prog(nc)
    NPP = 512 * ntg
    NPT = NPP * npass
    NSEQ = 16
    NST = NSEQ * 8
    TMAX = NPP + (NST if with_sample else 0)

    def din(name, shape):
        return nc.dram_tensor(name, list(shape), F32, kind="ExternalInput").ap()

    def dout(name, shape):
        return nc.dram_tensor(name, list(shape), F32, kind="ExternalOutput").ap()

    x_p = din("x_p", [NPT, D])
    x_s = din("x_s", [NST, D])
    st_conv = din("st_conv", [NSEQ * 2, D])
    st_pool = din("st_pool", [NSEQ * 15, D])
    g_pre_mix = din("g_pre_mix", [D])
    w_in = din("w_in", [D, 6 * D])
    b_gate = din("b_gate", [2 * D])
    w_conv = din("w_conv", [3, D])
    w_out_conv = din("w_out_conv", [D, D])
    w_pool_group = din("w_pool_group", [4, 256, 256])
    pool_scale = din("pool_scale", [D])
    w_o = din("w_o", [D, D])
    g_post_mix = din("g_post_mix", [D])
    g_pre_mlp = din("g_pre_mlp", [D])
    w_up = din("w_up", [D, DFF])
    w_down = din("w_down", [DFF, D])
    g_post_mlp = din("g_post_mlp", [D])

    y_p = dout("y_p", [NPT, D])
    y_s = dout("y_s", [NST, D])
    ncp = dout("ncp", [2, D])
    npp = dout("npp", [15, D])
    ncs = dout("ncs", [NSEQ * 2, D])
    nps = dout("nps", [NSEQ * 15, D])

    wd_bf = nc.dram_tensor("wd_bf", [8, 128, KC * 512], BF16, kind="Internal").ap()

    stack = ExitStack()
    with stack:
        NT_MAX = TMAX // 128
        A_ACT = 0
        A_HT = 0
        A_AP = KC * TMAX * 2
        A_PL = 2 * KC * TMAX * 2
        A_MG = 3 * KC * TMAX * 2
        A_X1 = 32 * TMAX * 2
        A_H2 = A_X1 + NT_MAX * 4096
        A_END = A_H2 + KC * TMAX * 2
        AR = stack.enter_context(nc.sbuf_tensor("AR", [128, A_END // 2], BF16))
        NSLOT = 6
        WS = stack.enter_context(nc.sbuf_tensor("WS", [128, NSLOT * 4096], BF16))
        WPG = stack.enter_context(nc.sbuf_tensor("WPG", [128, 4, 2, 256], BF16))
        TM_BYTES = 20096
        TM = stack.enter_context(nc.sbuf_tensor("TM", [128, TM_BYTES // 2], BF16))
        GBC = stack.enter_context(nc.sbuf_tensor("GBC", [128, 2, D], F32))
        IDB = stack.enter_context(nc.sbuf_tensor("IDB", [128, 128], BF16))
        IDF = stack.enter_context(nc.sbuf_tensor("IDF", [128, 128], F32))
        CS = stack.enter_context(nc.sbuf_tensor("CS", [128, 96], F32))
        STT_ = stack.enter_context(nc.sbuf_tensor("STAT", [128, 64], F32))
        ZST = stack.enter_context(nc.sbuf_tensor("ZST", [128, KC, 2], F32))
        UST = stack.enter_context(nc.sbuf_tensor("UST", [128, KC, 15], F32))
        FAC = stack.enter_context(nc.sbuf_tensor("FAC", [128, 5, 16], F32))
        PS = stack.enter_context(nc.psum_tensor("PS", [128, 8, 512], F32))

        hT = _view(AR, A_H2, BF16, [KC, TMAX])
        ApT = _view(AR, A_AP, BF16, [KC, TMAX])
        plT = _view(AR, A_PL, BF16, [KC, TMAX])
        mgT = _view(AR, A_MG, BF16, [KC, TMAX])
        actT = _view(AR, A_ACT, BF16, [32, TMAX])
        x1 = _view(AR, A_X1, F32, [NT_MAX, D])
        h2T = _view(AR, A_H2, BF16, [KC, TMAX])
        US_ = _view(AR, A_X1, F32, [KC, 128])
        ZS_ = _view(AR, A_X1 + KC * 128 * 4, F32, [KC, 32])

        C_GPM = 0
        C_GPL = 8
        C_BG = 16
        C_WC = 32
        C_PS = 56
        C_EPS = 64

        def cs(col, n=1):
            return CS[:, col:col + n]

        stat_ctr = [0]

        def stat():
            i = stat_ctr[0] % 64
            stat_ctr[0] += 1
            return STT_[:, i:i + 1]

        bank_ctr = [0]
        bank_pool = [None]

        def bank():
            if bank_pool[0] is not None:
                pool_ = bank_pool[0]
                b = pool_[bank_ctr[0] % len(pool_)]
                bank_ctr[0] += 1
                return b
            b = bank_ctr[0] % 8
            bank_ctr[0] += 1
            return b

        def bank2():
            if bank_ctr[0] % 2:
                bank_ctr[0] += 1
            b = bank_ctr[0] % 8
            bank_ctr[0] += 2
            return b

        slot_ctr = [0]

        def wslot():
            s = slot_ctr[0] % NSLOT
            slot_ctr[0] += 1
            return s

        def s5_order(gi_):
            od = [(nh, kg) for nh in range(2) for kg in range(4)]
            return od if gi_ % 2 == 0 else od[::-1]

        plan = []
        for ps_i in range(npass):
            nt_ = NPP // 128 + (1 if (with_sample and ps_i == npass - 1) else 0)
            for b in range(2):
                plan.append([(w_in, 0, 0 * D + b * 512), (w_in, 0, 1 * D + b * 512), (w_in, 0, 2 * D + b * 512)])
            for b in range(2):
                plan.append([(w_in, 0, 3 * D + b * 512), (w_in, 0, 4 * D + b * 512), (w_in, 0, 5 * D + b * 512),
                             (w_out_conv, 0, b * 512)])
            plan.append([(w_o, 0, 0), (w_o, 0, 512)])
            plan.append([(w_up, 0, 0), (w_up, 0, 512)])
            for blk in range(2, DFF // 512):
                plan.append([(w_up, 0, blk * 512)])
            for gi_, a in enumerate(range(0, nt_, 3)):
                for (nh, kg) in s5_order(gi_):
                    plan.append([(w_down, kg * 1024, nh * 512)])
        entries = [e for g in plan for e in g]
        gfirst = []
        gof = []
        acc_ = 0
        for gi_, g in enumerate(plan):
            gfirst.append(acc_)
            acc_ += len(g)
            gof += [gi_] * len(g)
        keyof = [(id(W_), r_, c_) for (W_, r_, c_) in entries]
        slot_of, need_load, free_after = [], [], []
        content = [None] * NSLOT
        last_use = [-1] * NSLOT
        for e, k_ in enumerate(keyof):
            if k_ in content and e >= 3 and k_[0] == id(w_down):
                sl_ = content.index(k_)
                slot_of.append(sl_)
                need_load.append(False)
                free_after.append(-1)
            else:
                same_grp = [slot_of[e2] for e2 in range(gfirst[gof[e]], e)]
                best, best_key = None, None
                for sl_ in range(NSLOT):
                    if sl_ in same_grp:
                        continue
                    if content[sl_] is None or content[sl_][0] != id(w_down):
                        nu = 10 ** 9
                    else:
                        nu = 10 ** 9
                        for e2 in range(e + 1, min(len(keyof), e + 24)):
                            if keyof[e2] == content[sl_]:
                                nu = e2
                                break
                    cand = (nu, -last_use[sl_])
                    if best_key is None or cand > best_key:
                        best, best_key = sl_, cand
                slot_of.append(best)
                need_load.append(True)
                free_after.append(gof[last_use[best]] if last_use[best] >= 0 else -1)
                content[best] = k_
            last_use[slot_of[e]] = e
        wstate = {"g": 0, "issued": 0, "conv": 0}
        conv_ops = []

        class WV:
            def __init__(self, e, split):
                self.split = split
                if split:
                    self.v = _view(WS, slot_of[e] * 8192, BF16, [4, KC, 128])
                else:
                    self.v = _view(WS, slot_of[e] * 8192, BF16, [KC, 512])

            def lhsT(self, k, jj):
                return self.v[:, jj, k, :] if self.split else self.v[:, k, jj * 128:(jj + 1) * 128]

            def full(self, k):
                assert not self.split
                return self.v[:, k, :]

        def wslot_view(e):
            return WV(e, e < 3)

        def wgroup(expect=None, topup=False):
            if topup:
                wstate["g"] -= 1
            g = wstate["g"]
            wstate["g"] += 1
            first, cnt = gfirst[g], len(plan[g])
            if expect is not None:
                assert [(a is b_, r, c) for (a, r, c), (b_, r2, c2) in zip(plan[g], expect)] and \
                    all(a is b_ and r == r2 and c == c2 for (a, r, c), (b_, r2, c2) in zip(plan[g], expect)), g
            if 4 <= g <= 7 and not topup:
                for c in (2 * (g - 4), 2 * (g - 4) + 1):
                    kg_, nh_ = c // 2, c % 2
                    conv_ops.append(P.dma("pool", wd_bf[c, :, :].rearrange("p (k c) -> p k c", c=512),
                                          w_down[kg_ * 1024:(kg_ + 1) * 1024, nh_ * 512:(nh_ + 1) * 512].rearrange(
                                              "(k p) c -> p k c", p=128),
                                          "wdc", name="wdconv"))
                    if c == 7:
                        for op_ in conv_ops:
                            op_.dma = ("wdc", P.dma_cnt["wdc"])
            while wstate["issued"] < len(entries):
                e = wstate["issued"]
                own = e < first + cnt
                ahead_ok = (free_after[e] < g) and not (g == 0 and not topup)
                if not (own or ahead_ok):
                    break
                if own:
                    assert free_after[e] < g
                W, r0, c0 = entries[e]
                if not need_load[e]:
                    pass
                elif e < 3:
                    if e == 0:
                        for jj_ in range(4):
                            for e_ in range(3):
                                W_, r_0, c_0 = entries[e_]
                                src = W_[r_0:r_0 + KC * 128, c_0 + jj_ * 128:c_0 + (jj_ + 1) * 128].rearrange(
                                    "(k p) c -> p k c", p=128)
                                P.dma("pool", wslot_view(e_).v[:, jj_, :, :], src, "wi%d_%d" % (e_, jj_), name="wload0",
                                      deps=([xload0[min(3, len(xload0) - 1)]] if (jj_ > 0 and xload0) else []))
                elif W is w_down:
                    src = wd_bf[(r0 // 1024) * 2 + (c0 // 512), :, :].rearrange("p (k c) -> p k c", c=512)
                    P.dma("pool", wslot_view(e).v, src, "w%d" % slot_of[e], name="wload_bf", deps=conv_ops)
                else:
                    src = W[r0:r0 + KC * 128, c0:c0 + 512].rearrange("(k p) c -> p k c", p=128)
                    P.dma("pool", wslot_view(e).v, src, "w%d" % slot_of[e], name="wload")
                wstate["issued"] += 1
            return [wslot_view(first + i) for i in range(cnt)]

        CST = _view(TM, 8192, F32, [128])
        r_ = lambda v: v.rearrange("(k p) -> k p", p=128)
        P.dma("act", CST[0:8, :], r_(g_pre_mix), "const")
        P.dma("act", CST[8:16, :], r_(g_pre_mlp), "const")
        P.dma("act", CST[16:32, :], r_(b_gate), "const")
        P.dma("act", CST[32:56, :], w_conv.rearrange("w (k p) -> (w k) p", p=128), "const")
        P.dma("act", CST[56:64, :], r_(pool_scale), "const")
        tot = P.dma_cnt["const"]
        for op in P.streams["act"]:
            if op.dma is not None and op.dma[0] == "const":
                op.dma = ("const", tot)

        P.memset("pool", IDF[:, :], 0.0)
        P.add("pool", lambda e: e.affine_select(out=IDF[:, :], in_=IDF[:, :], compare_op=ALU.not_equal,
                                                 fill=1.0, base=0, pattern=[[-1, 128]], channel_multiplier=1),
              reads=[IDF[:, :]], writes=[IDF[:, :]], name="ident")
        P.copy("dve", IDB[:, :], IDF[:, :])
        P.act(STT_[:, 63:64], IDF[:, 0:1], AF.Square, name="act_table_warm")
        bt0 = 0
        P.tr(PS[:, bt0, 0:64], CST[0:64, :], IDF[0:64, 0:64])
        P.copy("dve", CS[:, 0:64], PS[:, bt0, 0:64])
        P.add("pool", lambda e: e.iota(FAC[:, 4, :], pattern=[[1, 16]], base=1, channel_multiplier=0,
                                       allow_small_or_imprecise_dtypes=True),
              writes=[FAC[:, 4, :]], name="iota")
        for g, w in enumerate(POOL_W):
            P.ts("dve", FAC[:, g, :], FAC[:, 4, :], float(w), None, ALU.min)
            P.add("dve", lambda e, g=g: e.reciprocal(FAC[:, g, :], FAC[:, g, :]),
                  reads=[FAC[:, g, :]], writes=[FAC[:, g, :]], name="recip")
            P.ts("dve", FAC[:, g, :], FAC[:, g, :], float(w), None, ALU.mult)
            P.ts("dve", cs(C_PS + 2 * g, 2), cs(C_PS + 2 * g, 2), 1.0 / w, None, ALU.mult)
        P.memset("dve", cs(C_EPS), EPS)
        P.memset("dve", ZST[:, :, :], 0.0)
        P.memset("dve", UST[:, :, :], 0.0)

        def tmv(off, dt, shape):
            return _view(TM, off, dt, shape)

        XT = [tmv(0 + i * 4096, F32, [D]) for i in range(2)]
        XN = [tmv(8192 + i * 2048, BF16, [D]) for i in range(2)]
        JUNK = tmv(12288, BF16, [D])
        XV = [tmv(0 + i * 2048, F32, [512]) for i in range(2)]
        ZC = [tmv(4096 + i * 2064, F32, [514]) for i in range(2)]
        TCV = [tmv(8224 + i * 2048, F32, [512]) for i in range(2)]
        ZSX = [tmv(12320 + i * 640, F32, [16, 10]) for i in range(2)]
        UC = [tmv(i * 2112, F32, [527]) for i in range(3)]
        PB = tmv(6336, F32, [527])
        QB = tmv(8448, F32, [527])
        PBP = tmv(10560, F32, [527])
        QBP = tmv(12672, F32, [527])
        TMPC = tmv(14784, F32, [16])
        USX = [_view(AR, A_X1 + 2 * 4096 + i * 1472, F32, [16, 23]) for i in range(2)]
        PSX = _view(AR, A_X1 + 2 * 4096 + 2944, F32, [16, 23])
        QSX = _view(AR, A_X1 + 2 * 4096 + 4416, F32, [16, 23])
        SPT = [_view(AR, A_X1 + (NT_MAX - 1) * 4096, F32, [KC, 120]), tmv(14848, F32, [KC, 120])]
        SCT = tmv(18688, F32, [KC, 32])
        XNF = tmv(8192, F32, [D])
        SG1 = [tmv(0 + i * 2048, F32, [512]) for i in range(2)]
        SG2 = [tmv(4096 + i * 2048, F32, [512]) for i in range(2)]
        MT1 = [tmv(8192 + i * 2048, F32, [512]) for i in range(2)]
        MT2 = [tmv(12288 + i * 2048, F32, [512]) for i in range(2)]
        RL = [tmv(i * 2048, F32, [512]) for i in range(3)]
        if NT_MAX >= 9:
            STGS = [_view(AR, A_X1 + (4 + i) * 4096, F32, [D]) for i in range(4)]
        else:
            STGS = [tmv(i * 4096, F32, [D]) for i in range(4)]

        ctr = {"xt": 0, "xn": 0, "t4k": 0, "cv": 0, "pl": 0, "sg": 0, "rl": 0, "sx": 0}

        def rot(key, n):
            i = ctr[key] % n
            ctr[key] += 1
            return i

        def rstd_from(ss):
            P.act(ss, ss, AF.Sqrt, bias=cs(C_EPS), scale=1.0 / D, name="sqrt")
            P.add("dve", lambda e: e.reciprocal(ss, ss), reads=[ss], writes=[ss], name="rstd")
            return ss

        def make_pass(ps_i):
            last_pass = ps_i == npass - 1
            has_s = with_sample and last_pass
            tiles = []
            for i in range(NPP // 128):
                r0 = ps_i * NPP + i * 128
                tiles.append((x_p[r0:r0 + 128, :], y_p[r0:r0 + 128, :]))
            if has_s:
                tiles.append((x_s[:, :], y_s[:, :]))
            tgs = [(g * 512, 512, "p") for g in range(ntg)]
            if has_s:
                tgs.append((NPP, 128, "s"))
            return dict(idx=ps_i, last_pass=last_pass, has_s=has_s, tiles=tiles, tgs=tgs, NT=len(tiles))

        passes = [make_pass(i) for i in range(npass)]

        def norm_prep(src_tok, xn_eng="act"):
            ss = stat()
            P.act(JUNK, src_tok, AF.Square, accum_out=ss, name="sq")
            rstd_from(ss)
            xn = XN[rot("xn", 2)]
            if xn_eng == "act":
                P.act(xn, src_tok, AF.Identity, scale=ss, name="xn")
            else:
                P.ts("dve", xn, src_tok, ss, None, ALU.mult, name="xn")
            return xn

        def norm_tr(xn, dstT, col0, gcol):
            b = bank()
            pb = PS[:, b, :].bitcast(BF16)
            for k in range(KC):
                P.tr(pb[:, k * 128:(k + 1) * 128], xn[:, k * 128:(k + 1) * 128], IDB[:, :], signal=(k == KC - 1))
            gb = bass.AP(CS, gcol, [[96, 128], [1, KC], [0, 128]])
            P.tt("dve", dstT[:, :, col0:col0 + 128], pb.rearrange("p (k t) -> p k t", t=128), gb, ALU.mult,
                 name="evacT")

        def s0_steps(pinfo, lo=0, hi=None, with_state=True):
            st = {}
            steps = []
            NT_ = pinfo["NT"]
            hi = NT_ if hi is None else hi

            def mk(u):
                def step():
                    if lo + 2 <= u:
                        norm_tr(st.pop(u - 2), hT, (u - 2) * 128, C_GPM)
                    if u < hi:
                        if pinfo["idx"] == 0:
                            if u == 0:
                                for i_ in range(NT_):
                                    xload0.append(P.dma("sp", x1[:, i_, :], pinfo["tiles"][i_][0], "x0_%d" % i_, name="xload0"))
                            xt = x1[:, u, :]
                        else:
                            xs_ = rot("xt", 2)
                            xt = XT[xs_]
                            P.dma("sp", xt, pinfo["tiles"][u][0], "xt%d" % xs_, name="xload")
                        st[u] = norm_prep(xt, "dve" if pinfo["idx"] == 0 else "act")
                return step
            for u in range(lo, hi + 2):
                steps.append(mk(u))
            if pinfo["has_s"] and with_state:
                def ld():
                    P.dma("sp", XT[0][0:120, :], st_pool[0:120, :], "xt0", name="stload")
                    P.dma("sp", XT[1][0:120, :], st_pool[120:240, :], "xt1", name="stload")
                    P.dma("sp", XNF[0:32, :], st_conv[:, :], "stc", name="stload")

                def trs(which):
                    def f():
                        ci = 0
                        for j in range(KC):
                            for (src, nrow, dst) in which(j):
                                bt = bank()
                                P.tr(PS[:, bt, 0:nrow], src, IDF[0:nrow, 0:nrow])
                                P.copy("dve" if ci % 2 == 0 else "act", dst, PS[:, bt, 0:nrow])
                                ci += 1
                    return f
                steps.append(ld)
                steps.append(lambda: None)
                steps.append(trs(lambda j: [(XNF[0:32, j * 128:(j + 1) * 128], 32, SCT[:, j, :]),
                                            (XT[0][0:120, j * 128:(j + 1) * 128], 120, SPT[0][:, j, :])]))
                steps.append(trs(lambda j: [(XT[1][0:120, j * 128:(j + 1) * 128], 120, SPT[1][:, j, :])]))
            return steps

        xload0 = []
        if passes[0]["NT"] >= 8:
            pending_s0 = s0_steps(passes[0], 0, 4)
            s0_rest_p0 = s0_steps(passes[0], 4, None)
        else:
            pending_s0 = s0_steps(passes[0])
            s0_rest_p0 = []

        for ps_i in range(npass):
            pinfo = passes[ps_i]
            last_pass, has_s, tiles, tgs, NT = pinfo["last_pass"], pinfo["has_s"], pinfo["tiles"], pinfo["tgs"], pinfo["NT"]
            T = NT * 128

            s0_rest = s0_rest_p0 if ps_i == 0 else []
            for step in pending_s0:
                step()
            pending_s0 = []
            if ps_i == 0:
                gd_ = [xload0[-1]] if xload0 else []
                P.dma("sp", GBC[:, 0, :], bass.AP(g_post_mix.tensor, 0, [[0, 128], [1, D]]), "gbc0", deps=gd_)
                P.dma("sp", GBC[:, 1, :], bass.AP(g_post_mlp.tensor, 0, [[0, 128], [1, D]]), "gbc1", deps=gd_)

            for b in range(2):
                wxv, wbg, wcg = wgroup([(w_in, 0, 0 * D + b * 512), (w_in, 0, 1 * D + b * 512), (w_in, 0, 2 * D + b * 512)])
                for jj in range(4):
                    j = b * 4 + jj
                    prev_zc = None
                    for (c0, n, kind) in tgs:
                        bxv, bcg, bbg = bank(), bank(), bank()
                        for (bk, wt) in ((bxv, wxv), (bcg, wcg), (bbg, wbg)):
                            for k in range(KC):
                                P.mm(PS[:, bk, 0:n], wt.lhsT(k, jj), hT[:, k, c0:c0 + n],
                                     start=(k == 0), stop=(k == KC - 1))
                        si = rot("cv", 2)
                        xv = XV[si]
                        tcv = TCV[si]
                        if kind == "p":
                            zc = ZC[si]
                            zsl = lambda a, bb, zc=zc: zc[:, a:bb]
                            pview = lambda ap: ap
                            if prev_zc is None:
                                P.copy("dve", zc[:, 0:2], ZST[:, j, :])
                            else:
                                P.copy("dve", zc[:, 0:2], prev_zc[:, 512:514])
                            nn = n
                            xv_v, tcv_v = xv[:, 0:n], tcv[:, 0:n]
                            pcg, pbg, pxv = PS[:, bcg, 0:n], PS[:, bbg, 0:n], PS[:, bxv, 0:n]
                        else:
                            zc = ZSX[rot("sx", 2)]
                            zsl = lambda a, bb, zc=zc: zc[:, :, a:bb]
                            nn = 8
                            r3 = lambda ap: ap.rearrange("p (s t) -> p s t", t=8)
                            xv_v, tcv_v = r3(xv[:, 0:n]), r3(tcv[:, 0:n])
                            pcg, pbg, pxv = r3(PS[:, bcg, 0:n]), r3(PS[:, bbg, 0:n]), r3(PS[:, bxv, 0:n])
                            P.copy("dve", zc[:, :, 0:2], SCT[:, j, :].rearrange("p (s t) -> p s t", t=2))
                        P.copy("act", xv_v, pxv, name="xv")
                        P.tt("dve", zsl(2, 2 + nn), pcg, xv_v, ALU.mult, name="z")
                        P.act(tcv_v, zsl(0, nn), AF.Identity, scale=cs(C_WC + 0 * 8 + j), name="cv0")
                        P.stt("dve", tcv_v, zsl(1, 1 + nn), cs(C_WC + 1 * 8 + j), tcv_v, ALU.mult, ALU.add, name="cv1")
                        P.stt("dve", tcv_v, zsl(2, 2 + nn), cs(C_WC + 2 * 8 + j), tcv_v, ALU.mult, ALU.add, name="cv2")
                        if kind == "p":
                            P.tt("dve", ApT[:, j, c0:c0 + n], pbg, tcv_v, ALU.mult, name="Ap")
                            prev_zc = zc
                            if c0 + n == NPP:
                                P.copy("dve", ZST[:, j, :], zc[:, 512:514])
                        else:
                            P.tt("dve", ApT[:, j, c0:c0 + n].rearrange("p (s t) -> p s t", t=8), pbg, tcv_v, ALU.mult,
                                 name="Ap")
                            P.copy("dve", ZS_[:, j, :].rearrange("p (s t) -> p s t", t=2), zc[:, :, 8:10])
                        if ps_i == 0 and j == 0 and c0 == 0:
                            for step in s0_rest:
                                step()
                            s0_rest = []
                            wgroup(topup=True)
                            P.dma("pool", WPG[:, :, :, :], w_pool_group.rearrange("g (c p) o -> p g c o", p=128), "wpg")

            deferred = []

            def flush_deferred():
                while deferred:
                    deferred.pop(0)[1]()

            def s1b_chunk(j, jj, wu):
                g = j // 2
                w = POOL_W[g]
                prev_uc = None
                for (c0, n, kind) in tgs:
                    bu = bank()
                    for k in range(KC):
                        P.mm(PS[:, bu, 0:n], wu.lhsT(k, jj), hT[:, k, c0:c0 + n],
                             start=(k == 0), stop=(k == KC - 1))
                    pe_ = "pool" if (w > 2 and ((kind == "p" and (c0 // 512) % 2 == 0) or (kind == "s" and j % 2 == 0))) else "dve"
                    if kind == "p":
                        si = rot("pl", 3)
                        uc, pb_, qb_ = UC[si], PB, QB
                        if pe_ == "pool":
                            pb_, qb_ = PBP, QBP
                        sl = lambda buf, a, bb: buf[:, a:bb]
                        nn = n
                        if prev_uc is None:
                            P.copy("dve", uc[:, 0:15], UST[:, j, :])
                        else:
                            P.copy("dve", uc[:, 0:15], prev_uc[:, 512:527])
                        pu = PS[:, bu, 0:n]
                        outv = plT[:, j, c0:c0 + n]
                    else:
                        uc, pb_, qb_ = USX[j % 2], PSX, QSX
                        sl = lambda buf, a, bb: buf[:, :, a:bb]
                        nn = 8
                        pu = PS[:, bu, 0:n].rearrange("p (s t) -> p s t", t=8)
                        outv = plT[:, j, c0:c0 + n].rearrange("p (s t) -> p s t", t=8)
                        for h in range(2):
                            P.copy("dve" if h == 0 else "act", uc[:, h * 8:(h + 1) * 8, 0:15],
                                   SPT[h][:, j, :].rearrange("p (s t) -> p s t", t=15))
                    E = 15 + nn
                    if pe_ == "pool":
                        for d_ in [d for d in deferred if d[0] == kind]:
                            deferred.remove(d_)
                            d_[1]()
                    P.copy("act", sl(uc, 15, E), pu, name="u")
                    if w == 2:
                        P.tt(pe_, outv, sl(uc, 14, E - 1), sl(uc, 15, E), ALU.subtract, name="pool2")
                        sw = None
                    else:
                        P.tt(pe_, sl(pb_, 1, E), sl(uc, 1, E), sl(uc, 0, E - 1), ALU.add, name="s2")
                        P.tt(pe_, sl(qb_, 3, E), sl(pb_, 3, E), sl(pb_, 1, E - 2), ALU.add, name="s4")
                        sw = qb_
                        if w >= 8:
                            P.tt(pe_, sl(pb_, 7, E), sl(qb_, 7, E), sl(qb_, 3, E - 4), ALU.add, name="s8")
                            sw = pb_
                        if w >= 16:
                            P.tt(pe_, sl(qb_, 15, E), sl(pb_, 15, E), sl(pb_, 7, E - 8), ALU.add, name="s16")
                            sw = qb_
                    def fin(w=w, outv=outv, uc=uc, sw=sw, sl=sl, E=E, kind=kind, c0=c0, j=j, g=g):
                        if w > 2:
                            P.stt("dve", outv, sl(uc, 15, E), -float(w), sl(sw, 15, E), ALU.mult, ALU.add, name="pooled")
                        if kind == "p" and ps_i == 0 and c0 == 0:
                            if w == 2:
                                P.memset("dve", plT[:, j, 0:1], 0.0)
                            else:
                                nfix = w - 1
                                P.tt("dve", TMPC[:, 0:nfix], sw[:, 15:15 + nfix], FAC[:, g, 0:nfix], ALU.mult)
                                P.stt("dve", plT[:, j, 0:nfix], uc[:, 15:15 + nfix], -float(w), TMPC[:, 0:nfix],
                                      ALU.mult, ALU.add)
                    if kind == "p":
                        prev_uc = uc
                        if c0 + n == NPP:
                            P.copy("act", UST[:, j, :], uc[:, 512:527])
                    else:
                        P.copy("act", US_[:, j, :].rearrange("p (s t) -> p s t", t=8), uc[:, :, 15:23])
                    if pe_ == "pool":
                        deferred.append((kind, fin))
                    else:
                        fin()
                    while len(deferred) > (1 if pe_ == "pool" else 0):
                        deferred.pop(0)[1]()

            def emit_state_outputs():
                if last_pass:
                    b2 = bank2()
                    for j in range(KC):
                        P.tr(PS[0:2, b2 + j // 4, (j % 4) * 128:(j % 4 + 1) * 128], ZST[:, j, :], IDF[:, :])
                    pz = bass.AP(PS, b2 * 512, [[4096, 2], [1, D]])
                    P.copy("dve", STGS[0][0:2, :], pz)
                    P.dma("sp", ncp[:, :], STGS[0][0:2, :], "so0", is_output=True)
                    b2 = bank2()
                    for j in range(KC):
                        P.tr(PS[0:15, b2 + j // 4, (j % 4) * 128:(j % 4 + 1) * 128], UST[:, j, :], IDF[:, :])
                    pz = bass.AP(PS, b2 * 512, [[4096, 15], [1, D]])
                    P.copy("act", STGS[1][0:15, :], pz)
                    P.dma("sp", npp[:, :], STGS[1][0:15, :], "so1", is_output=True)
                    if has_s:
                        b2 = bank2()
                        for j in range(KC):
                            P.tr(PS[0:32, b2 + j // 4, (j % 4) * 128:(j % 4 + 1) * 128], ZS_[:, j, :], IDF[:, :])
                        pz = bass.AP(PS, b2 * 512, [[4096, 32], [1, D]])
                        P.copy("dve", STGS[2][0:32, :], pz)
                        P.dma("sp", ncs[:, :], STGS[2][0:32, :], "so2", is_output=True)
                        b2 = bank2()
                        for j in range(KC):
                            P.tr(PS[:, b2 + j // 4, (j % 4) * 128:(j % 4 + 1) * 128], US_[:, j, :], IDF[:, :])
                        pz = bass.AP(PS, b2 * 512, [[4096, 128], [1, D]])
                        P.copy("act", STGS[3][:, :], pz)
                        for s in range(NSEQ):
                            P.dma("sp", nps[s * 15 + 7:s * 15 + 15, :], STGS[3][s * 8:(s + 1) * 8, :], "so3", is_output=True)
                        P.dma("sp", nps.rearrange("(s r) d -> s r d", r=15)[:, 0:7, :],
                              st_pool.rearrange("(s r) d -> s r d", r=15)[:, 8:15, :], "so4", is_output=True)


            if has_s:
                S2T = [_view(AR, A_X1 + 5120, F32, [512]), _view(AR, A_X1 + 14080, F32, [512])]
            else:
                S2T = [tmv(14848, F32, [512]), tmv(16896, F32, [512])]

            def s2_chunk(m, jj, wgc, wgp, woc):
                flush_deferred()
                g = m // 2
                for (c0, n, kind) in tgs:
                    bA, bB, bg1, bg2 = bank(), bank(), bank(), bank()
                    for k in range(KC):
                        P.mm(PS[:, bg1, 0:n], wgc.lhsT(k, jj), hT[:, k, c0:c0 + n],
                             start=(k == 0), stop=(k == KC - 1))
                    for k in range(KC):
                        P.mm(PS[:, bg2, 0:n], wgp.lhsT(k, jj), hT[:, k, c0:c0 + n],
                             start=(k == 0), stop=(k == KC - 1))
                    for k in range(KC):
                        P.mm(PS[:, bA, 0:n], woc.lhsT(k, jj), ApT[:, k, c0:c0 + n],
                             start=(k == 0), stop=(k == KC - 1))
                    for c in range(2):
                        P.mm(PS[:, bB, 0:n], WPG[:, g, c, (m % 2) * 128:(m % 2 + 1) * 128],
                             plT[:, 2 * g + c, c0:c0 + n], start=(c == 0), stop=(c == 1))
                    s1, s2 = S2T[0][:, 0:n], S2T[1][:, 0:n]
                    t1, t2 = s1, s2
                    P.act(s1, PS[:, bg1, 0:n], AF.Sigmoid, bias=cs(C_BG + m), name="sig1")
                    P.act(s2, PS[:, bg2, 0:n], AF.Sigmoid, bias=cs(C_BG + 8 + m), name="sig2")
                    P.tt("dve", t1, PS[:, bA, 0:n], s1, ALU.mult, name="mA")
                    P.stt("dve", t2, PS[:, bB, 0:n], cs(C_PS + m), s2, ALU.mult, ALU.mult, name="mB")
                    P.tt("dve", mgT[:, m, c0:c0 + n], t1, t2, ALU.add, name="merge")

            for b in range(2):
                wu, wgc, wgp, woc = wgroup([(w_in, 0, 3 * D + b * 512), (w_in, 0, 4 * D + b * 512),
                                            (w_in, 0, 5 * D + b * 512), (w_out_conv, 0, b * 512)])
                c_ = b * 4
                s1b_chunk(c_ + 0, 0, wu)
                s1b_chunk(c_ + 1, 1, wu)
                s1b_chunk(c_ + 2, 2, wu)
                s2_chunk(c_ + 0, 0, wgc, wgp, woc)
                s1b_chunk(c_ + 3, 3, wu)
                s2_chunk(c_ + 1, 1, wgc, wgp, woc)
                s2_chunk(c_ + 2, 2, wgc, wgp, woc)
                if b == 1:
                    emit_state_outputs()
                s2_chunk(c_ + 3, 3, wgc, wgp, woc)
            flush_deferred()

            wo = wgroup([(w_o, 0, 0), (w_o, 0, 512)])
            s3b, s3x = {}, {}

            def s3_A(i):
                b2 = bank2()
                for nh in range(2):
                    for k in range(KC):
                        P.mm(PS[:, b2 + nh, :], mgT[:, k, i * 128:(i + 1) * 128], wo[nh].full(k),
                             start=(k == 0), stop=(k == KC - 1))
                s3b[i] = b2

            def s3_B1(i):
                pm = bass.AP(PS, s3b.pop(i) * 512, [[4096, 128], [1, D]])
                ss = stat()
                P.act(JUNK, pm, AF.Square, accum_out=ss, name="sq_mix")
                rstd_from(ss)
                xs_ = rot("xt", 2)
                xt = XT[xs_]
                if ps_i == 0:
                    P.stt("dve", xt, pm, ss, GBC[:, 0, :], ALU.mult, ALU.mult, name="mixn")
                else:
                    P.dma("sp", xt, tiles[i][0], "xt%d" % xs_, name="xreload")
                    P.stt("dve", x1[:, i, :], pm, ss, GBC[:, 0, :], ALU.mult, ALU.mult, name="mixn")
                P.tt("pool", x1[:, i, :], x1[:, i, :], xt, ALU.add, name="x1")

            def s3_B2(i):
                s3x[i] = norm_prep(x1[:, i, :])

            def s3_C(i):
                norm_tr(s3x.pop(i), h2T, i * 128, C_GPL)

            s4_early = {"w": None, "done": set()}

            def s4_unit(wu_, f, jj, c0, n):
                bu = bank()
                for k in range(KC):
                    P.mm(PS[:, bu, 0:n], wu_.lhsT(k, jj), h2T[:, k, c0:c0 + n],
                         start=(k == 0), stop=(k == KC - 1))
                rl = RL[rot("rl", 3)][:, 0:n]
                P.act(rl, PS[:, bu, 0:n], AF.Relu, name="relu")
                P.tt("dve", actT[:, f, c0:c0 + n], rl, rl, ALU.mult, name="sqr")

            for u in range(NT + 3):
                if u < NT:
                    s3_A(u)
                if 1 <= u <= NT:
                    s3_B1(u - 1)
                if 2 <= u <= NT + 1:
                    s3_B2(u - 2)
                if u >= 3:
                    s3_C(u - 3)
                u0_ = max(7, NT)
                if u >= u0_:
                    if s4_early["w"] is None:
                        s4_early["w"] = wgroup([(w_up, 0, 0), (w_up, 0, 512)])
                    todo_ = [(bl_, jj_) for bl_ in range(2) for jj_ in range(4)
                             if (bl_ * 4 + jj_, tgs[0][0]) not in s4_early["done"]]
                    if u < NT + 2:
                        todo_ = todo_[:2]
                    for (bl_, jj_) in todo_:
                        s4_unit(s4_early["w"][bl_], bl_ * 4 + jj_, jj_, tgs[0][0], tgs[0][1])
                        s4_early["done"].add((bl_ * 4 + jj_, tgs[0][0]))

            if s4_early["w"] is None:
                s4_early["w"] = wgroup([(w_up, 0, 0), (w_up, 0, 512)])
            for blk in range(DFF // 512):
                if blk < 2:
                    wu_ = s4_early["w"][blk]
                else:
                    wu_, = wgroup([(w_up, 0, blk * 512)])
                for jj in range(4):
                    f = blk * 4 + jj
                    for (c0, n, kind) in tgs:
                        if (f, c0) in s4_early["done"]:
                            continue
                        s4_unit(wu_, f, jj, c0, n)

            groups = [list(range(a, min(a + 3, NT))) for a in range(0, NT, 3)]
            if not last_pass:
                pending_s0 = s0_steps(passes[ps_i + 1])
            unit_i = 0
            bank_pool[0] = [6, 7]
            for G in groups:
                bk = {}
                for gi_, i in enumerate(G):
                    bk[i] = 2 * gi_
                od_ = s5_order(groups.index(G))
                for (nh, kg) in od_:
                    kgs_ = [kg2 for (nh2, kg2) in od_ if nh2 == nh]
                    first_k, last_k = (kg == kgs_[0]), (kg == kgs_[-1])
                    wd, = wgroup([(w_down, kg * 1024, nh * 512)])
                    if unit_i >= 1 and pending_s0:
                        pending_s0.pop(0)()
                    unit_i += 1
                    for i in G:
                        for kk in range(KC):
                            P.mm(PS[:, bk[i] + nh, :], actT[:, kg * 8 + kk, i * 128:(i + 1) * 128], wd.full(kk),
                                 start=(first_k and kk == 0), stop=(last_k and kk == KC - 1),
                                 signal=((last_k or i == G[-1]) and kk == KC - 1))
                for i in G:
                    pm = bass.AP(PS, bk[i] * 512, [[4096, 128], [1, D]])
                    ss = stat()
                    P.act(JUNK, pm, AF.Square, accum_out=ss, name="sq_mlp")
                    rstd_from(ss)
                    t4 = XT[rot("xt", 2)]
                    P.stt("dve", t4, pm, ss, GBC[:, 1, :], ALU.mult, ALU.mult, name="mlpn")
                    P.tt("dve", x1[:, i, :], x1[:, i, :], t4, ALU.add, name="y")
                    P.dma("sp", tiles[i][1], x1[:, i, :], "yo%d" % i, name="ystore", is_output=True)
            for step in pending_s0:
                step()
            pending_s0 = []
            bank_pool[0] = None

        nwait = P.emit(stack)
    return nc


_NC_CACHE = {}


def kernel(**inputs):
    f = lambda a: np.ascontiguousarray(np.asarray(a, dtype=np.float32))
    x_prompt = f(inputs["x_prompt"])
    x_sample = f(inputs["x_sample"])
    state_conv = f(inputs["state_conv"])
    state_pool = f(inputs["state_pool"])
    shared = {
        "g_pre_mix": f(inputs["g_pre_mix"]).reshape(D),
        "w_in": f(inputs["w_in"]).reshape(D, 6 * D),
        "b_gate": f(inputs["b_gate"]).reshape(2 * D),
        "w_conv": f(inputs["w_conv"]).reshape(3, D),
        "w_out_conv": f(inputs["w_out_conv"]).reshape(D, D),
        "w_pool_group": f(inputs["w_pool_group"]).reshape(4, 256, 256),
        "pool_scale": f(inputs["pool_scale"]).reshape(D),
        "w_o": f(inputs["w_o"]).reshape(D, D),
        "g_post_mix": f(inputs["g_post_mix"]).reshape(D),
        "g_pre_mlp": f(inputs["g_pre_mlp"]).reshape(D),
        "w_up": f(inputs["w_up"]).reshape(D, DFF),
        "w_down": f(inputs["w_down"]).reshape(DFF, D),
        "g_post_mlp": f(inputs["g_post_mlp"]).reshape(D),
    }
    in_maps = []
    for c in range(NCORES):
        m = dict(shared)
        m["x_p"] = x_prompt[c]
        m["x_s"] = x_sample[16 * c:16 * (c + 1)].reshape(128, D)
        m["st_conv"] = state_conv[0, 16 * c:16 * (c + 1)].reshape(32, D)
        m["st_pool"] = state_pool[0, 16 * c:16 * (c + 1)].reshape(240, D)
        in_maps.append(m)
    if "nc" not in _NC_CACHE:
        _NC_CACHE["nc"] = build_program()
    nc = _NC_CACHE["nc"]
    res = run_bass_kernel_spmd(nc, in_maps, core_ids=list(range(NCORES)))
    R = res.results
    y_prompt = np.stack([R[c]["y_p"] for c in range(NCORES)], axis=0)
    y_sample = np.concatenate([R[c]["y_s"].reshape(16, 8, D) for c in range(NCORES)], axis=0)
    ncp = np.stack([R[c]["ncp"] for c in range(NCORES)], axis=0)[None]
    npp = np.stack([R[c]["npp"] for c in range(NCORES)], axis=0)[None]
    ncs = np.concatenate([R[c]["ncs"].reshape(16, 2, D) for c in range(NCORES)], axis=0)[None]
    nps = np.concatenate([R[c]["nps"].reshape(16, 15, D) for c in range(NCORES)], axis=0)[None]
    return (y_prompt.astype(np.float32), y_sample.astype(np.float32), ncp.astype(np.float32),
            npp.astype(np.float32), ncs.astype(np.float32), nps.astype(np.float32))
```
